# Optimizing a Trainium2 kernel written in Bass

```python
import math, functools
import jax, jax.numpy as jnp
from jax import lax
import numpy as np

D_MODEL = 2048
BATCH = 2
SEQ = 8192
DEPTH = 4

GRID_W = 64
CTX_LEN = 256
HEAD_DIM = 128
RET_HEADS = 4
DN_HEADS = 4
ATT_HEADS = 8
ATT_KV_HEADS = 2
RET_W = RET_HEADS * HEAD_DIM
DN_W = DN_HEADS * HEAD_DIM
ATT_W = ATT_HEADS * HEAD_DIM
ATT_KV_W = ATT_KV_HEADS * HEAD_DIM
MIX_W = RET_W + DN_W + ATT_W
RET_CHUNK = 128
DN_CHUNK = 64
DN_CONV_K = 5
Q_BLOCK = 128
ROPE_THETA = 10000.0
D_FF = ((8 * D_MODEL + 3 * 256 - 1) // (3 * 256)) * 256
DEEPNORM_ALPHA = (2 * DEPTH) ** 0.25
DEEPNORM_BETA = (8 * DEPTH) ** -0.25
EPS = 1e-6
SPLIT_SIZES = (RET_W, RET_W, RET_W, RET_W, 3 * DN_W, DN_W, 2 * DN_HEADS, 2 * DN_HEADS, ATT_W, ATT_KV_W, ATT_KV_W)
PROJ_W = sum(SPLIT_SIZES)
SPLIT_POINTS = tuple(np.cumsum(SPLIT_SIZES)[:-1].tolist())

kernel_name = "hybrid_ret_gdn_gqa_diffusion_block"


def layer_norm(x, w, b):
    xf = x.astype(jnp.float32)
    mu = jnp.mean(xf, -1, keepdims=True)
    var = jnp.mean(jnp.square(xf - mu), -1, keepdims=True)
    return (xf - mu) * lax.rsqrt(var + EPS) * w + b


def rms_norm(x, w=None):
    xf = x.astype(jnp.float32)
    y = xf * lax.rsqrt(jnp.mean(xf * xf, -1, keepdims=True) + EPS)
    if w is not None:
        y = y * w
    return y.astype(x.dtype)


def l2_normalize(x):
    return x * lax.rsqrt(jnp.sum(x * x, -1, keepdims=True) + EPS)


def split_heads(a, n_heads):
    return a.reshape(a.shape[:-1] + (n_heads, HEAD_DIM))


def modulate(h, shift, scale):
    return h * (1.0 + scale) + shift


def post_norm(x, y, w, b):
    return layer_norm(DEEPNORM_ALPHA * x + y, w, b).astype(x.dtype)


def axial_rope(n_tokens):
    rows = n_tokens // GRID_W
    row = jnp.repeat(jnp.arange(rows, dtype=jnp.float32), GRID_W)
    col = jnp.tile(jnp.arange(GRID_W, dtype=jnp.float32), rows)
    n_freq = HEAD_DIM // 4
    inv = ROPE_THETA ** (-jnp.arange(n_freq, dtype=jnp.float32) / n_freq)
    ang = jnp.concatenate([row[:, None] * inv, col[:, None] * inv], -1)
    return jnp.cos(ang), jnp.sin(ang)


def apply_rope(x, cos, sin):
    xf = x.astype(jnp.float32)
    x1, x2 = jnp.split(xf, 2, -1)
    c = cos[None, :, None, :]
    s = sin[None, :, None, :]
    return jnp.concatenate([x1 * c - x2 * s, x1 * s + x2 * c], -1).astype(x.dtype)


def bidirectional(scan_f, scan_b, ctx_f, lat_f, ctx_b, lat_b, s0):
    flip = lambda seq: tuple(jnp.flip(a, axis=1) for a in seq)
    o_cf, s_cf = scan_f(*ctx_f, s0)
    o_lf, _ = scan_f(*lat_f, s_cf)
    o_cb, s_cb = scan_b(*flip(ctx_b), s0)
    o_lb, _ = scan_b(*flip(lat_b), s_cb)
    return o_cf + jnp.flip(o_cb, 1), o_lf + jnp.flip(o_lb, 1)


def retention_scan(q, k, v, s0, log_gamma):
    b, l, h, d = q.shape
    n = l // RET_CHUNK
    qc = q.reshape(b, n, RET_CHUNK, h, d)
    kc = k.reshape(b, n, RET_CHUNK, h, d)
    vc = v.reshape(b, n, RET_CHUNK, h, d)
    pos = jnp.arange(RET_CHUNK, dtype=jnp.float32)
    rel = pos[:, None] - pos[None, :]
    decay = jnp.where(rel >= 0, jnp.exp(jnp.maximum(rel, 0.0)[None] * log_gamma[:, None, None]), 0.0)
    intra = jnp.einsum('bnihd,bnjhd->bnhij', qc, kc) * decay
    o_intra = jnp.einsum('bnhij,bnjhd->bnihd', intra, vc)
    q_decay = jnp.exp((pos + 1.0)[:, None] * log_gamma[None, :])
    k_decay = jnp.exp((RET_CHUNK - 1.0 - pos)[:, None] * log_gamma[None, :])
    chunk_kv = jnp.einsum('bnjhd,jh,bnjhe->nbhde', kc, k_decay, vc)
    chunk_decay = jnp.exp(RET_CHUNK * log_gamma)[None, :, None, None]

    def step(s, u):
        return s * chunk_decay + u, s

    s_fin, s_prev = lax.scan(step, s0, chunk_kv)
    o_inter = jnp.einsum('bnihd,ih,nbhde->bnihe', qc, q_decay, s_prev)
    return (o_intra + o_inter).reshape(b, l, h, d), s_fin


def to_chunks(a, c):
    b, l = a.shape[:2]
    return jnp.swapaxes(a.reshape((b, l // c, c) + a.shape[2:]), 2, 3)


def gated_delta_scan(q, k, v, g, beta, s0):
    b, l, h, _ = q.shape
    c = DN_CHUNK
    qc, kc, vc = to_chunks(q, c), to_chunks(k, c), to_chunks(v, c)
    gc, bc = to_chunks(g, c), to_chunks(beta, c)
    g_cum = jnp.cumsum(gc, -1)
    tri = jnp.tril(jnp.ones((c, c), bool))
    strict = jnp.tril(jnp.ones((c, c), bool), -1)
    diff = g_cum[..., :, None] - g_cum[..., None, :]
    decay = jnp.where(tri, jnp.exp(jnp.where(tri, diff, 0.0)), 0.0)
    k_beta = kc * bc[..., None]
    v_beta = vc * bc[..., None]
    a = jnp.where(strict, jnp.einsum('bnhid,bnhjd->bnhij', k_beta, kc) * decay, 0.0)
    eye = jnp.eye(c, dtype=a.dtype)
    t = lax.linalg.triangular_solve(eye + a, jnp.broadcast_to(eye, a.shape), left_side=True,
                                    lower=True, unit_diagonal=True)
    w_val = jnp.einsum('bnhij,bnhjd->bnhid', t, v_beta)
    k_cum = jnp.einsum('bnhij,bnhjd->bnhid', t, k_beta * jnp.exp(g_cum)[..., None])
    qk = jnp.einsum('bnhid,bnhjd->bnhij', qc, kc) * decay
    q_g = qc * jnp.exp(g_cum)[..., None]
    k_g = kc * jnp.exp(g_cum[..., -1:] - g_cum)[..., None]
    g_last = jnp.exp(g_cum[..., -1])
    xs = tuple(jnp.moveaxis(z, 1, 0) for z in (w_val, k_cum, qk, q_g, k_g, g_last))

    def step(s, inp):
        w_i, kc_i, qk_i, qg_i, kg_i, gl_i = inp
        v_new = w_i - jnp.einsum('bhcd,bhde->bhce', kc_i, s)
        o = jnp.einsum('bhcd,bhde->bhce', qg_i, s) + jnp.einsum('bhij,bhje->bhie', qk_i, v_new)
        s = s * gl_i[..., None, None] + jnp.einsum('bhcd,bhce->bhde', kg_i, v_new)
        return s, o

    s_fin, o = lax.scan(step, s0, xs)
    return o.transpose(1, 0, 3, 2, 4).reshape(b, l, h, -1), s_fin


def short_conv(x, w):
    pad = DN_CONV_K // 2
    return lax.conv_general_dilated(x, w[:, None, :], window_strides=(1,), padding=[(pad, pad)],
                                    dimension_numbers=('NWC', 'WIO', 'NWC'),
                                    feature_group_count=x.shape[-1])


def block_attention(q, k, v):
    b, lq, h, d = q.shape
    kvh = k.shape[2]
    qb = q.reshape(b, lq // Q_BLOCK, Q_BLOCK, kvh, h // kvh, d).swapaxes(0, 1)

    def one_block(qi):
        s = jnp.einsum('bqhgd,bkhd->bhgqk', qi, k).astype(jnp.float32) * d ** -0.5
        p = jax.nn.softmax(s, axis=-1).astype(v.dtype)
        return jnp.einsum('bhgqk,bkhd->bqhgd', p, v)

    o = lax.map(one_block, qb)
    return o.swapaxes(0, 1).reshape(b, lq, h * d)


def retention_group(pc, pl, decay_logit, cos, sin):
    log_gamma = jax.nn.log_sigmoid(decay_logit.astype(jnp.float32))

    def qkv(p, rotate):
        q, k, v = (split_heads(a, RET_HEADS).astype(jnp.float32) for a in p[:3])
        if rotate:
            q, k = apply_rope(q, cos, sin), apply_rope(k, cos, sin)
        return q, k * HEAD_DIM ** -0.5, v

    ctx_seq = qkv(pc, False)
    lat_seq = qkv(pl, True)
    s0 = jnp.zeros((pc[0].shape[0], RET_HEADS, HEAD_DIM, HEAD_DIM), jnp.float32)
    o_c, o_l = bidirectional(functools.partial(retention_scan, log_gamma=log_gamma[0]),
                             functools.partial(retention_scan, log_gamma=log_gamma[1]),
                             ctx_seq, lat_seq, ctx_seq, lat_seq, s0)

    def out(o, g):
        y = rms_norm(o) * jax.nn.silu(split_heads(g, RET_HEADS).astype(jnp.float32))
        return y.reshape(y.shape[:2] + (RET_W,)).astype(g.dtype)

    return out(o_c, pc[3]), out(o_l, pl[3])


def deltanet_group(pc, pl, conv_w, a_log, dt_bias, norm_w):
    neg_a = -jnp.exp(a_log.astype(jnp.float32))
    dt_b = dt_bias.astype(jnp.float32)

    def prep(p):
        qkv, _, a, bb = p
        qkv = jax.nn.silu(short_conv(qkv, conv_w.astype(qkv.dtype)))
        q, k, v = (split_heads(t, DN_HEADS).astype(jnp.float32) for t in jnp.split(qkv, 3, -1))
        q = l2_normalize(q) * HEAD_DIM ** -0.5
        k = l2_normalize(k)
        bsz, n = a.shape[:2]
        g = neg_a * jax.nn.softplus(a.astype(jnp.float32).reshape(bsz, n, 2, DN_HEADS) + dt_b)
        beta = jax.nn.sigmoid(bb.astype(jnp.float32).reshape(bsz, n, 2, DN_HEADS))
        return (q, k, v, g[:, :, 0], beta[:, :, 0]), (q, k, v, g[:, :, 1], beta[:, :, 1])

    cf, cb = prep(pc)
    lf, lb = prep(pl)
    s0 = jnp.zeros((pc[0].shape[0], DN_HEADS, HEAD_DIM, HEAD_DIM), jnp.float32)
    o_c, o_l = bidirectional(gated_delta_scan, gated_delta_scan, cf, lf, cb, lb, s0)

    def out(o, z):
        y = rms_norm(o, norm_w) * jax.nn.silu(split_heads(z, DN_HEADS).astype(jnp.float32))
        return y.reshape(y.shape[:2] + (DN_W,)).astype(z.dtype)

    return out(o_c, pc[1]), out(o_l, pl[1])


def attention_group(pc, pl, qn_w, kn_w, cos, sin, keep_ctx):
    def qkv(p):
        q = rms_norm(split_heads(p[0], ATT_HEADS), qn_w)
        k = rms_norm(split_heads(p[1], ATT_KV_HEADS), kn_w)
        return q, k, split_heads(p[2], ATT_KV_HEADS)

    qc, kc, vc = qkv(pc)
    ql, kl, vl = qkv(pl)
    ql, kl = apply_rope(ql, cos, sin), apply_rope(kl, cos, sin)
    y_l = block_attention(ql, jnp.concatenate([kc, kl], 1), jnp.concatenate([vc, vl], 1))
    y_c = block_attention(qc, kc, vc) if keep_ctx else None
    return y_c, y_l


def hybrid_mixer(h_ctx, h_lat, w_in, ret_decay_logit, dn_conv_w, dn_a_log, dn_dt_bias, dn_norm_w,
                 att_qn_w, att_kn_w, cos, sin, keep_ctx):
    pc = jnp.split(h_ctx @ w_in, SPLIT_POINTS, axis=-1)
    pl = jnp.split(h_lat @ w_in, SPLIT_POINTS, axis=-1)
    rc, rl = retention_group(pc[0:4], pl[0:4], ret_decay_logit, cos, sin)
    dc, dl = deltanet_group(pc[4:8], pl[4:8], dn_conv_w, dn_a_log, dn_dt_bias, dn_norm_w)
    ac, al = attention_group(pc[8:11], pl[8:11], att_qn_w, att_kn_w, cos, sin, keep_ctx)
    y_lat = jnp.concatenate([rl, dl, al], -1)
    y_ctx = jnp.concatenate([rc, dc, ac], -1) if keep_ctx else None
    return y_ctx, y_lat


def swiglu(h, w_in, w_out):
    gate, up = jnp.split(h @ w_in, 2, -1)
    return (jax.nn.silu(gate) * up) @ w_out


def setup_inputs(seed: int = 0) -> dict:
    key = jax.random.key(seed)
    ks = jax.random.split(key, 24)
    f32 = jnp.float32

    def nrm(k, shape, scale):
        return jax.random.normal(k, shape, f32) * scale

    base_logit = jnp.log(2.0 ** (5.0 + jnp.arange(RET_HEADS, dtype=f32)) - 1.0)
    dt = jnp.exp(jax.random.uniform(ks[10], (DEPTH, 2, DN_HEADS), f32, math.log(1e-3), math.log(1e-1)))
    return {
        "x": nrm(ks[0], (BATCH, SEQ, D_MODEL), 1.0),
        "c": nrm(ks[1], (BATCH, D_MODEL), 1.0),
        "ctx": nrm(ks[2], (BATCH, CTX_LEN, D_MODEL), 1.0),
        "c_ctx": nrm(ks[3], (D_MODEL,), 1.0),
        "w_ada": nrm(ks[4], (DEPTH, D_MODEL, 6 * D_MODEL), 0.5 * D_MODEL ** -0.5),
        "b_ada": nrm(ks[5], (DEPTH, 6 * D_MODEL), 0.02),
        "w_in": nrm(ks[6], (DEPTH, D_MODEL, PROJ_W), D_MODEL ** -0.5),
        "ret_decay_logit": base_logit + nrm(ks[7], (DEPTH, 2, RET_HEADS), 0.1),
        "dn_conv_w": nrm(ks[8], (DEPTH, DN_CONV_K, 3 * DN_W), DN_CONV_K ** -0.5),
        "dn_a_log": jnp.log(jax.random.uniform(ks[9], (DEPTH, 2, DN_HEADS), f32, 1.0, 16.0)),
        "dn_dt_bias": dt + jnp.log(-jnp.expm1(-dt)),
        "dn_norm_w": 1.0 + nrm(ks[11], (DEPTH, HEAD_DIM), 0.02),
        "att_qn_w": 1.0 + nrm(ks[12], (DEPTH, HEAD_DIM), 0.02),
        "att_kn_w": 1.0 + nrm(ks[13], (DEPTH, HEAD_DIM), 0.02),
        "w_o": nrm(ks[14], (DEPTH, MIX_W, D_MODEL), MIX_W ** -0.5 * DEEPNORM_BETA),
        "ln1_w": 1.0 + nrm(ks[15], (DEPTH, D_MODEL), 0.02),
        "ln1_b": nrm(ks[16], (DEPTH, D_MODEL), 0.02),
        "w_ffn_in": nrm(ks[17], (DEPTH, D_MODEL, 2 * D_FF), D_MODEL ** -0.5),
        "w_ffn_out": nrm(ks[18], (DEPTH, D_FF, D_MODEL), D_FF ** -0.5 * DEEPNORM_BETA),
        "ln2_w": 1.0 + nrm(ks[19], (DEPTH, D_MODEL), 0.02),
        "ln2_b": nrm(ks[20], (DEPTH, D_MODEL), 0.02),
    }


def reference(x, c, ctx, c_ctx, w_ada, b_ada, w_in, ret_decay_logit, dn_conv_w, dn_a_log, dn_dt_bias,
              dn_norm_w, att_qn_w, att_kn_w, w_o, ln1_w, ln1_b, w_ffn_in, w_ffn_out, ln2_w, ln2_b):
    cos, sin = axial_rope(x.shape[1])
    cond_lat = jax.nn.silu(c)
    cond_ctx = jax.nn.silu(c_ctx)
    for i in range(DEPTH):
        keep_ctx = i < DEPTH - 1
        m_l = jnp.split((cond_lat @ w_ada[i] + b_ada[i])[:, None, :], 6, -1)
        m_c = jnp.split((cond_ctx @ w_ada[i] + b_ada[i])[None, None, :], 6, -1)
        y_c, y_l = hybrid_mixer(modulate(ctx, m_c[0], m_c[1]), modulate(x, m_l[0], m_l[1]), w_in[i],
                                ret_decay_logit[i], dn_conv_w[i], dn_a_log[i], dn_dt_bias[i], dn_norm_w[i],
                                att_qn_w[i], att_kn_w[i], cos, sin, keep_ctx)
        x = post_norm(x, m_l[2] * (y_l @ w_o[i]), ln1_w[i], ln1_b[i])
        x = post_norm(x, m_l[5] * swiglu(modulate(x, m_l[3], m_l[4]), w_ffn_in[i], w_ffn_out[i]),
                      ln2_w[i], ln2_b[i])
        if keep_ctx:
            ctx = post_norm(ctx, m_c[2] * (y_c @ w_o[i]), ln1_w[i], ln1_b[i])
            ctx = post_norm(ctx, m_c[5] * swiglu(modulate(ctx, m_c[3], m_c[4]), w_ffn_in[i], w_ffn_out[i]),
                            ln2_w[i], ln2_b[i])
    return x
```

```python
import contextlib
import math
import numpy as np
import ml_dtypes
import concourse.bass as bass
import concourse.mybir as mybir
from concourse.bass_utils import run_bass_kernel_spmd

F32 = mybir.dt.float32
BF16 = mybir.dt.bfloat16
I32 = mybir.dt.int32
AF = mybir.ActivationFunctionType
ALU = mybir.AluOpType

NCORE = 8
NL = 4
D = 2048
L = 8192
LC = 256
T = L + LC
TS = 2112
DFF = 5632
KC = 16
PW = 1168
ALPHA = (2 * NL) ** 0.25
EPS = 1e-6
GROUPS4 = [[0, 1, 2, 3], [4, 5, 6, 7]]
GROUPS8 = [[0, 1, 2, 3, 4, 5, 6, 7]]


class Trk:
    __slots__ = ("name", "w", "r", "chan")

    def __init__(self, name=""):
        self.name = name
        self.w = None
        self.r = {}
        self.chan = None


class Chan:
    __slots__ = ("sem", "cnt")

    def __init__(self, sem):
        self.sem = sem
        self.cnt = 0


class Ctx:
    ENG = ("pe", "act", "dve", "pool", "sp")

    def __init__(self, nc, stack):
        self.nc = nc
        self.stack = stack
        self.sem = {e: stack.enter_context(nc.semaphore("s_" + e)) for e in self.ENG}
        self.cnt = {e: 0 for e in self.ENG}
        self.seen = {e: {} for e in self.ENG}
        self.prog = {e: [] for e in self.ENG}
        self.chans = []
        self.free_chans = []
        self.phase_chans = []
        self.gstack = stack
        self.uid = 0
        self.same_engine_sync = True

    def sbuf(self, name, shape, dt):
        self.uid += 1
        t = self.stack.enter_context(self.nc.sbuf_tensor(f"{name}_{self.uid}", shape, dt))
        return t, Trk(name)

    def psum(self, name, shape, dt=F32):
        self.uid += 1
        t = self.stack.enter_context(self.nc.psum_tensor(f"{name}_{self.uid}", shape, dt))
        return t, Trk(name)

    def new_chan(self, name):
        if self.free_chans:
            c = self.free_chans.pop()
        else:
            self.uid += 1
            c = Chan(self.gstack.enter_context(self.nc.semaphore(f"c_{name}_{self.uid}")))
            self.chans.append(c)
        self.phase_chans.append(c)
        return c

    def release_phase_chans(self):
        self.free_chans.extend(self.phase_chans)
        self.phase_chans = []

    def _deps(self, E, reads, writes, extra=()):
        deps = {}

        def add(d):
            if d is None:
                return
            k = d[0]
            if k not in deps or deps[k][2] < d[2]:
                deps[k] = d

        for t in reads:
            add(t.w)
        for t in writes:
            add(t.w)
            for d in t.r.values():
                add(d)
        for d in extra:
            add(d)
        out = []
        for k, (kk, s, v) in deps.items():
            if k == E and (E == "pe" or not self.same_engine_sync):
                continue
            if self.seen[E].get(k, 0) >= v:
                continue
            self.seen[E][k] = v
            out.append((s, v))
        return out

    def op(self, E, fn, reads=(), writes=(), defer=False):
        for s, v in self._deps(E, reads, writes):
            self.prog[E].append(("wait", s, v))
        if defer:
            assert E == "pe"
            me = (E, self.sem[E], self.cnt[E] + 1)
            self.prog[E].append(("opq", fn))
        else:
            self.cnt[E] += 1
            me = (E, self.sem[E], self.cnt[E])
            self.prog[E].append(("op", fn, self.sem[E]))
        for t in writes:
            t.w = me
            t.r = {}
        for t in reads:
            if t not in writes:
                t.r[E] = me

    def _chan_for(self, reads, writes, chan):
        if chan is not None:
            return chan
        for t in list(writes) + list(reads):
            if t.chan is not None:
                return t.chan
        t = (list(writes) + list(reads))[0]
        t.chan = self.new_chan(t.name or "x")
        return t.chan

    def _async(self, Q, item_fn, inc, reads, writes, chan, serialize=True):
        chan = self._chan_for(reads, writes, chan)
        key = ("c", id(chan))
        extra = [(key, chan.sem, chan.cnt)] if (chan.cnt > 0 and serialize) else []
        for s, v in self._deps(Q, reads, writes, extra):
            self.prog[Q].append(("wait", s, v))
        chan.cnt += inc
        me = (key, chan.sem, chan.cnt)
        self.prog[Q].append(item_fn(chan.sem))
        for t in writes:
            t.w = me
            t.r = {}
        for t in reads:
            if t not in writes:
                t.r[key] = me

    def dma(self, Q, out, in_, reads=(), writes=(), chan=None):
        self._async(Q, lambda sem: ("dma", out, in_, sem), 16, reads, writes, chan)

    def cc(self, kind, groups, in_ap, out_ap, reads=(), writes=(), op=None, chan=None):
        if getattr(self, "no_cc", False):
            return
        if chan is None:
            chan = self.new_chan("cc")
        self._async("pool", lambda sem: ("cc", kind, groups, in_ap, out_ap, sem, op), 1, reads, writes, chan, serialize=False)

    def barrier(self):
        for E in self.ENG:
            for E2 in self.ENG:
                if E2 != E and self.cnt[E2] > self.seen[E].get(E2, 0):
                    self.prog[E].append(("wait", self.sem[E2], self.cnt[E2]))
                    self.seen[E][E2] = self.cnt[E2]
            for ch in self.chans:
                k = ("c", id(ch))
                if ch.cnt > self.seen[E].get(k, 0):
                    self.prog[E].append(("wait", ch.sem, ch.cnt))
                    self.seen[E][k] = ch.cnt
        self.release_phase_chans()

    def finish(self):
        for ch in self.chans:
            k = ("c", id(ch))
            if ch.cnt > self.seen["sp"].get(k, 0):
                self.prog["sp"].append(("wait", ch.sem, ch.cnt))

    def emit(self):
        nc = self.nc
        with nc.Block() as block:
            def make(E):
                def body(eng):
                    for item in self.prog[E]:
                        kind = item[0]
                        if kind == "wait":
                            eng.wait_ge(item[1], item[2])
                        elif kind == "op":
                            item[1](eng).then_inc(item[2], 1)
                        elif kind == "opq":
                            item[1](eng)
                        elif kind == "cc":
                            _, k, groups, in_ap, out_ap, sem, ccop = item
                            eng.collective_compute(k, ccop if ccop is not None else ALU.bypass,
                                                   replica_groups=groups, ins=[in_ap], outs=[out_ap]).then_inc(sem, 1)
                        else:
                            _, out, in_, sem = item
                            eng.dma_start(out=out, in_=in_).then_inc(sem, 16)
                return body
            block.tensor(make("pe"))
            block.scalar(make("act"))
            block.vector(make("dve"))
            block.gpsimd(make("pool"))
            block.sync(make("sp"))


def dap(t, offset, dims):
    h = t.tensor if hasattr(t, "tensor") else t
    return bass.AP(tensor=h, offset=offset, ap=[[int(s), int(n)] for s, n in dims])


class Builder:
    def __init__(self, n_layers=NL, debug=None):
        self.n_layers = n_layers
        self.debug = debug or {}
        self.nc = bass.Bass("TRN2", target_bir_lowering=False)
        self.dbg_outs = []

    def din(self, name, shape, dt=F32):
        return self.nc.dram_tensor(name, list(shape), dt, kind="ExternalInput").ap()

    def dout(self, name, shape, dt=F32):
        return self.nc.dram_tensor(name, list(shape), dt, kind="ExternalOutput").ap()

    def dscr(self, name, shape, dt):
        return self.nc.dram_tensor(name, list(shape), dt).ap()

    def declare(self):
        s = self
        NLW = s.n_layers
        if s.debug.get("mixtest"):
            s.PT_in = s.din("PT_in", [T, PW])
            s.PF_in = s.din("PF_in", [384, T])
        else:
            s.x_own = s.din("x_own", [TS, D])
            s.cT = s.din("cT", [128, 32])
            s.w_ada_t = s.din("w_ada_t", [NLW * 24 * 128, 2048])
            s.b_ada_s = s.din("b_ada_s", [128, 96])
            s.w_in_sel = s.din("w_in_sel", [NLW * 128, 13 * 2048])
            s.w_o_s = s.din("w_o_s", [NLW * 128, 8192])
            s.w_f1_s = s.din("w_f1_s", [NLW * 128, 45056])
            s.w_f2_s = s.din("w_f2_s", [NLW * 128, 22528])
        s.selb = s.din("selb", [128, 2])
        s.sel4 = s.din("sel4", [128, 4])
        s.lnp = s.din("lnp", [128, 256])
        s.hp = s.din("hp", [128, 24])
        s.convw = s.din("convw", [128, 60])
        s.nw = s.din("nw", [128, NL * 3 * 128])
        s.out = s.dout("out", [L // 4, D])
        s.Wsel = [s.dscr(f"Wsel{l}", [128, 13 * 2048], BF16) for l in range(NL)]
        s.wpo = [s.dscr(f"wpo{l}", [128, 8192], BF16) for l in range(NL)]
        s.wpf1 = [s.dscr(f"wpf1{l}", [128, 45056], BF16) for l in range(NL)]
        s.wpf2 = [s.dscr(f"wpf2{l}", [128, 22528], BF16) for l in range(NL)]
        s.Wo = [s.dscr(f"Wo{l}", [2048, 2048], BF16) for l in range(NL)]
        s.Wf1 = [s.dscr(f"Wf1{l}", [DFF, 4096], BF16) for l in range(NL)]
        s.Wf2 = [s.dscr(f"Wf2{l}", [2048, DFF], BF16) for l in range(NL)]
        s.mpart = s.dscr("mpart", [128, 192], F32)
        s.MG = s.dscr("MG", [512, 192], F32)
        s.COS = s.dscr("COS", [L, 64], F32)
        s.SIN = s.dscr("SIN", [L, 64], F32)
        s.XT = s.dscr("XT", [D, TS], F32)
        s.hpart = s.dscr("hpart", [D, TS], BF16)
        s.HG = s.dscr("HG", [4 * D, TS], BF16)
        s.PT = s.dscr("PT", [T, PW], F32)
        s.PF = s.dscr("PF", [3 * 128, T], F32)
        s.ypad = s.dscr("ypad", [16 * 512, TS], BF16)
        s.yown = s.dscr("yown", [D, TS], BF16)
        s.t_XT = Trk("XT"); s.t_hpart = Trk("hpart"); s.t_HG = Trk("HG")
        s.t_PT = Trk("PT"); s.t_PF = Trk("PF"); s.t_ypad = Trk("ypad"); s.t_yown = Trk("yown")
        s.t_COS = Trk("COS")
        s.t_W = [Trk(f"W{l}") for l in range(NL)]
        s.t_Wsel = [Trk(f"Wsel{l}") for l in range(NL)]

    def add_dbg(self, name, src_ap, shape, reads, dt=F32):
        if self.debug.get("no_dbg"):
            return
        o = self.dout(name, shape, dt)
        self.dbg_outs.append(name)
        self.c.dma("sp", o, src_ap, reads=reads, writes=[Trk(name)])

    def ps(self):
        i = self.ps_i
        self.ps_i = (i + 1) % 6
        return self.psb[i]

    def pst(self):
        i = self.psT_i
        self.psT_i = (i + 1) % 2
        return self.psT[i]

    def build(self):
        s = self
        nc = s.nc
        s.declare()
        with contextlib.ExitStack() as st:
            c = s.c = Ctx(nc, st)
            c.no_cc = bool(s.debug.get("no_cc"))
            s.psb = [c.psum(f"psb{i}", [128, 512]) for i in range(6)]
            s.ps_i = 0
            s.psT = [c.psum(f"psT{i}", [128, 1024], BF16) for i in range(2)]
            s.psT_i = 0
            with contextlib.ExitStack() as st_const:
                c.stack = st_const
                s.consts()
                if s.debug.get("mixtest"):
                    with contextlib.ExitStack() as stp:
                        c.stack = stp
                        s.setup_rope()
                        tcp = [Trk("cp0"), Trk("cp1"), Trk("cp2"), Trk("cp3")]
                        for n in range(66):
                            c.dma("sp", s.PT[n * 128:(n + 1) * 128, :], s.PT_in[n * 128:(n + 1) * 128, :], writes=[tcp[n % 4]])
                        for q in range(3):
                            for k in range(4):
                                c.dma("sp", s.PF[q * 128:(q + 1) * 128, k * 2112:(k + 1) * 2112], s.PF_in[q * 128:(q + 1) * 128, k * 2112:(k + 1) * 2112], writes=[tcp[k]])
                        c.barrier()
                else:
                    s.setup_phase()
                for l in range(s.n_layers):
                    s.layer(l)
                    if s.debug.get("stop_after"):
                        break
                if not s.debug.get("stop_after"):
                    s.final_phase()
                c.barrier()
            c.finish()
            c.emit()
        return nc

    def consts(self):
        s, c = self, self.c
        s.ones_f, s.t_ones_f = c.sbuf("ones_f", [128, 128], F32)
        c.op("pool", lambda e: e.memset(s.ones_f[:], 1.0), writes=[s.t_ones_f])
        s.ident_f, s.t_ident_f = c.sbuf("ident_f", [128, 128], F32)
        c.op("pool", lambda e: e.affine_select(out=s.ident_f[:], in_=s.ones_f[:], pattern=[[-1, 128]],
                                               compare_op=ALU.is_equal, fill=0.0, base=0, channel_multiplier=1),
             reads=[s.t_ones_f], writes=[s.t_ident_f])
        s.ident_b, s.t_ident_b = c.sbuf("ident_b", [128, 128], BF16)
        c.op("dve", lambda e: e.tensor_copy(out=s.ident_b[:], in_=s.ident_f[:]), reads=[s.t_ident_f], writes=[s.t_ident_b])
        s.ones_b, s.t_ones_b = c.sbuf("ones_b", [128, 128], BF16)
        c.op("dve", lambda e: e.tensor_copy(out=s.ones_b[:], in_=s.ones_f[:]), reads=[s.t_ones_f], writes=[s.t_ones_b])
        s.lnp_s, s.t_lnp = c.sbuf("lnp", [128, 256], F32)
        c.dma("sp", s.lnp_s[:], s.lnp, writes=[s.t_lnp])
        s.hp_s, s.t_hp = c.sbuf("hp", [128, 24], F32)
        c.dma("sp", s.hp_s[:], s.hp, writes=[s.t_hp])
        s.convw_s, s.t_convw = c.sbuf("convw", [128, 60], F32)
        c.dma("sp", s.convw_s[:], s.convw, writes=[s.t_convw])
        s.nw_s, s.t_nw = c.sbuf("nw", [128, NL * 3 * 128], F32)
        c.dma("sp", s.nw_s[:], s.nw, writes=[s.t_nw])
        s.selb_s, s.t_selb = c.sbuf("selb", [128, 2], F32)
        c.dma("sp", s.selb_s[:], s.selb, writes=[s.t_selb])
        s.sel4_s, s.t_sel4 = c.sbuf("sel4", [128, 4], F32)
        c.dma("sp", s.sel4_s[:], s.sel4, writes=[s.t_sel4])
        s.eps_t, s.t_eps = c.sbuf("eps_t", [128, 1], F32)
        c.op("pool", lambda e: e.memset(s.eps_t[:], EPS), writes=[s.t_eps])
        s.one_t, _ = c.sbuf("one_t", [128, 1], F32)
        c.op("pool", lambda e: e.memset(s.one_t[:], 1.0), writes=[s.t_eps])
        s.MM, s.t_MM = c.sbuf("MM", [128, NL, 96, 2], F32)

    def setup_phase(self):
        s, c = self, self.c
        with contextlib.ExitStack() as stp:
            c.stack = stp
            s.setup_weights()
            s.setup_ada()
            s.setup_rope()
            s.setup_xt()
            c.barrier()

    def convert(self, src, dst, X, reads_dst_trk):
        s, c = self, self.c
        CH = 4096
        for i, c0 in enumerate(range(0, X, CH)):
            n = min(CH, X - c0)
            sl = s.cv_i % 2
            s.cv_i += 1
            (a, ta), (b, tb) = s.cv_a[sl], s.cv_b[sl]
            c.dma("sp", a[:, 0:n], src[:, c0:c0 + n], writes=[ta])
            eng = ("dve", "act", "pool")[s.cv_i % 3]
            if eng == "act":
                c.op("act", lambda e, a=a, b=b, n=n: e.activation(out=b[:, 0:n], in_=a[:, 0:n], func=AF.Copy), reads=[ta], writes=[tb])
            else:
                c.op(eng, lambda e, a=a, b=b, n=n: e.tensor_copy(out=b[:, 0:n], in_=a[:, 0:n]), reads=[ta], writes=[tb])
            c.dma("sp", dst[:, c0:c0 + n], b[:, 0:n], reads=[tb], writes=[reads_dst_trk])

    def setup_weights(self):
        s, c = self, self.c
        s.cv_a = [c.sbuf(f"cva{i}", [128, 4096], F32) for i in range(2)]
        s.cv_b = [c.sbuf(f"cvb{i}", [128, 4096], BF16) for i in range(2)]
        s.cv_i = 0
        s.ch_w = c.new_chan("ccw")
        for l in range(s.n_layers):
            tp = [Trk(f"wpo{l}"), Trk(f"wpf1{l}"), Trk(f"wpf2{l}")]
            s.convert(s.w_in_sel[l * 128:(l + 1) * 128, :], s.Wsel[l], 13 * 2048, s.t_Wsel[l])
            s.convert(s.w_o_s[l * 128:(l + 1) * 128, :], s.wpo[l], 8192, tp[0])
            s.convert(s.w_f1_s[l * 128:(l + 1) * 128, :], s.wpf1[l], 45056, tp[1])
            s.convert(s.w_f2_s[l * 128:(l + 1) * 128, :], s.wpf2[l], 22528, tp[2])
            for (part, full, C, rb, nch, tpi) in ((s.wpo[l], s.Wo[l], 2048, 128, 4, tp[0]), (s.wpf1[l], s.Wf1[l], 4096, 128, 11, tp[1]),
                                                  (s.wpf2[l], s.Wf2[l], DFF, 64, 8, tp[2])):
                for ch in range(nch):
                    c.cc("AllGather", GROUPS4, dap(part, ch * rb * C, [(C, rb), (1, C)]), full[ch * 4 * rb:(ch + 1) * 4 * rb, :],
                         reads=[tpi], writes=[s.t_W[l]], chan=s.ch_w)

    def setup_ada(self):
        s, c = self, self.c
        cond, t_cond = c.sbuf("cond", [128, 32], F32)
        c.dma("sp", cond[:], s.cT, writes=[t_cond])
        c.op("act", lambda e: e.activation(out=cond[:], in_=cond[:], func=AF.Silu), reads=[t_cond], writes=[t_cond])
        bada, t_bada = c.sbuf("bada", [128, 96], F32)
        c.dma("sp", bada[:], s.b_ada_s, writes=[t_bada])
        mp, t_mp = c.sbuf("mp", [128, 192], F32)
        c.op("pool", lambda e: e.memset(mp[:], 0.0), writes=[t_mp])
        wt = [c.sbuf(f"adaw{i}", [128, 2048], F32) for i in range(2)]
        for lt in range(s.n_layers * 24):
            w, tw = wt[lt % 2]
            c.dma("sp", w[:], s.w_ada_t[lt * 128:(lt + 1) * 128, :], writes=[tw])
            ps, tps = s.ps()
            for kc in range(KC):
                c.op("pe", lambda e, w=w, ps=ps, kc=kc: e.matmul(ps[:, 0:2], lhsT=w[:, kc * 128:(kc + 1) * 128],
                                                                rhs=cond[:, kc * 2:(kc + 1) * 2], start=(kc == 0), stop=(kc == KC - 1)),
                     reads=[tw, t_cond], writes=[tps])
            c.op("dve", lambda e, ps=ps, lt=lt: e.tensor_scalar(out=mp[:, lt * 2:(lt + 1) * 2], in0=ps[:, 0:2],
                                                               scalar1=bada[:, lt:lt + 1], scalar2=None, op0=ALU.add),
                 reads=[tps, t_bada], writes=[t_mp])
        t_mpart = Trk("mpart"); t_MG = Trk("MG")
        c.dma("sp", s.mpart, mp[:], reads=[t_mp], writes=[t_mpart])
        c.cc("AllGather", GROUPS4, s.mpart, s.MG, reads=[t_mpart], writes=[t_MG])
        Mg, t_Mg = c.sbuf("Mg", [128, 4, 192], F32)
        c.dma("sp", Mg[:], s.MG.rearrange("(r p) x -> p r x", p=128), reads=[t_MG], writes=[t_Mg])
        Mv = Mg[:].rearrange("p r (l t i) -> p r l t i", l=NL, t=24, i=2)
        for l in range(s.n_layers):
            dst = s.MM[:, l, :, :].rearrange("p (r t) w -> p r t w", r=4, t=24)
            for w_ in range(2):
                c.op("dve", lambda e, l=l, dst=dst, w_=w_: e.tensor_copy(out=dst[:, :, :, w_], in_=Mv[:, :, l, :, w_]), reads=[t_Mg], writes=[s.t_MM])
            for comp in (1, 4):
                v = s.MM[:, l, comp * 16:(comp + 1) * 16, :]
                c.op("dve", lambda e, v=v: e.tensor_scalar(out=v, in0=v, scalar1=1.0, scalar2=None, op0=ALU.add), reads=[], writes=[s.t_MM])
            for comp in (2, 5):
                v = s.MM[:, l, comp * 16:(comp + 1) * 16, :]
                c.op("dve", lambda e, v=v: e.tensor_scalar(out=v, in0=v, scalar1=1.0 / ALPHA, scalar2=None, op0=ALU.mult), reads=[], writes=[s.t_MM])

    def setup_rope(self):
        s, c = self, self.c
        io, t_io = c.sbuf("io", [128, 64], I32)
        c.op("pool", lambda e: e.iota(io[:], pattern=[[128, 64]], base=0, channel_multiplier=1), writes=[t_io])
        ri, t_ri = c.sbuf("ri", [128, 64], I32)
        ci, t_ci = c.sbuf("ci", [128, 64], I32)
        c.op("dve", lambda e: e.tensor_single_scalar(out=ri[:], in_=io[:], scalar=6, op=ALU.arith_shift_right), reads=[t_io], writes=[t_ri])
        c.op("dve", lambda e: e.tensor_single_scalar(out=ci[:], in_=io[:], scalar=63, op=ALU.bitwise_and), reads=[t_io], writes=[t_ci])
        rf, t_rf = c.sbuf("rf", [128, 64], F32)
        cf, t_cf = c.sbuf("cf", [128, 64], F32)
        c.op("dve", lambda e: e.tensor_copy(out=rf[:], in_=ri[:]), reads=[t_ri], writes=[t_rf])
        c.op("dve", lambda e: e.tensor_copy(out=cf[:], in_=ci[:]), reads=[t_ci], writes=[t_cf])
        ii, t_ii = c.sbuf("ii", [128, 32], I32)
        c.op("pool", lambda e: e.iota(ii[:], pattern=[[1, 32]], base=0, channel_multiplier=0), writes=[t_ii])
        inv, t_inv = c.sbuf("inv", [128, 32], F32)
        c.op("dve", lambda e: e.tensor_copy(out=inv[:], in_=ii[:]), reads=[t_ii], writes=[t_inv])
        c.op("act", lambda e: e.activation(out=inv[:], in_=inv[:], func=AF.Exp, scale=-math.log(10000.0) / 32.0), reads=[t_inv], writes=[t_inv])
        ang, t_ang = c.sbuf("ang", [128, 64, 64], F32)
        invb = inv[:].unsqueeze(1).broadcast_to([128, 64, 32])
        c.op("dve", lambda e: e.tensor_tensor(out=ang[:, :, 0:32], in0=rf[:].unsqueeze(2).broadcast_to([128, 64, 32]), in1=invb, op=ALU.mult),
             reads=[t_rf, t_inv], writes=[t_ang])
        c.op("dve", lambda e: e.tensor_tensor(out=ang[:, :, 32:64], in0=cf[:].unsqueeze(2).broadcast_to([128, 64, 32]), in1=invb, op=ALU.mult),
             reads=[t_cf, t_inv], writes=[t_ang])
        ki, t_ki = c.sbuf("ki", [128, 64, 64], I32)
        kf, t_kf = c.sbuf("kf", [128, 64, 64], F32)
        sn, t_sn = c.sbuf("sn", [128, 64, 64], F32)
        cs, t_cs = c.sbuf("cs", [128, 64, 64], F32)
        for (dst, t_dst, shift) in ((sn, t_sn, 0.0), (cs, t_cs, math.pi / 2)):
            c.op("dve", lambda e, dst=dst, shift=shift: e.tensor_scalar(out=dst[:], in0=ang[:], scalar1=shift, scalar2=None, op0=ALU.add),
                 reads=[t_ang], writes=[t_dst])
            c.op("dve", lambda e, dst=dst: e.tensor_scalar(out=ki[:], in0=dst[:], scalar1=1.0 / (2 * math.pi), scalar2=None, op0=ALU.mult),
                 reads=[t_dst], writes=[t_ki])
            c.op("dve", lambda e: e.tensor_copy(out=kf[:], in_=ki[:]), reads=[t_ki], writes=[t_kf])
            c.op("dve", lambda e, dst=dst: e.scalar_tensor_tensor(out=dst[:], in0=kf[:], scalar=-2 * math.pi, in1=dst[:], op0=ALU.mult, op1=ALU.add),
                 reads=[t_kf], writes=[t_dst])
            c.op("dve", lambda e, dst=dst: e.tensor_scalar(out=dst[:], in0=dst[:], scalar1=math.pi, scalar2=-math.pi, op0=ALU.min, op1=ALU.max),
                 reads=[], writes=[t_dst])
            c.op("act", lambda e, dst=dst: e.activation(out=dst[:], in_=dst[:], func=AF.Sin), reads=[], writes=[t_dst])
        c.dma("sp", s.SIN.rearrange("(n p) d -> p n d", p=128), sn[:], reads=[t_sn], writes=[s.t_COS])
        c.dma("sp", s.COS.rearrange("(n p) d -> p n d", p=128), cs[:], reads=[t_cs], writes=[s.t_COS])

    def setup_xt(self):
        s, c = self, self.c
        xin = [c.sbuf(f"xin{i}", [128, D], F32) for i in range(2)]
        xo = [c.sbuf(f"xo{i}", [128, KC, 128], F32) for i in range(2)]
        tiles = [(0, 64)] + [(64 + 128 * i, 128) for i in range(16)]
        for ti, (r0, m) in enumerate(tiles):
            a, ta = xin[ti % 2]
            o, to = xo[ti % 2]
            c.dma("sp", a[0:m, :], s.x_own[r0:r0 + m, :], writes=[ta])
            for g in range(4):
                ps, tps = s.ps()
                for q in range(4):
                    kc = g * 4 + q
                    c.op("pe", lambda e, a=a, ps=ps, kc=kc, q=q, m=m: e.transpose(out=ps[:, q * 128:q * 128 + m], in_=a[0:m, kc * 128:(kc + 1) * 128],
                                                                                  identity=s.ident_f[0:m, 0:m]),
                         reads=[ta, s.t_ident_f], writes=[tps])
                eng = "act" if g % 2 else "dve"
                src = ps[:, :].rearrange("p (q t) -> p q t", q=4)[:, :, 0:m]
                if eng == "act":
                    c.op("act", lambda e, o=o, src=src, g=g, m=m: e.activation(out=o[:, g * 4:(g + 1) * 4, 0:m], in_=src, func=AF.Copy), reads=[tps], writes=[to])
                else:
                    c.op("dve", lambda e, o=o, src=src, g=g, m=m: e.tensor_copy(out=o[:, g * 4:(g + 1) * 4, 0:m], in_=src), reads=[tps], writes=[to])
            c.dma("sp", s.XT.rearrange("(kc p) t -> p kc t", p=128)[:, :, r0:r0 + m], o[:, :, 0:m], reads=[to], writes=[s.t_XT])

    def layer(self, l):
        s, c = self, self.c
        if not s.debug.get("mixtest"):
            with contextlib.ExitStack() as stp:
                c.stack = stp
                s.phase_a0(l)
                c.barrier()
            with contextlib.ExitStack() as stp:
                c.stack = stp
                s.phase_a1(l)
                c.barrier()
        if self.debug.get("stop_after") == "a1":
            self.dump_a1(l)
            return
        for ph in (s.phase_att, s.phase_ret, s.phase_dn):
            if ph.__name__ in self.debug.get("skip", ()):
                continue
            with contextlib.ExitStack() as stp:
                c.stack = stp
                ph(l)
                c.barrier()
        if s.debug.get("zero_ypad"):
            with contextlib.ExitStack() as stp:
                c.stack = stp
                zt_, tzt_ = c.sbuf("zpad", [128, TS], BF16)
                c.op("dve", lambda e: e.memset(zt_[:], 0.0), writes=[tzt_])
                for i in range(64):
                    c.dma("sp", s.ypad[i * 128:(i + 1) * 128, :], zt_[:], reads=[tzt_], writes=[s.t_ypad])
                c.barrier()
        if not s.debug.get("no_rs"):
            c.cc("ReduceScatter", GROUPS4, s.ypad, s.yown, reads=[s.t_ypad], writes=[s.t_yown], op=ALU.add)
        if self.debug.get("stop_after") == "mix":
            s.add_dbg("d_yown", s.yown, [D, TS], [s.t_yown], dt=BF16)
            return
        with contextlib.ExitStack() as stp:
            c.stack = stp
            s.phase_c(l)
            c.barrier()
        if self.debug.get("stop_after") == "c":
            s.add_dbg("d_yown", s.yown, [D, TS], [s.t_yown], dt=BF16)
            s.add_dbg("d_XT", s.XT, [D, TS], [s.t_XT])
            return

    def own_tiles(self):
        return [(0, 64, 1)] + [(64 + 512 * i, 512, 0) for i in range(4)]

    def phase_a0(self, l):
        s, c = self, self.c
        xt = [c.sbuf(f"a0x{i}", [128, KC, 512], F32) for i in range(2)]
        hb = [c.sbuf(f"a0h{i}", [128, KC, 512], BF16) for i in range(2)]
        XTv = s.XT.rearrange("(kc p) t -> p kc t", p=128)
        HPv = s.hpart.rearrange("(kc p) t -> p kc t", p=128)
        for ti, (t0, n, w) in enumerate(s.own_tiles()):
            a, ta = xt[ti % 2]
            h, th = hb[ti % 2]
            c.dma("sp", a[:, :, 0:n], XTv[:, :, t0:t0 + n], reads=[s.t_XT], writes=[ta])
            for kc in range(KC):
                eng = "dve" if kc % 2 == 0 else "pool"
                c.op(eng, lambda e, a=a, h=h, kc=kc, n=n, w=w: e.tensor_scalar(out=h[:, kc, 0:n], in0=a[:, kc, 0:n],
                                                                              scalar1=s.MM[:, l, 16 + kc, w:w + 1], scalar2=s.MM[:, l, kc, w:w + 1],
                                                                              op0=ALU.mult, op1=ALU.add),
                     reads=[ta, s.t_MM], writes=[th])
            c.dma("sp", HPv[:, :, t0:t0 + n], h[:, :, 0:n], reads=[th], writes=[s.t_hpart])
        ch_h = c.new_chan("cch")
        for kc in range(KC):
            c.cc("AllGather", GROUPS4, s.hpart[kc * 128:(kc + 1) * 128, :], s.HG[kc * 512:(kc + 1) * 512, :],
                 reads=[s.t_hpart], writes=[s.t_HG], chan=ch_h)

    def phase_a1(self, l):
        s, c = self, self.c
        wsel, t_wsel = c.sbuf("wsel", [128, 13, 2048], BF16)
        c.dma("sp", wsel[:], s.Wsel[l].rearrange("p (t c) -> p t c", t=13), reads=[s.t_Wsel[l]], writes=[t_wsel])
        ht = [c.sbuf(f"a1h{i}", [128, KC, 512], BF16) for i in range(2)]
        stg = [c.sbuf(f"a1s{i}", [128, PW], F32) for i in range(2)]
        stf = [c.sbuf(f"a1f{i}", [128, 512], F32) for i in range(2)]
        HGv = s.HG.rearrange("(kc a p) t -> a p kc t", a=4, kc=KC, p=128)
        PFv = s.PF.rearrange("(s p) t -> s p t", p=128)
        it = 0
        si = 0
        fi = 0
        for js in range(4):
            for (t0, n, w) in s.own_tiles():
                tok0 = (64 * js) if w == 1 else (LC + 2048 * js + (t0 - 64))
                h, th = ht[it % 2]
                it += 1
                c.dma("sp", h[:, :, 0:n], HGv[js][:, :, t0:t0 + n], reads=[s.t_HG], writes=[th])
                for sub in range((n + 127) // 128):
                    m = min(128, n - sub * 128)
                    sg, tsg = stg[si % 2]
                    si += 1
                    for gi, (tl0, ntl, c0, ncols) in enumerate([(0, 4, 0, 512), (4, 4, 512, 512), (8, 2, 1024, 144)]):
                        ps, tps = s.ps()
                        for kc in range(KC):
                            c.op("pe", lambda e, ps=ps, h=h, kc=kc, sub=sub, m=m, tl0=tl0, ntl=ntl: e.matmul(
                                ps[0:m, 0:ntl * 128], lhsT=h[:, kc, sub * 128:sub * 128 + m],
                                rhs=wsel[:, tl0:tl0 + ntl, kc * 128:(kc + 1) * 128], start=(kc == 0), stop=(kc == KC - 1)),
                                 reads=[th, t_wsel], writes=[tps], defer=(kc != KC - 1))
                        if gi == 1:
                            c.op("act", lambda e, ps=ps, sg=sg, m=m, c0=c0, ncols=ncols: e.activation(out=sg[0:m, c0:c0 + ncols], in_=ps[0:m, 0:ncols], func=AF.Copy),
                                 reads=[tps], writes=[tsg])
                        else:
                            c.op("dve", lambda e, ps=ps, sg=sg, m=m, c0=c0, ncols=ncols: e.tensor_copy(out=sg[0:m, c0:c0 + ncols], in_=ps[0:m, 0:ncols]),
                                 reads=[tps], writes=[tsg])
                    r0 = tok0 + sub * 128
                    c.dma("sp", s.PT[r0:r0 + m, :], sg[0:m, :], reads=[tsg], writes=[s.t_PT])
                for q in range(3):
                    ps, tps = s.ps()
                    for kc in range(KC):
                        c.op("pe", lambda e, ps=ps, h=h, kc=kc, q=q, n=n: e.matmul(
                            ps[:, 0:n], lhsT=wsel[:, 10 + q, kc * 128:(kc + 1) * 128], rhs=h[:, kc, 0:n],
                            start=(kc == 0), stop=(kc == KC - 1)), reads=[th, t_wsel], writes=[tps], defer=(kc != KC - 1))
                    sf, tsf = stf[fi % 2]
                    fi += 1
                    c.op("act", lambda e, ps=ps, sf=sf, n=n: e.activation(out=sf[:, 0:n], in_=ps[:, 0:n], func=AF.Copy), reads=[tps], writes=[tsf])
                    c.dma("sp", PFv[q][:, tok0:tok0 + n], sf[:, 0:n], reads=[tsf], writes=[s.t_PF])


    def scatter_y(self, ysb, t_ysb, sidx, q0, nq):
        s, c = self, self.c
        ysc, t_ysc = s.ysc[s.ysc_i % 2]
        s.ysc_i += 1
        for blk in range(4):
            eng = "dve" if blk % 2 == 0 else "pool"
            c.op(eng, lambda e, blk=blk: e.tensor_scalar(out=ysc[:, blk, 0:nq], in0=ysb, scalar1=s.sel4_s[:, blk:blk + 1], scalar2=None, op0=ALU.mult),
                 reads=[t_ysb, s.t_sel4], writes=[t_ysc])
        YP = s.ypad.rearrange("(jt blk s d) t -> jt s d blk t", jt=4, blk=4, s=4, d=128)
        pos = q0
        while pos < q0 + nq:
            if pos < LC:
                jt, col, room = pos // 64, pos % 64, 64 - pos % 64
            else:
                tl = pos - LC
                jt, col, room = tl // 2048, 64 + tl % 2048, 2048 - tl % 2048
            n = min(room, q0 + nq - pos)
            o = pos - q0
            c.dma("sp", YP[jt][sidx][:, :, col:col + n], ysc[:, :, o:o + n], reads=[t_ysc], writes=[s.t_ypad])
            pos += n

    def alloc_scatter(self):
        s, c = self, self.c
        s.ysc = [c.sbuf(f"ysc{i}", [128, 4, 512], BF16) for i in range(2)]
        s.ysc_i = 0
        s.sel4_b, s.t_sel4b = c.sbuf("sel4b", [128, 4], BF16)
        c.op("dve", lambda e: e.tensor_copy(out=s.sel4_b[:], in_=s.sel4_s[:]), reads=[s.t_sel4], writes=[s.t_sel4b])

    def rope_ops(self, xn, t_xn, xr, t_xr, cs, sn, t_cs, nh, tmp, t_tmp):
        c = self.c
        cb = cs.unsqueeze(1).broadcast_to([128, nh, 64])
        sb = sn.unsqueeze(1).broadcast_to([128, nh, 64])
        x1 = xn[:, 0:nh, 0:64]
        x2 = xn[:, 0:nh, 64:128]
        t1, t2 = tmp[:, 0, 0:nh, :], tmp[:, 1, 0:nh, :]
        c.op("dve", lambda e: e.tensor_tensor(out=t1, in0=x1, in1=cb, op=ALU.mult), reads=[t_xn, t_cs], writes=[t_tmp])
        c.op("dve", lambda e: e.tensor_tensor(out=t2, in0=x2, in1=sb, op=ALU.mult), reads=[t_xn, t_cs], writes=[t_tmp])
        c.op("dve", lambda e: e.tensor_tensor(out=xr[:, 0:nh, 0:64], in0=t1, in1=t2, op=ALU.subtract), reads=[t_tmp], writes=[t_xr])
        c.op("dve", lambda e: e.tensor_tensor(out=t1, in0=x1, in1=sb, op=ALU.mult), reads=[t_xn, t_cs], writes=[t_tmp])
        c.op("dve", lambda e: e.tensor_tensor(out=t2, in0=x2, in1=cb, op=ALU.mult), reads=[t_xn, t_cs], writes=[t_tmp])
        c.op("dve", lambda e: e.tensor_tensor(out=xr[:, 0:nh, 64:128], in0=t1, in1=t2, op=ALU.add), reads=[t_tmp], writes=[t_xr])

    def phase_att(self, l):
        s, c = self, self.c
        s.alloc_scatter()
        QT = [c.sbuf(f"QT{i}", [128, T], BF16) for i in range(2)]
        KT, t_KT = c.sbuf("KT", [128, T], BF16)
        V, t_V = c.sbuf("V", [128, 66, 128], BF16)
        A = [c.sbuf(f"attA{i}", [128, 512], F32) for i in range(2)]
        CS = [c.sbuf(f"attCS{i}", [128, 2, 64], F32) for i in range(2)]
        ss, t_ss = c.sbuf("att_ss", [128, 4], F32)
        junk, t_junk = c.sbuf("att_junk", [128, 128], F32)
        xn, t_xn = c.sbuf("att_xn", [128, 3, 128], F32)
        xr, t_xr = c.sbuf("att_xr", [128, 3, 128], BF16)
        tmp, t_tmp = c.sbuf("att_tmp", [128, 2, 3, 64], F32)
        nwq = s.nw_s[:, (l * 3 + 1) * 128:(l * 3 + 2) * 128]
        nwk = s.nw_s[:, (l * 3 + 2) * 128:(l * 3 + 3) * 128]
        for n in range(66):
            a, ta = A[n % 2]
            c.dma("sp", a[:], s.PT[n * 128:(n + 1) * 128, 512:1024], reads=[s.t_PT], writes=[ta])
            if n >= 2:
                cs, tcs = CS[n % 2]
                c.dma("sp", cs[:, 0, :], s.COS[(n - 2) * 128:(n - 1) * 128, :], reads=[s.t_COS], writes=[tcs])
                c.dma("sp", cs[:, 1, :], s.SIN[(n - 2) * 128:(n - 1) * 128, :], reads=[s.t_COS], writes=[tcs])
            sub = s.debug.get("sub", 9)
            if sub < 1:
                continue
            for i in range(3):
                c.op("act", lambda e, a=a, i=i: e.activation(out=junk[:], in_=a[:, i * 128:(i + 1) * 128], func=AF.Square, accum_out=ss[:, i:i + 1]),
                     reads=[ta], writes=[t_junk, t_ss])
            c.op("dve", lambda e: e.tensor_scalar(out=ss[:, 0:3], in0=ss[:, 0:3], scalar1=1.0 / 128, scalar2=EPS, op0=ALU.mult, op1=ALU.add), reads=[], writes=[t_ss])
            c.op("act", lambda e: e.activation(out=ss[:, 0:3], in_=ss[:, 0:3], func=AF.Sqrt), reads=[], writes=[t_ss])
            c.op("dve", lambda e: e.reciprocal(out=ss[:, 0:3], in_=ss[:, 0:3]), reads=[], writes=[t_ss])
            if sub < 2:
                continue
            for i in range(3):
                wv = nwq if i < 2 else nwk
                c.op("dve", lambda e, a=a, i=i, wv=wv: e.scalar_tensor_tensor(out=xn[:, i, :], in0=a[:, i * 128:(i + 1) * 128], scalar=ss[:, i:i + 1],
                                                                              in1=wv, op0=ALU.mult, op1=ALU.mult), reads=[ta, t_ss, s.t_nw], writes=[t_xn])
            if sub < 3:
                continue
            if n >= 2:
                s.rope_ops(xn, t_xn, xr, t_xr, cs[:, 0, :], cs[:, 1, :], tcs, 3, tmp, t_tmp)
            else:
                c.op("dve", lambda e: e.tensor_copy(out=xr[:], in_=xn[:]), reads=[t_xn], writes=[t_xr])
            if sub < 4:
                continue
            pT, tpT = s.pst()
            for i in range(3):
                c.op("pe", lambda e, pT=pT, i=i: e.transpose(out=pT[:, i * 128:(i + 1) * 128], in_=xr[:, i, :], identity=s.ident_b[:]),
                     reads=[t_xr, s.t_ident_b], writes=[tpT])
            for i, (dst, tdst) in enumerate([QT[0], QT[1], (KT, t_KT)]):
                eng = "dve"
                if eng == "act":
                    c.op("act", lambda e, pT=pT, i=i, dst=dst, n=n: e.activation(out=dst[:, n * 128:(n + 1) * 128], in_=pT[:, i * 128:(i + 1) * 128], func=AF.Copy),
                         reads=[tpT], writes=[tdst])
                else:
                    c.op("dve", lambda e, pT=pT, i=i, dst=dst, n=n: e.tensor_copy(out=dst[:, n * 128:(n + 1) * 128], in_=pT[:, i * 128:(i + 1) * 128]),
                         reads=[tpT], writes=[tdst])
            c.op("act", lambda e, a=a, n=n: e.activation(out=V[:, n, :], in_=a[:, 384:512], func=AF.Copy), reads=[ta], writes=[t_V])
        if s.debug.get("att_stage") == 1:
            return
        Pb = [c.sbuf(f"attP{i}", [128, 512], BF16) for i in range(3)]
        rden, t_rden = c.sbuf("att_rden", [128, 512], F32)
        ysb = [c.sbuf(f"att_y{i}", [128, 512], BF16) for i in range(2)]
        sc = 128.0 ** -0.5
        pi = 0
        gi = 0
        groups = [(0, 256, [0, 1])] + [(LC + 512 * g, 512, list(range(66))) for g in range(16)]
        for hh in range(2):
            q, tq = QT[hh]
            for (q0, nq, blocks) in groups:
                psO, tpsO = s.ps()
                psD, tpsD = s.ps()
                pend = []

                def emit_S(kb, psO=psO, psD=psD, q=q, tq=tq, q0=q0, nq=nq, pend=pend):
                    psS, tpsS = s.ps()
                    while psS is psO or psS is psD:
                        psS, tpsS = s.ps()
                    c.op("pe", lambda e, psS=psS, kb=kb: e.matmul(psS[:, 0:nq], lhsT=KT[:, kb * 128:(kb + 1) * 128], rhs=q[:, q0:q0 + nq],
                                                                 start=True, stop=True), reads=[t_KT, tq], writes=[tpsS])
                    pend.append((psS, tpsS))

                LOOK = 2
                for kb in blocks[:LOOK]:
                    emit_S(kb)
                for bi, kb in enumerate(blocks):
                    if bi + LOOK < len(blocks):
                        emit_S(blocks[bi + LOOK])
                    psS, tpsS = pend[bi]
                    pb, tpb = Pb[pi % 3]
                    pi += 1
                    c.op("act", lambda e, psS=psS, pb=pb, nq=nq: e.activation(out=pb[:, 0:nq], in_=psS[:, 0:nq], func=AF.Exp, scale=sc), reads=[tpsS], writes=[tpb])
                    first, last = bi == 0, bi == len(blocks) - 1
                    c.op("pe", lambda e, psO=psO, kb=kb, pb=pb, nq=nq, first=first, last=last: e.matmul(psO[:, 0:nq], lhsT=V[:, kb, :], rhs=pb[:, 0:nq], start=first, stop=last),
                         reads=[t_V, tpb], writes=[tpsO])
                    c.op("pe", lambda e, psD=psD, pb=pb, nq=nq, first=first, last=last: e.matmul(psD[:, 0:nq], lhsT=s.ones_b[:], rhs=pb[:, 0:nq], start=first, stop=last),
                         reads=[s.t_ones_b, tpb], writes=[tpsD])
                c.op("dve", lambda e, psD=psD, nq=nq: e.reciprocal(out=rden[:, 0:nq], in_=psD[:, 0:nq]), reads=[tpsD], writes=[t_rden])
                y, ty = ysb[gi % 2]
                gi += 1
                c.op("dve", lambda e, psO=psO, y=y, nq=nq: e.tensor_tensor(out=y[:, 0:nq], in0=psO[:, 0:nq], in1=rden[:, 0:nq], op=ALU.mult), reads=[tpsO, t_rden], writes=[ty])
                if s.debug.get("att_stage") != 2:
                    s.scatter_y(y[:, 0:nq], ty, 2 + hh, q0, nq)

    def phase_ret(self, l):
        s, c = self, self.c
        s.alloc_scatter()
        sc = 128.0 ** -0.5
        lg, t_lg = c.sbuf("r_lg", [128, 2], F32)
        c.op("act", lambda e: e.activation(out=lg[:], in_=s.hp_s[:, l * 2:l * 2 + 2], func=AF.Exp, scale=-1.0), reads=[s.t_hp], writes=[t_lg])
        c.op("act", lambda e: e.activation(out=lg[:], in_=lg[:], func=AF.Ln, bias=1.0), reads=[], writes=[t_lg])
        c.op("dve", lambda e: e.tensor_scalar(out=lg[:], in0=lg[:], scalar1=-1.0, scalar2=None, op0=ALU.mult), reads=[], writes=[t_lg])
        ii, t_ii = c.sbuf("r_ii", [128, 128], I32)
        fi, t_fi = c.sbuf("r_fi", [128, 128], F32)
        DT = [c.sbuf(f"r_DT{d}", [128, 128], F32) for d in range(2)]
        RQ = [c.sbuf(f"r_RQ{d}", [128, 128], F32) for d in range(2)]
        kd, t_kd = c.sbuf("r_kd", [128, 2], F32)
        g128, t_g128 = c.sbuf("r_g128", [128, 2], F32)

        def iota_f(pattern, base, cm, n):
            c.op("pool", lambda e: e.iota(ii[:, 0:n], pattern=pattern, base=base, channel_multiplier=cm), reads=[t_fi], writes=[t_ii])
            c.op("dve", lambda e: e.tensor_copy(out=fi[:, 0:n], in_=ii[:, 0:n]), reads=[t_ii], writes=[t_fi])

        for d in range(2):
            dt_, tdt = DT[d]
            if d == 0:
                iota_f([[1, 128]], 0, -1, 128)
            else:
                iota_f([[-1, 128]], 0, 1, 128)
            c.op("dve", lambda e: e.tensor_scalar(out=fi[:], in0=fi[:], scalar1=0.0, scalar2=None, op0=ALU.max), reads=[], writes=[t_fi])
            c.op("act", lambda e, d=d, dt_=dt_: e.activation(out=dt_[:], in_=fi[:], func=AF.Exp, scale=lg[:, d:d + 1]), reads=[t_fi, t_lg], writes=[tdt])
            c.op("dve", lambda e, dt_=dt_: e.tensor_scalar(out=dt_[:], in0=dt_[:], scalar1=sc, scalar2=None, op0=ALU.mult), reads=[], writes=[tdt])
            if d == 0:
                c.op("pool", lambda e, dt_=dt_: e.affine_select(out=dt_[:], in_=dt_[:], pattern=[[1, 128]], compare_op=ALU.is_ge, fill=0.0, base=0, channel_multiplier=-1),
                     reads=[], writes=[tdt])
            else:
                c.op("pool", lambda e, dt_=dt_: e.affine_select(out=dt_[:], in_=dt_[:], pattern=[[-1, 128]], compare_op=ALU.is_ge, fill=0.0, base=0, channel_multiplier=1),
                     reads=[], writes=[tdt])
            rq, trq = RQ[d]
            if d == 0:
                iota_f([[1, 128]], 1, 0, 128)
            else:
                iota_f([[-1, 128]], 128, 0, 128)
            c.op("act", lambda e, d=d, rq=rq: e.activation(out=rq[:], in_=fi[:], func=AF.Exp, scale=lg[:, d:d + 1]), reads=[t_fi, t_lg], writes=[trq])
            if d == 0:
                iota_f([[0, 1]], 127, -1, 1)
            else:
                iota_f([[0, 1]], 0, 1, 1)
            c.op("act", lambda e, d=d: e.activation(out=kd[:, d:d + 1], in_=fi[:, 0:1], func=AF.Exp, scale=lg[:, d:d + 1]), reads=[t_fi, t_lg], writes=[t_kd])
        c.op("dve", lambda e: e.tensor_scalar(out=kd[:], in0=kd[:], scalar1=sc, scalar2=None, op0=ALU.mult), reads=[], writes=[t_kd])
        c.op("act", lambda e: e.activation(out=g128[:], in_=lg[:], func=AF.Exp, scale=128.0), reads=[t_lg], writes=[t_g128])
        QTa, t_QTa = c.sbuf("r_QT", [128, 66, 128], BF16)
        KTa, t_KTa = c.sbuf("r_KT", [128, 66, 128], BF16)
        Ka, t_Ka = c.sbuf("r_K", [128, 66, 128], BF16)
        Va, t_Va = c.sbuf("r_V", [128, 66, 128], BF16)
        SG, t_SG = c.sbuf("r_SG", [128, 66, 128], F32)
        Of, t_Of = c.sbuf("r_Of", [128, 66, 128], F32)
        A = [c.sbuf(f"r_A{i}", [128, 512], F32) for i in range(2)]
        CS = [c.sbuf(f"r_CS{i}", [128, 2, 64], F32) for i in range(2)]
        xr, t_xr = c.sbuf("r_xr", [128, 2, 128], BF16)
        tmp, t_tmp = c.sbuf("r_tmp", [128, 2, 2, 64], F32)
        for n in range(66):
            a, ta = A[n % 2]
            c.dma("sp", a[:], s.PT[n * 128:(n + 1) * 128, 0:512], reads=[s.t_PT], writes=[ta])
            av = a[:, 0:256].rearrange("p (h d) -> p h d", h=2)
            if n >= 2:
                cs, tcs = CS[n % 2]
                c.dma("sp", cs[:, 0, :], s.COS[(n - 2) * 128:(n - 1) * 128, :], reads=[s.t_COS], writes=[tcs])
                c.dma("sp", cs[:, 1, :], s.SIN[(n - 2) * 128:(n - 1) * 128, :], reads=[s.t_COS], writes=[tcs])
                s.rope_ops(av, ta, xr, t_xr, cs[:, 0, :], cs[:, 1, :], tcs, 2, tmp, t_tmp)
            else:
                c.op("dve", lambda e, av=av: e.tensor_copy(out=xr[:], in_=av), reads=[ta], writes=[t_xr])
            pT, tpT = s.pst()
            for i in range(2):
                c.op("pe", lambda e, pT=pT, i=i: e.transpose(out=pT[:, i * 128:(i + 1) * 128], in_=xr[:, i, :], identity=s.ident_b[:]),
                     reads=[t_xr, s.t_ident_b], writes=[tpT])
            c.op("dve", lambda e, pT=pT, n=n: e.tensor_copy(out=QTa[:, n, :], in_=pT[:, 0:128]), reads=[tpT], writes=[t_QTa])
            c.op("dve", lambda e, pT=pT, n=n: e.tensor_copy(out=KTa[:, n, :], in_=pT[:, 128:256]), reads=[tpT], writes=[t_KTa])
            c.op("act", lambda e, n=n: e.activation(out=Ka[:, n, :], in_=xr[:, 1, :], func=AF.Copy), reads=[t_xr], writes=[t_Ka])
            c.op("act", lambda e, a=a, n=n: e.activation(out=Va[:, n, :], in_=a[:, 256:384], func=AF.Copy), reads=[ta], writes=[t_Va])
            c.op("act", lambda e, a=a, n=n: e.activation(out=SG[:, n, :], in_=a[:, 384:512], func=AF.Silu), reads=[ta], writes=[t_SG])
        S, t_S = c.sbuf("r_S", [128, 128], F32)
        Sb, t_Sb = c.sbuf("r_Sb", [128, 128], BF16)
        ATb = [c.sbuf(f"r_AT{i}", [128, 128], BF16) for i in range(2)]
        QdT = [c.sbuf(f"r_QdT{i}", [128, 128], BF16) for i in range(2)]
        Vd = [c.sbuf(f"r_Vd{i}", [128, 128], BF16) for i in range(2)]
        osum, t_osum = c.sbuf("r_osum", [128, 128], F32)
        junk, t_junk = c.sbuf("r_junk", [128, 128], F32)
        ss, t_ss = c.sbuf("r_ss", [128, 1], F32)
        yb, t_yb = c.sbuf("r_yb", [128, 128], BF16)
        ysb = [c.sbuf(f"r_ysb{i}", [128, 128], BF16) for i in range(2)]
        orders = [[0, 1] + list(range(2, 66)), [1, 0] + list(range(65, 1, -1))]
        st = {}

        def r_prep(n, d, par):
            at, tat = ATb[par]
            qd, tqd = QdT[par]
            vd, tvd = Vd[par]
            psa, tpsa = s.ps()
            c.op("pe", lambda e: e.matmul(psa[:, 0:128], lhsT=KTa[:, n, :], rhs=QTa[:, n, :], start=True, stop=True), reads=[t_KTa, t_QTa], writes=[tpsa])
            c.op("dve", lambda e: e.tensor_tensor(out=at[:], in0=psa[:, 0:128], in1=DT[d][0][:], op=ALU.mult), reads=[tpsa, DT[d][1]], writes=[tat])
            c.op("pool", lambda e: e.tensor_tensor(out=qd[:], in0=QTa[:, n, :], in1=RQ[d][0][:], op=ALU.mult), reads=[t_QTa, RQ[d][1]], writes=[tqd])
            c.op("pool", lambda e: e.tensor_scalar(out=vd[:], in0=Va[:, n, :], scalar1=kd[:, d:d + 1], scalar2=None, op0=ALU.mult), reads=[t_Va, t_kd], writes=[tvd])
            pso, tpso = s.ps()
            c.op("pe", lambda e: e.matmul(pso[:, 0:128], lhsT=at[:], rhs=Va[:, n, :], start=True, stop=False), reads=[tat, t_Va], writes=[tpso])
            pss, tpss = s.ps()
            c.op("pe", lambda e: e.matmul(pss[:, 0:128], lhsT=Ka[:, n, :], rhs=vd[:], start=True, stop=True), reads=[t_Ka, tvd], writes=[tpss])
            st[par] = (pso, tpso, pss, tpss)

        def r_seq(n, d, par, it):
            qd, tqd = QdT[par]
            pso, tpso, pss, tpss = st[par]
            c.op("pe", lambda e: e.matmul(pso[:, 0:128], lhsT=qd[:], rhs=Sb[:], start=False, stop=True), reads=[tqd, t_Sb], writes=[tpso])
            c.op("dve", lambda e: e.scalar_tensor_tensor(out=S[:], in0=S[:], scalar=g128[:, d:d + 1], in1=pss[:, 0:128], op0=ALU.mult, op1=ALU.add),
                 reads=[tpss, t_g128], writes=[t_S])
            c.op("dve", lambda e: e.tensor_copy(out=Sb[:], in_=S[:]), reads=[t_S], writes=[t_Sb])
            if d == 0:
                c.op("act", lambda e: e.activation(out=Of[:, n, :], in_=pso[:, 0:128], func=AF.Copy), reads=[tpso], writes=[t_Of])
            else:
                c.op("dve", lambda e: e.tensor_tensor(out=osum[:], in0=pso[:, 0:128], in1=Of[:, n, :], op=ALU.add), reads=[tpso, t_Of], writes=[t_osum])
                c.op("act", lambda e: e.activation(out=junk[:], in_=osum[:], func=AF.Square, accum_out=ss[:]), reads=[t_osum], writes=[t_junk, t_ss])
                c.op("dve", lambda e: e.tensor_scalar(out=ss[:], in0=ss[:], scalar1=1.0 / 128, scalar2=EPS, op0=ALU.mult, op1=ALU.add), reads=[], writes=[t_ss])
                c.op("act", lambda e: e.activation(out=ss[:], in_=ss[:], func=AF.Sqrt), reads=[], writes=[t_ss])
                c.op("dve", lambda e: e.reciprocal(out=ss[:], in_=ss[:]), reads=[], writes=[t_ss])
                c.op("dve", lambda e: e.scalar_tensor_tensor(out=yb[:], in0=osum[:], scalar=ss[:, 0:1], in1=SG[:, n, :], op0=ALU.mult, op1=ALU.mult),
                     reads=[t_osum, t_ss, t_SG], writes=[t_yb])
                pT, tpT = s.pst()
                c.op("pe", lambda e: e.transpose(out=pT[:, 0:128], in_=yb[:], identity=s.ident_b[:]), reads=[t_yb, s.t_ident_b], writes=[tpT])
                y, ty = ysb[it % 2]
                c.op("dve", lambda e: e.tensor_copy(out=y[:], in_=pT[:, 0:128]), reads=[tpT], writes=[ty])
                s.scatter_y(y[:], ty, 0, n * 128, 128)

        it = 0
        for d in range(2):
            c.op("dve", lambda e: e.memset(S[:], 0.0), reads=[], writes=[t_S])
            c.op("dve", lambda e: e.memset(Sb[:], 0.0), reads=[], writes=[t_Sb])
            order = orders[d]
            r_prep(order[0], d, it % 2)
            for i, n in enumerate(order):
                if i + 1 < len(order):
                    r_prep(order[i + 1], d, (it + 1) % 2)
                r_seq(n, d, it % 2, it)
                it += 1

    def phase_dn(self, l):
        s, c = self, self.c
        s.alloc_scatter()
        cw0 = l * 15
        def mask(name, pattern, cm, op):
            t, tt = c.sbuf(name, [128, 128], F32)
            c.op("pool", lambda e: e.affine_select(out=t[:], in_=s.ones_f[:], pattern=pattern, compare_op=op, fill=0.0, base=0, channel_multiplier=cm),
                 reads=[s.t_ones_f], writes=[tt])
            return t, tt
        Mge = mask("d_Mge", [[1, 128]], -1, ALU.is_ge)
        Mgt = mask("d_Mgt", [[1, 128]], -1, ALU.is_gt)
        Mle = mask("d_Mle", [[-1, 128]], 1, ALU.is_ge)
        Mlt = mask("d_Mlt", [[-1, 128]], 1, ALU.is_gt)
        TRI = [Mge, Mle]
        STL = [Mlt, Mgt]
        QTd, t_QTd = c.sbuf("d_QT", [128, T], BF16)
        KTd, t_KTd = c.sbuf("d_KT", [128, T], BF16)
        Ktm, t_Ktm = c.sbuf("d_Ktm", [128, 66, 128], BF16)
        Vtm, t_Vtm = c.sbuf("d_Vtm", [128, 66, 128], BF16)
        Of, t_Of = c.sbuf("d_Of", [128, 66, 128], F32)
        xin = [c.sbuf(f"d_xin{i}", [128, 516], F32) for i in range(2)]
        acc = [c.sbuf(f"d_acc{i}", [128, 512], F32) for i in range(2)]
        sq, t_sq = c.sbuf("d_sq", [128, 512], F32)
        rn, t_rn = c.sbuf("d_rn", [128, 512], F32)
        vt, t_vt = c.sbuf("d_vt", [128, 512], BF16)
        PFv = s.PF.rearrange("(s p) t -> s p t", p=128)
        seqs = [(0, LC)] + [(LC, T)]
        tiles = []
        for (sa, sb_) in seqs:
            t0 = sa
            while t0 < sb_:
                n = min(512, sb_ - t0)
                tiles.append((t0, n, t0 == sa, t0 + n == sb_))
                t0 += n
        k = 0
        for q in range(3):
            for (t0, n, first, last) in tiles:
                xi, txi = xin[k % 2]
                ac, tac = acc[k % 2]
                k += 1
                lo = 0 if not first else 2
                hi = n + 4 if not last else n + 2
                if first:
                    c.op("pool", lambda e, xi=xi: e.memset(xi[:, 0:2], 0.0), reads=[], writes=[txi])
                if last:
                    c.op("pool", lambda e, xi=xi, n=n: e.memset(xi[:, n + 2:n + 4], 0.0), reads=[], writes=[txi])
                c.dma("sp", xi[:, lo:hi], PFv[q][:, t0 - 2 + lo:t0 - 2 + hi], reads=[s.t_PF], writes=[txi])
                wq = cw0 + q * 5
                c.op("dve", lambda e, xi=xi, ac=ac, n=n, wq=wq: e.tensor_scalar(out=ac[:, 0:n], in0=xi[:, 0:n], scalar1=s.convw_s[:, wq:wq + 1], scalar2=None, op0=ALU.mult),
                     reads=[txi, s.t_convw], writes=[tac])
                for tap in range(1, 5):
                    c.op("dve", lambda e, xi=xi, ac=ac, n=n, wq=wq, tap=tap: e.scalar_tensor_tensor(out=ac[:, 0:n], in0=xi[:, tap:tap + n], scalar=s.convw_s[:, wq + tap:wq + tap + 1],
                                                                                                  in1=ac[:, 0:n], op0=ALU.mult, op1=ALU.add), reads=[txi, s.t_convw], writes=[tac])
                c.op("act", lambda e, ac=ac, n=n: e.activation(out=ac[:, 0:n], in_=ac[:, 0:n], func=AF.Silu), reads=[], writes=[tac])
                if q < 2:
                    c.op("dve", lambda e, ac=ac, n=n: e.tensor_tensor(out=sq[:, 0:n], in0=ac[:, 0:n], in1=ac[:, 0:n], op=ALU.mult), reads=[tac], writes=[t_sq])
                    ps, tps = s.ps()
                    c.op("pe", lambda e, ps=ps, n=n: e.matmul(ps[:, 0:n], lhsT=s.ones_f[:], rhs=sq[:, 0:n], start=True, stop=True), reads=[t_sq, s.t_ones_f], writes=[tps])
                    c.op("act", lambda e, ps=ps, n=n: e.activation(out=rn[:, 0:n], in_=ps[:, 0:n], func=AF.Sqrt, bias=s.eps_t[:, 0:1]), reads=[tps, s.t_eps], writes=[t_rn])
                    c.op("dve", lambda e, n=n: e.reciprocal(out=rn[:, 0:n], in_=rn[:, 0:n]), reads=[], writes=[t_rn])
                    dst, tdst = (QTd, t_QTd) if q == 0 else (KTd, t_KTd)
                    scl = 128.0 ** -0.5 if q == 0 else 1.0
                    c.op("dve", lambda e, ac=ac, n=n, dst=dst, t0=t0, scl=scl: e.scalar_tensor_tensor(out=dst[:, t0:t0 + n], in0=ac[:, 0:n], scalar=scl, in1=rn[:, 0:n],
                                                                                                     op0=ALU.mult, op1=ALU.mult), reads=[tac, t_rn], writes=[tdst])
                    src, tsrc = dst, tdst
                    soff = t0
                else:
                    c.op("dve", lambda e, ac=ac, n=n: e.tensor_copy(out=vt[:, 0:n], in_=ac[:, 0:n]), reads=[tac], writes=[t_vt])
                    src, tsrc = vt, t_vt
                    soff = 0
                if q >= 1:
                    dtm, tdtm = (Ktm, t_Ktm) if q == 1 else (Vtm, t_Vtm)
                    for sub in range(n // 128):
                        pT, tpT = s.pst()
                        c.op("pe", lambda e, pT=pT, src=src, soff=soff, sub=sub: e.transpose(out=pT[:, 0:128], in_=src[:, soff + sub * 128:soff + (sub + 1) * 128], identity=s.ident_b[:]),
                             reads=[tsrc, s.t_ident_b], writes=[tpT])
                        nb_ = (t0 + sub * 128) // 128
                        c.op("dve", lambda e, pT=pT, dtm=dtm, nb_=nb_: e.tensor_copy(out=dtm[:, nb_, :], in_=pT[:, 0:128]), reads=[tpT], writes=[tdtm])
        AB, t_AB = c.sbuf("d_AB", [128, 66, 4], F32)
        c.dma("sp", AB[:], s.PT.rearrange("(n p) c -> p n c", p=128)[:, :, 1152:1156], reads=[s.t_PT], writes=[t_AB])
        nega, t_nega = c.sbuf("d_nega", [128, 2], F32)
        c.op("act", lambda e: e.activation(out=nega[:], in_=s.hp_s[:, 8 + l * 2:8 + l * 2 + 2], func=AF.Exp), reads=[s.t_hp], writes=[t_nega])
        c.op("dve", lambda e: e.tensor_scalar(out=nega[:], in0=nega[:], scalar1=-1.0, scalar2=None, op0=ALU.mult), reads=[], writes=[t_nega])
        G, t_G = c.sbuf("d_G", [128, 2, 66], F32)
        Bt, t_Bt = c.sbuf("d_B", [128, 2, 66], F32)
        NB, t_NB = c.sbuf("d_NB", [128, 2, 66], F32)
        for d in range(2):
            c.op("act", lambda e, d=d: e.activation(out=G[:, d, :], in_=AB[:, :, d], func=AF.Exp, bias=s.hp_s[:, 16 + l * 2 + d:16 + l * 2 + d + 1]),
                 reads=[t_AB, s.t_hp], writes=[t_G])
            c.op("act", lambda e, d=d: e.activation(out=G[:, d, :], in_=G[:, d, :], func=AF.Ln, bias=s.one_t[:, 0:1]), reads=[s.t_eps], writes=[t_G])
            c.op("dve", lambda e, d=d: e.tensor_scalar(out=G[:, d, :], in0=G[:, d, :], scalar1=nega[:, d:d + 1], scalar2=None, op0=ALU.mult), reads=[t_nega], writes=[t_G])
            c.op("act", lambda e, d=d: e.activation(out=Bt[:, d, :], in_=AB[:, :, 2 + d], func=AF.Sigmoid), reads=[t_AB], writes=[t_Bt])
        c.op("dve", lambda e: e.tensor_scalar(out=NB[:], in0=Bt[:], scalar1=-1.0, scalar2=None, op0=ALU.mult), reads=[t_Bt], writes=[t_NB])
        def T2(name, dt=F32, shape=(128, 128)):
            return [c.sbuf(f"{name}{i}", list(shape), dt) for i in range(2)]
        Xg = T2("d_Xg"); sm = T2("d_sm", F32, (128, 8)); Er = T2("d_Er"); DTm = T2("d_DTm"); DLm = T2("d_DLm")
        Pm = T2("d_P"); PTm = T2("d_PT"); Mi = T2("d_M"); Xa = T2("d_Xa"); Xb = T2("d_Xb"); XTa = T2("d_XTa"); XTb = T2("d_XTb")
        TTb = T2("d_TT", BF16); bv = T2("d_bv", BF16); kbg = T2("d_kbg", BF16)
        wv = T2("d_wv"); kcT = T2("d_kcT", BF16); qkT = T2("d_qkT", BF16); qgT = T2("d_qgT", BF16); kg = T2("d_kg", BF16)
        vnb, t_vnb = c.sbuf("d_vnb", [128, 128], BF16)
        S, t_S = c.sbuf("d_S", [128, 128], F32)
        Sb, t_Sb = c.sbuf("d_Sb", [128, 128], BF16)
        zt = [c.sbuf(f"d_z{i}", [128, 128], F32) for i in range(2)]
        osum, t_osum = c.sbuf("d_osum", [128, 128], F32)
        junk, t_junk = c.sbuf("d_junk", [128, 128], F32)
        ss, t_ss = c.sbuf("d_ss", [128, 1], F32)
        yb, t_yb = c.sbuf("d_yb", [128, 128], BF16)
        ysb = [c.sbuf(f"d_ysb{i}", [128, 128], BF16) for i in range(2)]
        nwd = s.nw_s[:, (l * 3) * 128:(l * 3 + 1) * 128]

        def evac(eng, dst, tdst, ps, tps):
            if eng == "act":
                c.op("act", lambda e: e.activation(out=dst[:], in_=ps[:, 0:128], func=AF.Copy), reads=[tps], writes=[tdst])
            else:
                c.op("dve", lambda e: e.tensor_copy(out=dst[:], in_=ps[:, 0:128]), reads=[tps], writes=[tdst])

        def block_prep(n, d, par):
            blk = slice(n * 128, (n + 1) * 128)
            tri, ttri = TRI[d]
            stl, tstl = STL[d]
            g = G[:, d, n:n + 1]
            xg, txg = Xg[par]
            smt, tsm = sm[par]
            c.op("dve", lambda e: e.tensor_scalar(out=xg[:], in0=tri[:], scalar1=g, scalar2=None, op0=ALU.mult), reads=[ttri, t_G], writes=[txg])
            psm, tpsm = s.ps()
            c.op("pe", lambda e: e.matmul(psm[:, 0:1], lhsT=tri[:], rhs=G[:, d, n:n + 1], start=True, stop=True), reads=[ttri, t_G], writes=[tpsm])
            c.op("pe", lambda e: e.matmul(psm[:, 2:3], lhsT=s.ones_f[:], rhs=G[:, d, n:n + 1], start=True, stop=True), reads=[s.t_ones_f, t_G], writes=[tpsm])
            psR, tpsR = s.ps()
            c.op("pe", lambda e: e.matmul(psR[:, 0:128], lhsT=s.ones_f[:], rhs=xg[:], start=True, stop=True), reads=[s.t_ones_f, txg], writes=[tpsR])
            c.op("dve", lambda e: e.tensor_copy(out=smt[:, 0:1], in_=psm[:, 0:1]), reads=[tpsm], writes=[tsm])
            c.op("dve", lambda e: e.tensor_copy(out=smt[:, 1:2], in_=psm[:, 2:3]), reads=[tpsm], writes=[tsm])
            c.op("act", lambda e: e.activation(out=smt[:, 2:3], in_=smt[:, 0:1], func=AF.Exp), reads=[], writes=[tsm])
            c.op("act", lambda e: e.activation(out=smt[:, 3:4], in_=smt[:, 0:1], func=AF.Exp, scale=-1.0, bias=smt[:, 1:2]), reads=[], writes=[tsm])
            c.op("act", lambda e: e.activation(out=smt[:, 4:5], in_=smt[:, 1:2], func=AF.Exp), reads=[], writes=[tsm])
            c.op("dve", lambda e: e.tensor_tensor(out=smt[:, 5:6], in0=smt[:, 2:3], in1=Bt[:, d, n:n + 1], op=ALU.mult), reads=[t_Bt], writes=[tsm])
            er, ter = Er[par]
            c.op("act", lambda e: e.activation(out=er[:], in_=psR[:, 0:128], func=AF.Exp), reads=[tpsR], writes=[ter])
            dtm, tdtm = DTm[par]
            c.op("dve", lambda e: e.tensor_scalar(out=dtm[:], in0=psR[:, 0:128], scalar1=smt[:, 0:1], scalar2=0.0, op0=ALU.subtract, op1=ALU.min), reads=[tpsR, tsm], writes=[tdtm])
            c.op("act", lambda e: e.activation(out=dtm[:], in_=dtm[:], func=AF.Exp), reads=[], writes=[tdtm])
            c.op("dve", lambda e: e.tensor_tensor(out=dtm[:], in0=dtm[:], in1=tri[:], op=ALU.mult), reads=[ttri], writes=[tdtm])
            dlm, tdlm = DLm[par]
            c.op("dve", lambda e: e.tensor_scalar(out=dlm[:], in0=psR[:, 0:128], scalar1=smt[:, 0:1], scalar2=0.0, op0=ALU.subtract, op1=ALU.max), reads=[tpsR, tsm], writes=[tdlm])
            c.op("act", lambda e: e.activation(out=dlm[:], in_=dlm[:], func=AF.Exp, scale=-1.0), reads=[], writes=[tdlm])
            c.op("pool", lambda e: e.tensor_tensor(out=dlm[:], in0=dlm[:], in1=stl[:], op=ALU.mult), reads=[tstl], writes=[tdlm])
            psK, tpsK = s.ps()
            c.op("pe", lambda e: e.matmul(psK[:, 0:128], lhsT=KTd[:, blk], rhs=KTd[:, blk], start=True, stop=True), reads=[t_KTd], writes=[tpsK])
            pm, tpm = Pm[par]
            c.op("dve", lambda e: e.scalar_tensor_tensor(out=pm[:], in0=psK[:, 0:128], scalar=NB[:, d, n:n + 1], in1=dlm[:], op0=ALU.mult, op1=ALU.mult),
                 reads=[tpsK, t_NB, tdlm], writes=[tpm])
            psT_, tpsT_ = s.ps()
            c.op("pe", lambda e: e.transpose(out=psT_[:, 0:128], in_=pm[:], identity=s.ident_f[:]), reads=[tpm, s.t_ident_f], writes=[tpsT_])
            ptm, tptm = PTm[par]
            evac("act", ptm, tptm, psT_, tpsT_)
            mi, tmi = Mi[par]
            c.op("dve", lambda e: e.tensor_tensor(out=mi[:], in0=ptm[:], in1=s.ident_f[:], op=ALU.add), reads=[tptm, s.t_ident_f], writes=[tmi])
            X, tX = ptm, tptm
            XT_, tXT = pm, tpm
            bufs = [(Xa[par], XTa[par]), (Xb[par], XTb[par])]
            for m in range(1, 7):
                (xn, txn), (xtn, txtn) = bufs[m % 2]
                if m < 6:
                    p1, tp1 = s.ps()
                    c.op("pe", lambda e, p1=p1, X=X, XT_=XT_: e.matmul(p1[:, 0:128], lhsT=XT_[:], rhs=X[:], start=True, stop=True), reads=[tX, tXT], writes=[tp1])
                    evac("act", xn, txn, p1, tp1)
                p2, tp2 = s.ps()
                c.op("pe", lambda e, p2=p2, X=X, XT_=XT_: e.matmul(p2[:, 0:128], lhsT=X[:], rhs=XT_[:], start=True, stop=True), reads=[tX, tXT], writes=[tp2])
                evac("dve", xtn, txtn, p2, tp2)
                p3, tp3 = s.ps()
                c.op("pe", lambda e, p3=p3, xtn=xtn: e.matmul(p3[:, 0:128], lhsT=xtn[:], rhs=mi[:], start=True, stop=True), reads=[txtn, tmi], writes=[tp3])
                c.op("dve", lambda e, p3=p3: e.tensor_tensor(out=mi[:], in0=mi[:], in1=p3[:, 0:128], op=ALU.add), reads=[tp3], writes=[tmi])
                X, tX, XT_, tXT = xn, txn, xtn, txtn
            tt, ttt = TTb[par]
            c.op("act", lambda e: e.activation(out=tt[:], in_=mi[:], func=AF.Copy), reads=[tmi], writes=[ttt])
            b_, tb_ = bv[par]
            c.op("pool", lambda e: e.tensor_scalar(out=b_[:], in0=Vtm[:, n, :], scalar1=Bt[:, d, n:n + 1], scalar2=None, op0=ALU.mult), reads=[t_Vtm, t_Bt], writes=[tb_])
            kb_, tkb_ = kbg[par]
            c.op("pool", lambda e: e.tensor_scalar(out=kb_[:], in0=Ktm[:, n, :], scalar1=smt[:, 5:6], scalar2=None, op0=ALU.mult), reads=[t_Ktm, tsm], writes=[tkb_])
            pw, tpw = s.ps()
            c.op("pe", lambda e: e.matmul(pw[:, 0:128], lhsT=tt[:], rhs=b_[:], start=True, stop=True), reads=[ttt, tb_], writes=[tpw])
            evac("act", wv[par][0], wv[par][1], pw, tpw)
            pk, tpk = s.ps()
            c.op("pe", lambda e: e.matmul(pk[:, 0:128], lhsT=kb_[:], rhs=tt[:], start=True, stop=True), reads=[tkb_, ttt], writes=[tpk])
            evac("dve", kcT[par][0], kcT[par][1], pk, tpk)
            pq, tpq = s.ps()
            c.op("pe", lambda e: e.matmul(pq[:, 0:128], lhsT=KTd[:, blk], rhs=QTd[:, blk], start=True, stop=True), reads=[t_KTd, t_QTd], writes=[tpq])
            c.op("dve", lambda e: e.tensor_tensor(out=qkT[par][0][:], in0=pq[:, 0:128], in1=dtm[:], op=ALU.mult), reads=[tpq, tdtm], writes=[qkT[par][1]])
            c.op("pool", lambda e: e.tensor_tensor(out=qgT[par][0][:], in0=QTd[:, blk], in1=er[:], op=ALU.mult), reads=[t_QTd, ter], writes=[qgT[par][1]])
            c.op("pool", lambda e: e.tensor_scalar(out=kg[par][0][:], in0=Ktm[:, n, :], scalar1=smt[:, 3:4], scalar2=None, op0=ALU.mult), reads=[t_Ktm, tsm], writes=[kg[par][1]])

        def block_seq(n, d, par, it):
            smt, tsm = sm[par]
            p1, tp1 = s.ps()
            c.op("pe", lambda e: e.matmul(p1[:, 0:128], lhsT=kcT[par][0][:], rhs=Sb[:], start=True, stop=True), reads=[kcT[par][1], t_Sb], writes=[tp1])
            c.op("dve", lambda e: e.tensor_tensor(out=vnb[:], in0=wv[par][0][:], in1=p1[:, 0:128], op=ALU.subtract), reads=[wv[par][1], tp1], writes=[t_vnb])
            po, tpo = s.ps()
            c.op("pe", lambda e: e.matmul(po[:, 0:128], lhsT=qgT[par][0][:], rhs=Sb[:], start=True, stop=False), reads=[qgT[par][1], t_Sb], writes=[tpo])
            c.op("pe", lambda e: e.matmul(po[:, 0:128], lhsT=qkT[par][0][:], rhs=vnb[:], start=False, stop=True), reads=[qkT[par][1], t_vnb], writes=[tpo])
            pS, tpS = s.ps()
            c.op("pe", lambda e: e.matmul(pS[:, 0:128], lhsT=kg[par][0][:], rhs=vnb[:], start=True, stop=True), reads=[kg[par][1], t_vnb], writes=[tpS])
            c.op("dve", lambda e: e.scalar_tensor_tensor(out=S[:], in0=S[:], scalar=smt[:, 4:5], in1=pS[:, 0:128], op0=ALU.mult, op1=ALU.add), reads=[tpS, tsm], writes=[t_S])
            c.op("dve", lambda e: e.tensor_copy(out=Sb[:], in_=S[:]), reads=[t_S], writes=[t_Sb])
            if d == 0:
                c.op("act", lambda e: e.activation(out=Of[:, n, :], in_=po[:, 0:128], func=AF.Copy), reads=[tpo], writes=[t_Of])
            else:
                z, tz = zt[it % 2]
                c.dma("sp", z[:], s.PT[n * 128:(n + 1) * 128, 1024:1152], reads=[s.t_PT], writes=[tz])
                c.op("act", lambda e: e.activation(out=z[:], in_=z[:], func=AF.Silu), reads=[], writes=[tz])
                c.op("dve", lambda e: e.tensor_tensor(out=osum[:], in0=po[:, 0:128], in1=Of[:, n, :], op=ALU.add), reads=[tpo, t_Of], writes=[t_osum])
                c.op("act", lambda e: e.activation(out=junk[:], in_=osum[:], func=AF.Square, accum_out=ss[:]), reads=[t_osum], writes=[t_junk, t_ss])
                c.op("dve", lambda e: e.tensor_scalar(out=ss[:], in0=ss[:], scalar1=1.0 / 128, scalar2=EPS, op0=ALU.mult, op1=ALU.add), reads=[], writes=[t_ss])
                c.op("act", lambda e: e.activation(out=ss[:], in_=ss[:], func=AF.Sqrt), reads=[], writes=[t_ss])
                c.op("dve", lambda e: e.reciprocal(out=ss[:], in_=ss[:]), reads=[], writes=[t_ss])
                c.op("dve", lambda e: e.scalar_tensor_tensor(out=osum[:], in0=osum[:], scalar=ss[:, 0:1], in1=nwd, op0=ALU.mult, op1=ALU.mult), reads=[t_ss, s.t_nw], writes=[t_osum])
                c.op("dve", lambda e: e.tensor_tensor(out=yb[:], in0=osum[:], in1=z[:], op=ALU.mult), reads=[tz, t_osum], writes=[t_yb])
                pT, tpT = s.pst()
                c.op("pe", lambda e: e.transpose(out=pT[:, 0:128], in_=yb[:], identity=s.ident_b[:]), reads=[t_yb, s.t_ident_b], writes=[tpT])
                y, ty = ysb[it % 2]
                c.op("dve", lambda e: e.tensor_copy(out=y[:], in_=pT[:, 0:128]), reads=[tpT], writes=[ty])
                s.scatter_y(y[:], ty, 1, n * 128, 128)

        orders = [[0, 1] + list(range(2, 66)), [1, 0] + list(range(65, 1, -1))]
        it = 0
        for d in range(2):
            c.op("dve", lambda e: e.memset(S[:], 0.0), reads=[], writes=[t_S])
            c.op("dve", lambda e: e.memset(Sb[:], 0.0), reads=[], writes=[t_Sb])
            order = orders[d]
            block_prep(order[0], d, it % 2)
            for i, n in enumerate(order):
                if i + 1 < len(order):
                    block_prep(order[i + 1], d, (it + 1) % 2)
                block_seq(n, d, it % 2, it)
                it += 1

    def layer_norm_fm(self, xb, t_xb, n, l, which_w, which_b):
        s, c = self, self.c
        ps1, tps1 = s.ps()
        ps2, tps2 = s.ps()
        for f in range(KC):
            zb, tzb = s.c_zb[f % 2]
            zq, tzq = s.c_zq[f % 2]
            c.op("dve", lambda e, zb=zb, f=f: e.tensor_copy(out=zb[:, 0:n], in_=xb[:, f, 0:n]), reads=[t_xb], writes=[tzb])
            c.op("act", lambda e, zq=zq, f=f: e.activation(out=zq[:, 0:n], in_=xb[:, f, 0:n], func=AF.Square), reads=[t_xb], writes=[tzq])
            c.op("pe", lambda e, zb=zb, f=f: e.matmul(ps1[:, 0:n], lhsT=s.ones_b[:], rhs=zb[:, 0:n], start=(f == 0), stop=(f == KC - 1)),
                 reads=[tzb, s.t_ones_b], writes=[tps1])
            c.op("pe", lambda e, zq=zq, f=f: e.matmul(ps2[:, 0:n], lhsT=s.ones_b[:], rhs=zq[:, 0:n], start=(f == 0), stop=(f == KC - 1)),
                 reads=[tzq, s.t_ones_b], writes=[tps2])
        mean, msq, rstd, nmr = s.c_ln
        t_ln = s.t_c_ln
        c.op("act", lambda e: e.activation(out=mean[:, 0:n], in_=ps1[:, 0:n], func=AF.Copy, scale=1.0 / D), reads=[tps1], writes=[t_ln])
        c.op("dve", lambda e: e.tensor_tensor(out=msq[:, 0:n], in0=mean[:, 0:n], in1=mean[:, 0:n], op=ALU.mult), reads=[], writes=[t_ln])
        c.op("dve", lambda e: e.scalar_tensor_tensor(out=rstd[:, 0:n], in0=ps2[:, 0:n], scalar=1.0 / D, in1=msq[:, 0:n], op0=ALU.mult, op1=ALU.subtract),
             reads=[tps2], writes=[t_ln])
        c.op("dve", lambda e: e.tensor_scalar(out=rstd[:, 0:n], in0=rstd[:, 0:n], scalar1=0.0, scalar2=EPS / (ALPHA * ALPHA), op0=ALU.max, op1=ALU.add), reads=[], writes=[t_ln])
        c.op("act", lambda e: e.activation(out=rstd[:, 0:n], in_=rstd[:, 0:n], func=AF.Sqrt), reads=[], writes=[t_ln])
        c.op("dve", lambda e: e.reciprocal(out=rstd[:, 0:n], in_=rstd[:, 0:n]), reads=[], writes=[t_ln])
        c.op("dve", lambda e: e.tensor_tensor(out=nmr[:, 0:n], in0=mean[:, 0:n], in1=rstd[:, 0:n], op=ALU.mult), reads=[], writes=[t_ln])
        for f in range(KC):
            eng = "dve" if f % 2 == 0 else "pool"
            c.op(eng, lambda e, f=f: e.tensor_tensor(out=xb[:, f, 0:n], in0=xb[:, f, 0:n], in1=rstd[:, 0:n], op=ALU.mult), reads=[t_ln], writes=[t_xb])
            c.op(eng, lambda e, f=f: e.tensor_tensor(out=xb[:, f, 0:n], in0=xb[:, f, 0:n], in1=nmr[:, 0:n], op=ALU.subtract), reads=[t_ln], writes=[t_xb])
            wi = (which_w * 4 + l) * 16 + f
            bi_ = (which_b * 4 + l) * 16 + f
            c.op("act", lambda e, f=f, wi=wi, bi_=bi_: e.activation(out=xb[:, f, 0:n], in_=xb[:, f, 0:n], func=AF.Identity,
                                                                   scale=s.lnp_s[:, wi:wi + 1], bias=s.lnp_s[:, bi_:bi_ + 1]),
                 reads=[s.t_lnp], writes=[t_xb])

    def phase_c(self, l):
        s, c = self, self.c
        xb, t_xb = c.sbuf("c_x", [128, KC, 512], F32)
        yb, t_yb = c.sbuf("c_y", [128, KC, 512], BF16)
        hid, t_hid = c.sbuf("c_hid", [128, 44, 512], BF16)
        s.c_zb = [c.sbuf(f"c_zb{i}", [128, 512], BF16) for i in range(2)]
        s.c_zq = [c.sbuf(f"c_zq{i}", [128, 512], BF16) for i in range(2)]
        lnt, s.t_c_ln = c.sbuf("c_ln", [128, 4, 512], F32)
        s.c_ln = [lnt[:, i, :] for i in range(4)]
        sg = [c.sbuf(f"c_sg{i}", [128, 512], F32) for i in range(2)]
        wo = [c.sbuf(f"c_wo{i}", [128, 2048], BF16) for i in range(2)]
        wf1 = [c.sbuf(f"c_wf1{i}", [128, 4096], BF16) for i in range(2)]
        wf2 = [c.sbuf(f"c_wf2{i}", [128, DFF], BF16) for i in range(2)]
        XTv = s.XT.rearrange("(kc p) t -> p kc t", p=128)
        YOv = s.yown.rearrange("(kc p) t -> p kc t", p=128)
        for (t0_, n_, w_) in s.own_tiles():
            s.phase_c_tile(l, t0_, n_, w_, xb, t_xb, yb, t_yb, hid, t_hid, sg, wo, wf1, wf2, XTv, YOv)

    def phase_c_tile(self, l, t0, n, w, xb, t_xb, yb, t_yb, hid, t_hid, sg, wo, wf1, wf2, XTv, YOv):
        s, c = self, self.c
        if True:
            c.dma("sp", yb[:, :, 0:n], YOv[:, :, t0:t0 + n], reads=[s.t_yown], writes=[t_yb])
            c.dma("sp", xb[:, :, 0:n], XTv[:, :, t0:t0 + n], reads=[s.t_XT], writes=[t_xb])
            for f in range(KC):
                wt, twt = wo[f % 2]
                c.dma("sp", wt[:], s.Wo[l][f * 128:(f + 1) * 128, :], reads=[s.t_W[l]], writes=[twt])
                ps, tps = s.ps()
                for kc in range(KC):
                    c.op("pe", lambda e, ps=ps, wt=wt, kc=kc: e.matmul(ps[:, 0:n], lhsT=wt[:, kc * 128:(kc + 1) * 128], rhs=yb[:, kc, 0:n],
                                                                      start=(kc == 0), stop=(kc == KC - 1)), reads=[twt, t_yb], writes=[tps], defer=(kc != KC - 1))
                c.op("dve", lambda e, ps=ps, f=f: e.scalar_tensor_tensor(out=xb[:, f, 0:n], in0=ps[:, 0:n], scalar=s.MM[:, l, 32 + f, w:w + 1],
                                                                        in1=xb[:, f, 0:n], op0=ALU.mult, op1=ALU.add), reads=[tps, s.t_MM], writes=[t_xb])
            s.layer_norm_fm(xb, t_xb, n, l, 0, 1)
            for f in range(KC):
                eng = "dve" if f % 2 == 0 else "pool"
                c.op(eng, lambda e, f=f: e.tensor_scalar(out=yb[:, f, 0:n], in0=xb[:, f, 0:n], scalar1=s.MM[:, l, 64 + f, w:w + 1],
                                                        scalar2=s.MM[:, l, 48 + f, w:w + 1], op0=ALU.mult, op1=ALU.add), reads=[t_xb, s.t_MM], writes=[t_yb])
            for hc in range(44):
                wt, twt = wf1[hc % 2]
                c.dma("sp", wt[:], s.Wf1[l][hc * 128:(hc + 1) * 128, :], reads=[s.t_W[l]], writes=[twt])
                psG, tpsG = s.ps()
                psU, tpsU = s.ps()
                for kc in range(KC):
                    c.op("pe", lambda e, psG=psG, wt=wt, kc=kc: e.matmul(psG[:, 0:n], lhsT=wt[:, kc * 256:kc * 256 + 128], rhs=yb[:, kc, 0:n],
                                                                        start=(kc == 0), stop=(kc == KC - 1)), reads=[twt, t_yb], writes=[tpsG], defer=(kc != KC - 1))
                for kc in range(KC):
                    c.op("pe", lambda e, psU=psU, wt=wt, kc=kc: e.matmul(psU[:, 0:n], lhsT=wt[:, kc * 256 + 128:kc * 256 + 256], rhs=yb[:, kc, 0:n],
                                                                        start=(kc == 0), stop=(kc == KC - 1)), reads=[twt, t_yb], writes=[tpsU], defer=(kc != KC - 1))
                g, tg = sg[hc % 2]
                c.op("act", lambda e, psG=psG, g=g: e.activation(out=g[:, 0:n], in_=psG[:, 0:n], func=AF.Silu), reads=[tpsG], writes=[tg])
                c.op("dve", lambda e, psU=psU, g=g, hc=hc: e.tensor_tensor(out=hid[:, hc, 0:n], in0=psU[:, 0:n], in1=g[:, 0:n], op=ALU.mult),
                     reads=[tpsU, tg], writes=[t_hid])
            for f in range(KC):
                wt, twt = wf2[f % 2]
                c.dma("sp", wt[:], s.Wf2[l][f * 128:(f + 1) * 128, :], reads=[s.t_W[l]], writes=[twt])
                ps, tps = s.ps()
                for hc in range(44):
                    c.op("pe", lambda e, ps=ps, wt=wt, hc=hc: e.matmul(ps[:, 0:n], lhsT=wt[:, hc * 128:(hc + 1) * 128], rhs=hid[:, hc, 0:n],
                                                                      start=(hc == 0), stop=(hc == 43)), reads=[twt, t_hid], writes=[tps], defer=(hc != 43))
                c.op("dve", lambda e, ps=ps, f=f: e.scalar_tensor_tensor(out=xb[:, f, 0:n], in0=ps[:, 0:n], scalar=s.MM[:, l, 80 + f, w:w + 1],
                                                                        in1=xb[:, f, 0:n], op0=ALU.mult, op1=ALU.add), reads=[tps, s.t_MM], writes=[t_xb])
            s.layer_norm_fm(xb, t_xb, n, l, 2, 3)
            c.dma("sp", XTv[:, :, t0:t0 + n], xb[:, :, 0:n], reads=[t_xb], writes=[s.t_XT])

    def dump_a1(self, l):
        s, c = self, self.c
        s.add_dbg("d_MM", s.MM[:].rearrange("p l c w -> p (l c w)"), [128, NL * 96 * 2], [s.t_MM])
        for i, r0 in enumerate([0, 256, 4096, 8320]):
            s.add_dbg(f"d_PT{i}", s.PT[r0:r0 + 128, :], [128, PW], [s.t_PT])
        s.add_dbg("d_PF0", s.PF[:, 0:512], [384, 512], [s.t_PF])
        s.add_dbg("d_PF1", s.PF[:, T - 512:T], [384, 512], [s.t_PF])
        s.add_dbg("d_COS", s.COS, [L, 64], [s.t_COS])
        s.add_dbg("d_SIN", s.SIN, [L, 64], [s.t_COS])

    def final_phase(self):
        s, c = self, self.c
        with contextlib.ExitStack() as stp:
            c.stack = stp
            xin = [c.sbuf(f"fx{i}", [128, KC, 128], F32) for i in range(2)]
            xo = [c.sbuf(f"fo{i}", [128, D], F32) for i in range(2)]
            XTv = s.XT.rearrange("(kc p) t -> p kc t", p=128)
            t_out = Trk("out")
            for ti in range(16):
                a, ta = xin[ti % 2]
                o, to = xo[ti % 2]
                c.dma("sp", a[:], XTv[:, :, 64 + ti * 128:64 + (ti + 1) * 128], reads=[s.t_XT], writes=[ta])
                for g in range(4):
                    ps, tps = s.ps()
                    for q in range(4):
                        kc = g * 4 + q
                        c.op("pe", lambda e, a=a, ps=ps, kc=kc, q=q: e.transpose(out=ps[:, q * 128:(q + 1) * 128], in_=a[:, kc, :], identity=s.ident_f[:]),
                             reads=[ta, s.t_ident_f], writes=[tps])
                    c.op("dve", lambda e, o=o, ps=ps, g=g: e.tensor_copy(out=o[:, g * 512:(g + 1) * 512], in_=ps[:, :]), reads=[tps], writes=[to])
                c.dma("sp", s.out[ti * 128:(ti + 1) * 128, :], o[:], reads=[to], writes=[t_out])
            c.barrier()


def _in_tile_cols(j):
    hk = j // 2
    return [
        (0 + j * 128, 128), (512 + j * 128, 128), (1024 + j * 128, 128), (1536 + j * 128, 128),
        (4112 + (2 * j) * 128, 128), (4112 + (2 * j + 1) * 128, 128), (5136 + hk * 128, 128), (5392 + hk * 128, 128),
        (3584 + j * 128, 128), ([4096 + j, 4100 + j, 4104 + j, 4108 + j], 4),
        (2048 + j * 128, 128), (2560 + j * 128, 128), (3072 + j * 128, 128),
    ]


def make_in_maps(inp, n_layers=NL):
    x, cvec, ctx, c_ctx = inp["x"], inp["c"], inp["ctx"], inp["c_ctx"]
    f32 = np.float32
    lnp = np.stack([inp["ln1_w"], inp["ln1_b"], inp["ln2_w"], inp["ln2_b"]], 0)
    lnp = lnp.reshape(4, NL, KC, 128).transpose(3, 0, 1, 2).reshape(128, 256)
    w_o, w_f1, w_f2 = inp["w_o"], inp["w_ffn_in"], inp["w_ffn_out"]
    Wot = np.empty((NL, 2048, 2048), f32)
    Wf1t = np.empty((NL, DFF, 4096), f32)
    Wf2t = np.empty((NL, 2048, DFF), f32)
    rows = []
    for blk in range(4):
        rows += [blk * 128, 512 + blk * 128, 1024 + (2 * blk) * 128, 1024 + (2 * blk + 1) * 128]
    for l in range(NL):
        wo_perm = np.concatenate([w_o[l, r:r + 128, :] for r in rows], 0)
        Wot[l] = wo_perm.reshape(KC, 128, KC, 128).transpose(2, 1, 0, 3).reshape(2048, 2048)
        Wf1t[l] = w_f1[l].reshape(KC, 128, 2, 44, 128).transpose(3, 1, 0, 2, 4).reshape(DFF, 4096)
        Wf2t[l] = w_f2[l].reshape(44, 128, KC, 128).transpose(2, 1, 0, 3).reshape(2048, DFF)
    w_ada = inp["w_ada"]
    maps = []
    for r in range(NCORE):
        b, j = r // 4, r % 4
        m = {}
        m["x_own"] = np.ascontiguousarray(np.concatenate([ctx[b, 64 * j:64 * j + 64], x[b, 2048 * j:2048 * (j + 1)]], 0))
        m["cT"] = np.ascontiguousarray(np.stack([cvec[b], c_ctx], -1).reshape(KC, 128, 2).transpose(1, 0, 2).reshape(128, 32))
        wa = w_ada[:, :, 3072 * j:3072 * (j + 1)].reshape(NL, KC, 128, 24, 128).transpose(0, 3, 2, 1, 4)
        m["w_ada_t"] = np.ascontiguousarray(wa).reshape(NL * 24 * 128, 2048)[:n_layers * 24 * 128]
        m["b_ada_s"] = np.ascontiguousarray(inp["b_ada"][:, 3072 * j:3072 * (j + 1)].reshape(NL, 24, 128).transpose(2, 0, 1).reshape(128, 96))
        sb = np.zeros((128, 2), f32); sb[:, b] = 1.0
        m["selb"] = sb
        s4 = np.zeros((128, 4), f32); s4[:, j] = 1.0
        m["sel4"] = s4
        wsel = np.zeros((NL, 13, 128, KC, 128), f32)
        for ti, (c0, nc_) in enumerate(_in_tile_cols(j)):
            idx = np.asarray(c0) if isinstance(c0, list) else np.arange(c0, c0 + nc_)
            wsel[:, ti, :, :, :nc_] = inp["w_in"][:, :, idx].reshape(NL, KC, 128, nc_).transpose(0, 2, 1, 3)
        m["w_in_sel"] = np.ascontiguousarray(wsel.transpose(0, 2, 1, 3, 4)).reshape(NL * 128, 13 * 2048)[:n_layers * 128]
        m["w_o_s"] = np.ascontiguousarray(Wot.reshape(NL, 4, 4, 128, 2048)[:, :, j]).reshape(NL * 128, 8192)[:n_layers * 128]
        m["w_f1_s"] = np.ascontiguousarray(Wf1t.reshape(NL, 11, 4, 128, 4096)[:, :, j]).reshape(NL * 128, 45056)[:n_layers * 128]
        m["w_f2_s"] = np.ascontiguousarray(Wf2t.reshape(NL, 8, 4, 64, DFF)[:, :, j]).reshape(NL * 128, 22528)[:n_layers * 128]
        m["lnp"] = np.ascontiguousarray(lnp)
        hp = np.concatenate([inp["ret_decay_logit"][:, :, j].reshape(-1), inp["dn_a_log"][:, :, j].reshape(-1),
                             inp["dn_dt_bias"][:, :, j].reshape(-1)]).astype(f32)
        m["hp"] = np.ascontiguousarray(np.broadcast_to(hp[None, :], (128, 24)))
        cw = inp["dn_conv_w"]
        cwj = np.stack([cw[:, :, s_ * 512 + j * 128: s_ * 512 + (j + 1) * 128] for s_ in range(3)], 1)
        m["convw"] = np.ascontiguousarray(cwj.transpose(3, 0, 1, 2).reshape(128, 60))
        nw = np.stack([inp["dn_norm_w"], inp["att_qn_w"], inp["att_kn_w"]], 1).reshape(-1)
        m["nw"] = np.ascontiguousarray(np.broadcast_to(nw[None, :], (128, NL * 3 * 128)))
        maps.append(m)
    return maps


_CACHE = {}


def kernel(**inputs):
    inp = {k: np.asarray(v) for k, v in inputs.items()}
    if "nc" not in _CACHE:
        _CACHE["nc"] = Builder().build()
    nc = _CACHE["nc"]
    maps = make_in_maps(inp)
    res = run_bass_kernel_spmd(nc, maps, core_ids=list(range(NCORE)))
    out = np.empty((2, L, D), np.float32)
    for r in range(NCORE):
        b, j = r // 4, r % 4
        out[b, 2048 * j:2048 * (j + 1)] = res.results[r]["out"]
    return out
```

```python
import contextlib
import math
import numpy as np
import ml_dtypes
import concourse.bass as bass
import concourse.mybir as mybir
from concourse.bass_utils import run_bass_kernel_spmd

F32 = mybir.dt.float32
BF16 = mybir.dt.bfloat16
I32 = mybir.dt.int32
AF = mybir.ActivationFunctionType
ALU = mybir.AluOpType

NCORE = 8
NL = 4
D = 2048
L = 8192
LC = 256
T = L + LC
TS = 2112
DFF = 5632
KC = 16
PW = 1168
ALPHA = (2 * NL) ** 0.25
EPS = 1e-6
GROUPS4 = [[0, 1, 2, 3], [4, 5, 6, 7]]
GROUPS8 = [[0, 1, 2, 3, 4, 5, 6, 7]]


class Trk:
    __slots__ = ("name", "w", "r", "chan")

    def __init__(self, name=""):
        self.name = name
        self.w = None
        self.r = {}
        self.chan = None


class Chan:
    __slots__ = ("sem", "cnt")

    def __init__(self, sem):
        self.sem = sem
        self.cnt = 0


class Ctx:
    ENG = ("pe", "act", "dve", "pool", "sp")

    def __init__(self, nc, stack):
        self.nc = nc
        self.stack = stack
        self.sem = {e: stack.enter_context(nc.semaphore("s_" + e)) for e in self.ENG}
        self.cnt = {e: 0 for e in self.ENG}
        self.seen = {e: {} for e in self.ENG}
        self.prog = {e: [] for e in self.ENG}
        self.chans = []
        self.free_chans = []
        self.phase_chans = []
        self.gstack = stack
        self.uid = 0
        self.same_engine_sync = True

    def sbuf(self, name, shape, dt):
        self.uid += 1
        t = self.stack.enter_context(self.nc.sbuf_tensor(f"{name}_{self.uid}", shape, dt))
        return t, Trk(name)

    def psum(self, name, shape, dt=F32):
        self.uid += 1
        t = self.stack.enter_context(self.nc.psum_tensor(f"{name}_{self.uid}", shape, dt))
        return t, Trk(name)

    def new_chan(self, name):
        if self.free_chans:
            c = self.free_chans.pop()
        else:
            self.uid += 1
            c = Chan(self.gstack.enter_context(self.nc.semaphore(f"c_{name}_{self.uid}")))
            self.chans.append(c)
        self.phase_chans.append(c)
        return c

    def release_phase_chans(self):
        self.free_chans.extend(self.phase_chans)
        self.phase_chans = []

    def _deps(self, E, reads, writes, extra=()):
        deps = {}

        def add(d):
            if d is None:
                return
            k = d[0]
            if k not in deps or deps[k][2] < d[2]:
                deps[k] = d

        for t in reads:
            add(t.w)
        for t in writes:
            add(t.w)
            for d in t.r.values():
                add(d)
        for d in extra:
            add(d)
        out = []
        for k, (kk, s, v) in deps.items():
            if k == E and (E == "pe" or not self.same_engine_sync):
                continue
            if self.seen[E].get(k, 0) >= v:
                continue
            self.seen[E][k] = v
            out.append((s, v))
        return out

    def op(self, E, fn, reads=(), writes=(), defer=False):
        for s, v in self._deps(E, reads, writes):
            self.prog[E].append(("wait", s, v))
        if defer:
            assert E == "pe"
            me = (E, self.sem[E], self.cnt[E] + 1)
            self.prog[E].append(("opq", fn))
        else:
            self.cnt[E] += 1
            me = (E, self.sem[E], self.cnt[E])
            self.prog[E].append(("op", fn, self.sem[E]))
        for t in writes:
            t.w = me
            t.r = {}
        for t in reads:
            if t not in writes:
                t.r[E] = me

    def _chan_for(self, reads, writes, chan):
        if chan is not None:
            return chan
        for t in list(writes) + list(reads):
            if t.chan is not None:
                return t.chan
        t = (list(writes) + list(reads))[0]
        t.chan = self.new_chan(t.name or "x")
        return t.chan

    def _async(self, Q, item_fn, inc, reads, writes, chan, serialize=True):
        chan = self._chan_for(reads, writes, chan)
        key = ("c", id(chan))
        extra = [(key, chan.sem, chan.cnt)] if (chan.cnt > 0 and serialize) else []
        for s, v in self._deps(Q, reads, writes, extra):
            self.prog[Q].append(("wait", s, v))
        chan.cnt += inc
        me = (key, chan.sem, chan.cnt)
        self.prog[Q].append(item_fn(chan.sem))
        for t in writes:
            t.w = me
            t.r = {}
        for t in reads:
            if t not in writes:
                t.r[key] = me

    def dma(self, Q, out, in_, reads=(), writes=(), chan=None):
        self._async(Q, lambda sem: ("dma", out, in_, sem), 16, reads, writes, chan)

    def cc(self, kind, groups, in_ap, out_ap, reads=(), writes=(), op=None, chan=None):
        if getattr(self, "no_cc", False):
            return
        if chan is None:
            chan = self.new_chan("cc")
        self._async("pool", lambda sem: ("cc", kind, groups, in_ap, out_ap, sem, op), 1, reads, writes, chan, serialize=False)

    def barrier(self):
        for E in self.ENG:
            for E2 in self.ENG:
                if E2 != E and self.cnt[E2] > self.seen[E].get(E2, 0):
                    self.prog[E].append(("wait", self.sem[E2], self.cnt[E2]))
                    self.seen[E][E2] = self.cnt[E2]
            for ch in self.chans:
                k = ("c", id(ch))
                if ch.cnt > self.seen[E].get(k, 0):
                    self.prog[E].append(("wait", ch.sem, ch.cnt))
                    self.seen[E][k] = ch.cnt
        self.release_phase_chans()

    def finish(self):
        for ch in self.chans:
            k = ("c", id(ch))
            if ch.cnt > self.seen["sp"].get(k, 0):
                self.prog["sp"].append(("wait", ch.sem, ch.cnt))

    def emit(self):
        nc = self.nc
        with nc.Block() as block:
            def make(E):
                def body(eng):
                    for item in self.prog[E]:
                        kind = item[0]
                        if kind == "wait":
                            eng.wait_ge(item[1], item[2])
                        elif kind == "op":
                            item[1](eng).then_inc(item[2], 1)
                        elif kind == "opq":
                            item[1](eng)
                        elif kind == "cc":
                            _, k, groups, in_ap, out_ap, sem, ccop = item
                            eng.collective_compute(k, ccop if ccop is not None else ALU.bypass,
                                                   replica_groups=groups, ins=[in_ap], outs=[out_ap]).then_inc(sem, 1)
                        else:
                            _, out, in_, sem = item
                            eng.dma_start(out=out, in_=in_).then_inc(sem, 16)
                return body
            block.tensor(make("pe"))
            block.scalar(make("act"))
            block.vector(make("dve"))
            block.gpsimd(make("pool"))
            block.sync(make("sp"))


def dap(t, offset, dims):
    h = t.tensor if hasattr(t, "tensor") else t
    return bass.AP(tensor=h, offset=offset, ap=[[int(s), int(n)] for s, n in dims])


class Builder:
    def __init__(self, n_layers=NL, debug=None):
        self.n_layers = n_layers
        self.debug = debug or {}
        self.nc = bass.Bass("TRN2", target_bir_lowering=False)
        self.dbg_outs = []

    def din(self, name, shape, dt=F32):
        return self.nc.dram_tensor(name, list(shape), dt, kind="ExternalInput").ap()

    def dout(self, name, shape, dt=F32):
        return self.nc.dram_tensor(name, list(shape), dt, kind="ExternalOutput").ap()

    def dscr(self, name, shape, dt):
        return self.nc.dram_tensor(name, list(shape), dt).ap()

    def declare(self):
        s = self
        NLW = s.n_layers
        if s.debug.get("mixtest"):
            s.PT_in = s.din("PT_in", [T, PW])
            s.PF_in = s.din("PF_in", [384, T])
        else:
            s.x_own = s.din("x_own", [TS, D])
            s.cT = s.din("cT", [128, 32])
            s.w_ada_t = s.din("w_ada_t", [NLW * 24 * 128, 2048])
            s.b_ada_s = s.din("b_ada_s", [128, 96])
            s.w_in_sel = s.din("w_in_sel", [NLW * 128, 13 * 2048])
            s.w_o_s = s.din("w_o_s", [NLW * 128, 8192])
            s.w_f1_s = s.din("w_f1_s", [NLW * 128, 45056])
            s.w_f2_s = s.din("w_f2_s", [NLW * 128, 22528])
        s.selb = s.din("selb", [128, 2])
        s.sel4 = s.din("sel4", [128, 4])
        s.lnp = s.din("lnp", [128, 256])
        s.hp = s.din("hp", [128, 24])
        s.convw = s.din("convw", [128, 60])
        s.nw = s.din("nw", [128, NL * 3 * 128])
        s.out = s.dout("out", [L // 4, D])
        s.Wsel = [s.dscr(f"Wsel{l}", [128, 13 * 2048], BF16) for l in range(NL)]
        s.wpo = [s.dscr(f"wpo{l}", [128, 8192], BF16) for l in range(NL)]
        s.wpf1 = [s.dscr(f"wpf1{l}", [128, 45056], BF16) for l in range(NL)]
        s.wpf2 = [s.dscr(f"wpf2{l}", [128, 22528], BF16) for l in range(NL)]
        s.Wo = [s.dscr(f"Wo{l}", [2048, 2048], BF16) for l in range(NL)]
        s.Wf1 = [s.dscr(f"Wf1{l}", [DFF, 4096], BF16) for l in range(NL)]
        s.Wf2 = [s.dscr(f"Wf2{l}", [2048, DFF], BF16) for l in range(NL)]
        s.mpart = s.dscr("mpart", [128, 192], F32)
        s.MG = s.dscr("MG", [512, 192], F32)
        s.COS = s.dscr("COS", [L, 64], F32)
        s.SIN = s.dscr("SIN", [L, 64], F32)
        s.XT = s.dscr("XT", [D, TS], F32)
        s.hpart = s.dscr("hpart", [D, TS], BF16)
        s.HG = s.dscr("HG", [4 * D, TS], BF16)
        s.PT = s.dscr("PT", [T, PW], F32)
        s.PF = s.dscr("PF", [3 * 128, T], F32)
        s.ypad = s.dscr("ypad", [16 * 512, TS], BF16)
        s.yown = s.dscr("yown", [D, TS], BF16)
        s.t_XT = Trk("XT"); s.t_hpart = Trk("hpart"); s.t_HG = Trk("HG")
        s.t_PT = Trk("PT"); s.t_PF = Trk("PF"); s.t_ypad = Trk("ypad"); s.t_yown = Trk("yown")
        s.t_COS = Trk("COS")
        s.t_W = [Trk(f"W{l}") for l in range(NL)]
        s.t_Wsel = [Trk(f"Wsel{l}") for l in range(NL)]

    def add_dbg(self, name, src_ap, shape, reads, dt=F32):
        if self.debug.get("no_dbg"):
            return
        o = self.dout(name, shape, dt)
        self.dbg_outs.append(name)
        self.c.dma("sp", o, src_ap, reads=reads, writes=[Trk(name)])

    def ps(self):
        i = self.ps_i
        self.ps_i = (i + 1) % 6
        return self.psb[i]

    def pst(self):
        i = self.psT_i
        self.psT_i = (i + 1) % 2
        return self.psT[i]

    def build(self):
        s = self
        nc = s.nc
        s.declare()
        with contextlib.ExitStack() as st:
            c = s.c = Ctx(nc, st)
            c.no_cc = bool(s.debug.get("no_cc"))
            s.psb = [c.psum(f"psb{i}", [128, 512]) for i in range(6)]
            s.ps_i = 0
            s.psT = [c.psum(f"psT{i}", [128, 1024], BF16) for i in range(2)]
            s.psT_i = 0
            with contextlib.ExitStack() as st_const:
                c.stack = st_const
                s.consts()
                if s.debug.get("mixtest"):
                    with contextlib.ExitStack() as stp:
                        c.stack = stp
                        s.setup_rope()
                        tcp = [Trk("cp0"), Trk("cp1"), Trk("cp2"), Trk("cp3")]
                        for n in range(66):
                            c.dma("sp", s.PT[n * 128:(n + 1) * 128, :], s.PT_in[n * 128:(n + 1) * 128, :], writes=[tcp[n % 4]])
                        for q in range(3):
                            for k in range(4):
                                c.dma("sp", s.PF[q * 128:(q + 1) * 128, k * 2112:(k + 1) * 2112], s.PF_in[q * 128:(q + 1) * 128, k * 2112:(k + 1) * 2112], writes=[tcp[k]])
                        c.barrier()
                else:
                    s.setup_phase()
                for l in range(s.n_layers):
                    s.layer(l)
                    if s.debug.get("stop_after"):
                        break
                if not s.debug.get("stop_after"):
                    s.final_phase()
                c.barrier()
            c.finish()
            c.emit()
        return nc

    def consts(self):
        s, c = self, self.c
        s.ones_f, s.t_ones_f = c.sbuf("ones_f", [128, 128], F32)
        c.op("pool", lambda e: e.memset(s.ones_f[:], 1.0), writes=[s.t_ones_f])
        s.ident_f, s.t_ident_f = c.sbuf("ident_f", [128, 128], F32)
        c.op("pool", lambda e: e.affine_select(out=s.ident_f[:], in_=s.ones_f[:], pattern=[[-1, 128]],
                                               compare_op=ALU.is_equal, fill=0.0, base=0, channel_multiplier=1),
             reads=[s.t_ones_f], writes=[s.t_ident_f])
        s.ident_b, s.t_ident_b = c.sbuf("ident_b", [128, 128], BF16)
        c.op("dve", lambda e: e.tensor_copy(out=s.ident_b[:], in_=s.ident_f[:]), reads=[s.t_ident_f], writes=[s.t_ident_b])
        s.ones_b, s.t_ones_b = c.sbuf("ones_b", [128, 128], BF16)
        c.op("dve", lambda e: e.tensor_copy(out=s.ones_b[:], in_=s.ones_f[:]), reads=[s.t_ones_f], writes=[s.t_ones_b])
        s.lnp_s, s.t_lnp = c.sbuf("lnp", [128, 256], F32)
        c.dma("sp", s.lnp_s[:], s.lnp, writes=[s.t_lnp])
        s.hp_s, s.t_hp = c.sbuf("hp", [128, 24], F32)
        c.dma("sp", s.hp_s[:], s.hp, writes=[s.t_hp])
        s.convw_s, s.t_convw = c.sbuf("convw", [128, 60], F32)
        c.dma("sp", s.convw_s[:], s.convw, writes=[s.t_convw])
        s.nw_s, s.t_nw = c.sbuf("nw", [128, NL * 3 * 128], F32)
        c.dma("sp", s.nw_s[:], s.nw, writes=[s.t_nw])
        s.selb_s, s.t_selb = c.sbuf("selb", [128, 2], F32)
        c.dma("sp", s.selb_s[:], s.selb, writes=[s.t_selb])
        s.sel4_s, s.t_sel4 = c.sbuf("sel4", [128, 4], F32)
        c.dma("sp", s.sel4_s[:], s.sel4, writes=[s.t_sel4])
        s.eps_t, s.t_eps = c.sbuf("eps_t", [128, 1], F32)
        c.op("pool", lambda e: e.memset(s.eps_t[:], EPS), writes=[s.t_eps])
        s.one_t, _ = c.sbuf("one_t", [128, 1], F32)
        c.op("pool", lambda e: e.memset(s.one_t[:], 1.0), writes=[s.t_eps])
        s.MM, s.t_MM = c.sbuf("MM", [128, NL, 96, 2], F32)

    def setup_phase(self):
        s, c = self, self.c
        with contextlib.ExitStack() as stp:
            c.stack = stp
            s.setup_weights()
            s.setup_ada()
            s.setup_rope()
            s.setup_xt()
            c.barrier()

    def convert(self, src, dst, X, reads_dst_trk):
        s, c = self, self.c
        CH = 4096
        for i, c0 in enumerate(range(0, X, CH)):
            n = min(CH, X - c0)
            sl = s.cv_i % 2
            s.cv_i += 1
            (a, ta), (b, tb) = s.cv_a[sl], s.cv_b[sl]
            c.dma("sp", a[:, 0:n], src[:, c0:c0 + n], writes=[ta])
            eng = ("dve", "act", "pool")[s.cv_i % 3]
            if eng == "act":
                c.op("act", lambda e, a=a, b=b, n=n: e.activation(out=b[:, 0:n], in_=a[:, 0:n], func=AF.Copy), reads=[ta], writes=[tb])
            else:
                c.op(eng, lambda e, a=a, b=b, n=n: e.tensor_copy(out=b[:, 0:n], in_=a[:, 0:n]), reads=[ta], writes=[tb])
            c.dma("sp", dst[:, c0:c0 + n], b[:, 0:n], reads=[tb], writes=[reads_dst_trk])

    def setup_weights(self):
        s, c = self, self.c
        s.cv_a = [c.sbuf(f"cva{i}", [128, 4096], F32) for i in range(2)]
        s.cv_b = [c.sbuf(f"cvb{i}", [128, 4096], BF16) for i in range(2)]
        s.cv_i = 0
        s.ch_w = c.new_chan("ccw")
        for l in range(s.n_layers):
            tp = [Trk(f"wpo{l}"), Trk(f"wpf1{l}"), Trk(f"wpf2{l}")]
            s.convert(s.w_in_sel[l * 128:(l + 1) * 128, :], s.Wsel[l], 13 * 2048, s.t_Wsel[l])
            s.convert(s.w_o_s[l * 128:(l + 1) * 128, :], s.wpo[l], 8192, tp[0])
            s.convert(s.w_f1_s[l * 128:(l + 1) * 128, :], s.wpf1[l], 45056, tp[1])
            s.convert(s.w_f2_s[l * 128:(l + 1) * 128, :], s.wpf2[l], 22528, tp[2])
            for (part, full, C, rb, nch, tpi) in ((s.wpo[l], s.Wo[l], 2048, 128, 4, tp[0]), (s.wpf1[l], s.Wf1[l], 4096, 128, 11, tp[1]),
                                                  (s.wpf2[l], s.Wf2[l], DFF, 64, 8, tp[2])):
                for ch in range(nch):
                    c.cc("AllGather", GROUPS4, dap(part, ch * rb * C, [(C, rb), (1, C)]), full[ch * 4 * rb:(ch + 1) * 4 * rb, :],
                         reads=[tpi], writes=[s.t_W[l]], chan=s.ch_w)

    def setup_ada(self):
        s, c = self, self.c
        cond, t_cond = c.sbuf("cond", [128, 32], F32)
        c.dma("sp", cond[:], s.cT, writes=[t_cond])
        c.op("act", lambda e: e.activation(out=cond[:], in_=cond[:], func=AF.Silu), reads=[t_cond], writes=[t_cond])
        bada, t_bada = c.sbuf("bada", [128, 96], F32)
        c.dma("sp", bada[:], s.b_ada_s, writes=[t_bada])
        mp, t_mp = c.sbuf("mp", [128, 192], F32)
        c.op("pool", lambda e: e.memset(mp[:], 0.0), writes=[t_mp])
        wt = [c.sbuf(f"adaw{i}", [128, 2048], F32) for i in range(2)]
        for lt in range(s.n_layers * 24):
            w, tw = wt[lt % 2]
            c.dma("sp", w[:], s.w_ada_t[lt * 128:(lt + 1) * 128, :], writes=[tw])
            ps, tps = s.ps()
            for kc in range(KC):
                c.op("pe", lambda e, w=w, ps=ps, kc=kc: e.matmul(ps[:, 0:2], lhsT=w[:, kc * 128:(kc + 1) * 128],
                                                                rhs=cond[:, kc * 2:(kc + 1) * 2], start=(kc == 0), stop=(kc == KC - 1)),
                     reads=[tw, t_cond], writes=[tps])
            c.op("dve", lambda e, ps=ps, lt=lt: e.tensor_scalar(out=mp[:, lt * 2:(lt + 1) * 2], in0=ps[:, 0:2],
                                                               scalar1=bada[:, lt:lt + 1], scalar2=None, op0=ALU.add),
                 reads=[tps, t_bada], writes=[t_mp])
        t_mpart = Trk("mpart"); t_MG = Trk("MG")
        c.dma("sp", s.mpart, mp[:], reads=[t_mp], writes=[t_mpart])
        c.cc("AllGather", GROUPS4, s.mpart, s.MG, reads=[t_mpart], writes=[t_MG])
        Mg, t_Mg = c.sbuf("Mg", [128, 4, 192], F32)
        c.dma("sp", Mg[:], s.MG.rearrange("(r p) x -> p r x", p=128), reads=[t_MG], writes=[t_Mg])
        Mv = Mg[:].rearrange("p r (l t i) -> p r l t i", l=NL, t=24, i=2)
        for l in range(s.n_layers):
            dst = s.MM[:, l, :, :].rearrange("p (r t) w -> p r t w", r=4, t=24)
            for w_ in range(2):
                c.op("dve", lambda e, l=l, dst=dst, w_=w_: e.tensor_copy(out=dst[:, :, :, w_], in_=Mv[:, :, l, :, w_]), reads=[t_Mg], writes=[s.t_MM])
            for comp in (1, 4):
                v = s.MM[:, l, comp * 16:(comp + 1) * 16, :]
                c.op("dve", lambda e, v=v: e.tensor_scalar(out=v, in0=v, scalar1=1.0, scalar2=None, op0=ALU.add), reads=[], writes=[s.t_MM])
            for comp in (2, 5):
                v = s.MM[:, l, comp * 16:(comp + 1) * 16, :]
                c.op("dve", lambda e, v=v: e.tensor_scalar(out=v, in0=v, scalar1=1.0 / ALPHA, scalar2=None, op0=ALU.mult), reads=[], writes=[s.t_MM])

    def setup_rope(self):
        s, c = self, self.c
        io, t_io = c.sbuf("io", [128, 64], I32)
        c.op("pool", lambda e: e.iota(io[:], pattern=[[128, 64]], base=0, channel_multiplier=1), writes=[t_io])
        ri, t_ri = c.sbuf("ri", [128, 64], I32)
        ci, t_ci = c.sbuf("ci", [128, 64], I32)
        c.op("dve", lambda e: e.tensor_single_scalar(out=ri[:], in_=io[:], scalar=6, op=ALU.arith_shift_right), reads=[t_io], writes=[t_ri])
        c.op("dve", lambda e: e.tensor_single_scalar(out=ci[:], in_=io[:], scalar=63, op=ALU.bitwise_and), reads=[t_io], writes=[t_ci])
        rf, t_rf = c.sbuf("rf", [128, 64], F32)
        cf, t_cf = c.sbuf("cf", [128, 64], F32)
        c.op("dve", lambda e: e.tensor_copy(out=rf[:], in_=ri[:]), reads=[t_ri], writes=[t_rf])
        c.op("dve", lambda e: e.tensor_copy(out=cf[:], in_=ci[:]), reads=[t_ci], writes=[t_cf])
        ii, t_ii = c.sbuf("ii", [128, 32], I32)
        c.op("pool", lambda e: e.iota(ii[:], pattern=[[1, 32]], base=0, channel_multiplier=0), writes=[t_ii])
        inv, t_inv = c.sbuf("inv", [128, 32], F32)
        c.op("dve", lambda e: e.tensor_copy(out=inv[:], in_=ii[:]), reads=[t_ii], writes=[t_inv])
        c.op("act", lambda e: e.activation(out=inv[:], in_=inv[:], func=AF.Exp, scale=-math.log(10000.0) / 32.0), reads=[t_inv], writes=[t_inv])
        ang, t_ang = c.sbuf("ang", [128, 64, 64], F32)
        invb = inv[:].unsqueeze(1).broadcast_to([128, 64, 32])
        c.op("dve", lambda e: e.tensor_tensor(out=ang[:, :, 0:32], in0=rf[:].unsqueeze(2).broadcast_to([128, 64, 32]), in1=invb, op=ALU.mult),
             reads=[t_rf, t_inv], writes=[t_ang])
        c.op("dve", lambda e: e.tensor_tensor(out=ang[:, :, 32:64], in0=cf[:].unsqueeze(2).broadcast_to([128, 64, 32]), in1=invb, op=ALU.mult),
             reads=[t_cf, t_inv], writes=[t_ang])
        ki, t_ki = c.sbuf("ki", [128, 64, 64], I32)
        kf, t_kf = c.sbuf("kf", [128, 64, 64], F32)
        sn, t_sn = c.sbuf("sn", [128, 64, 64], F32)
        cs, t_cs = c.sbuf("cs", [128, 64, 64], F32)
        for (dst, t_dst, shift) in ((sn, t_sn, 0.0), (cs, t_cs, math.pi / 2)):
            c.op("dve", lambda e, dst=dst, shift=shift: e.tensor_scalar(out=dst[:], in0=ang[:], scalar1=shift, scalar2=None, op0=ALU.add),
                 reads=[t_ang], writes=[t_dst])
            c.op("dve", lambda e, dst=dst: e.tensor_scalar(out=ki[:], in0=dst[:], scalar1=1.0 / (2 * math.pi), scalar2=None, op0=ALU.mult),
                 reads=[t_dst], writes=[t_ki])
            c.op("dve", lambda e: e.tensor_copy(out=kf[:], in_=ki[:]), reads=[t_ki], writes=[t_kf])
            c.op("dve", lambda e, dst=dst: e.scalar_tensor_tensor(out=dst[:], in0=kf[:], scalar=-2 * math.pi, in1=dst[:], op0=ALU.mult, op1=ALU.add),
                 reads=[t_kf], writes=[t_dst])
            c.op("dve", lambda e, dst=dst: e.tensor_scalar(out=dst[:], in0=dst[:], scalar1=math.pi, scalar2=-math.pi, op0=ALU.min, op1=ALU.max),
                 reads=[], writes=[t_dst])
            c.op("act", lambda e, dst=dst: e.activation(out=dst[:], in_=dst[:], func=AF.Sin), reads=[], writes=[t_dst])
        c.dma("sp", s.SIN.rearrange("(n p) d -> p n d", p=128), sn[:], reads=[t_sn], writes=[s.t_COS])
        c.dma("sp", s.COS.rearrange("(n p) d -> p n d", p=128), cs[:], reads=[t_cs], writes=[s.t_COS])

    def setup_xt(self):
        s, c = self, self.c
        xin = [c.sbuf(f"xin{i}", [128, D], F32) for i in range(2)]
        xo = [c.sbuf(f"xo{i}", [128, KC, 128], F32) for i in range(2)]
        tiles = [(0, 64)] + [(64 + 128 * i, 128) for i in range(16)]
        for ti, (r0, m) in enumerate(tiles):
            a, ta = xin[ti % 2]
            o, to = xo[ti % 2]
            c.dma("sp", a[0:m, :], s.x_own[r0:r0 + m, :], writes=[ta])
            for g in range(4):
                ps, tps = s.ps()
                for q in range(4):
                    kc = g * 4 + q
                    c.op("pe", lambda e, a=a, ps=ps, kc=kc, q=q, m=m: e.transpose(out=ps[:, q * 128:q * 128 + m], in_=a[0:m, kc * 128:(kc + 1) * 128],
                                                                                  identity=s.ident_f[0:m, 0:m]),
                         reads=[ta, s.t_ident_f], writes=[tps])
                eng = "act" if g % 2 else "dve"
                src = ps[:, :].rearrange("p (q t) -> p q t", q=4)[:, :, 0:m]
                if eng == "act":
                    c.op("act", lambda e, o=o, src=src, g=g, m=m: e.activation(out=o[:, g * 4:(g + 1) * 4, 0:m], in_=src, func=AF.Copy), reads=[tps], writes=[to])
                else:
                    c.op("dve", lambda e, o=o, src=src, g=g, m=m: e.tensor_copy(out=o[:, g * 4:(g + 1) * 4, 0:m], in_=src), reads=[tps], writes=[to])
            c.dma("sp", s.XT.rearrange("(kc p) t -> p kc t", p=128)[:, :, r0:r0 + m], o[:, :, 0:m], reads=[to], writes=[s.t_XT])

    def layer(self, l):
        s, c = self, self.c
        if not s.debug.get("mixtest"):
            with contextlib.ExitStack() as stp:
                c.stack = stp
                s.wsel, s.t_wsel = c.sbuf("wsel", [128, 13, 2048], BF16)
                c.dma("sp", s.wsel[:], s.Wsel[l].rearrange("p (t c) -> p t c", t=13), reads=[s.t_Wsel[l]], writes=[s.t_wsel])
                s.phase_a0(l)
                s.phase_a1(l)
                c.barrier()
        if self.debug.get("stop_after") == "a1":
            self.dump_a1(l)
            return
        for ph in (s.phase_att, s.phase_ret, s.phase_dn):
            if ph.__name__ in self.debug.get("skip", ()):
                continue
            with contextlib.ExitStack() as stp:
                c.stack = stp
                ph(l)
                c.barrier()
        if s.debug.get("zero_ypad"):
            with contextlib.ExitStack() as stp:
                c.stack = stp
                zt_, tzt_ = c.sbuf("zpad", [128, TS], BF16)
                c.op("dve", lambda e: e.memset(zt_[:], 0.0), writes=[tzt_])
                for i in range(64):
                    c.dma("sp", s.ypad[i * 128:(i + 1) * 128, :], zt_[:], reads=[tzt_], writes=[s.t_ypad])
                c.barrier()
        if not s.debug.get("no_rs"):
            c.cc("ReduceScatter", GROUPS4, s.ypad, s.yown, reads=[s.t_ypad], writes=[s.t_yown], op=ALU.add)
        if self.debug.get("stop_after") == "mix":
            s.add_dbg("d_yown", s.yown, [D, TS], [s.t_yown], dt=BF16)
            return
        with contextlib.ExitStack() as stp:
            c.stack = stp
            s.phase_c(l)
            c.barrier()
        if self.debug.get("stop_after") == "c":
            s.add_dbg("d_yown", s.yown, [D, TS], [s.t_yown], dt=BF16)
            s.add_dbg("d_XT", s.XT, [D, TS], [s.t_XT])
            return

    def own_tiles(self):
        return [(0, 64, 1)] + [(64 + 512 * i, 512, 0) for i in range(4)]

    def phase_a0(self, l):
        s, c = self, self.c
        xt = [c.sbuf(f"a0x{i}", [128, TS], F32) for i in range(2)]
        hb = [c.sbuf(f"a0h{i}", [128, TS], BF16) for i in range(2)]
        ch_h = c.new_chan("cch")
        t_hp = [Trk(f"hpart{kc}") for kc in range(KC)]
        for kc in range(KC):
            a, ta = xt[kc % 2]
            h, th = hb[kc % 2]
            c.dma("sp", a[:], s.XT[kc * 128:(kc + 1) * 128, :], reads=[s.t_XT], writes=[ta])
            c.op("act", lambda e, a=a, h=h, kc=kc: e.activation(out=h[:, 0:64], in_=a[:, 0:64], func=AF.Identity,
                                                                scale=s.MM[:, l, 16 + kc, 1:2], bias=s.MM[:, l, kc, 1:2]),
                 reads=[ta, s.t_MM], writes=[th])
            c.op("dve", lambda e, a=a, h=h, kc=kc: e.tensor_scalar(out=h[:, 64:TS], in0=a[:, 64:TS], scalar1=s.MM[:, l, 16 + kc, 0:1],
                                                                   scalar2=s.MM[:, l, kc, 0:1], op0=ALU.mult, op1=ALU.add),
                 reads=[ta, s.t_MM], writes=[th])
            c.dma("sp", s.hpart[kc * 128:(kc + 1) * 128, :], h[:], reads=[th], writes=[t_hp[kc]])
            c.cc("AllGather", GROUPS4, s.hpart[kc * 128:(kc + 1) * 128, :], s.HG[kc * 512:(kc + 1) * 512, :],
                 reads=[t_hp[kc]], writes=[s.t_HG], chan=ch_h)

    def phase_a1(self, l):
        s, c = self, self.c
        wsel, t_wsel = s.wsel, s.t_wsel
        ht = [c.sbuf(f"a1h{i}", [128, KC, 512], BF16) for i in range(2)]
        stg = [c.sbuf(f"a1s{i}", [128, PW], F32) for i in range(2)]
        stf = [c.sbuf(f"a1f{i}", [128, 512], F32) for i in range(2)]
        HGv = s.HG.rearrange("(kc a p) t -> a p kc t", a=4, kc=KC, p=128)
        PFv = s.PF.rearrange("(s p) t -> s p t", p=128)
        it = 0
        si = 0
        fi = 0
        for js in range(4):
            for (t0, n, w) in s.own_tiles():
                tok0 = (64 * js) if w == 1 else (LC + 2048 * js + (t0 - 64))
                h, th = ht[it % 2]
                it += 1
                c.dma("sp", h[:, :, 0:n], HGv[js][:, :, t0:t0 + n], reads=[s.t_HG], writes=[th])
                for sub in range((n + 127) // 128):
                    m = min(128, n - sub * 128)
                    sg, tsg = stg[si % 2]
                    si += 1
                    for gi, (tl0, ntl, c0, ncols) in enumerate([(0, 4, 0, 512), (4, 4, 512, 512), (8, 2, 1024, 144)]):
                        ps, tps = s.ps()
                        for kc in range(KC):
                            c.op("pe", lambda e, ps=ps, h=h, kc=kc, sub=sub, m=m, tl0=tl0, ntl=ntl: e.matmul(
                                ps[0:m, 0:ntl * 128], lhsT=h[:, kc, sub * 128:sub * 128 + m],
                                rhs=wsel[:, tl0:tl0 + ntl, kc * 128:(kc + 1) * 128], start=(kc == 0), stop=(kc == KC - 1)),
                                 reads=[th, t_wsel], writes=[tps], defer=(kc != KC - 1))
                        if gi == 1:
                            c.op("act", lambda e, ps=ps, sg=sg, m=m, c0=c0, ncols=ncols: e.activation(out=sg[0:m, c0:c0 + ncols], in_=ps[0:m, 0:ncols], func=AF.Copy),
                                 reads=[tps], writes=[tsg])
                        else:
                            c.op("dve", lambda e, ps=ps, sg=sg, m=m, c0=c0, ncols=ncols: e.tensor_copy(out=sg[0:m, c0:c0 + ncols], in_=ps[0:m, 0:ncols]),
                                 reads=[tps], writes=[tsg])
                    r0 = tok0 + sub * 128
                    c.dma("sp", s.PT[r0:r0 + m, :], sg[0:m, :], reads=[tsg], writes=[s.t_PT])
                for q in range(3):
                    ps, tps = s.ps()
                    for kc in range(KC):
                        c.op("pe", lambda e, ps=ps, h=h, kc=kc, q=q, n=n: e.matmul(
                            ps[:, 0:n], lhsT=wsel[:, 10 + q, kc * 128:(kc + 1) * 128], rhs=h[:, kc, 0:n],
                            start=(kc == 0), stop=(kc == KC - 1)), reads=[th, t_wsel], writes=[tps], defer=(kc != KC - 1))
                    sf, tsf = stf[fi % 2]
                    fi += 1
                    c.op("act", lambda e, ps=ps, sf=sf, n=n: e.activation(out=sf[:, 0:n], in_=ps[:, 0:n], func=AF.Copy), reads=[tps], writes=[tsf])
                    c.dma("sp", PFv[q][:, tok0:tok0 + n], sf[:, 0:n], reads=[tsf], writes=[s.t_PF])


    def scatter_y(self, ysb, t_ysb, sidx, q0, nq):
        s, c = self, self.c
        ysc, t_ysc = s.ysc[s.ysc_i % 2]
        s.ysc_i += 1
        for blk in range(4):
            eng = "dve" if blk % 2 == 0 else "pool"
            c.op(eng, lambda e, blk=blk: e.tensor_scalar(out=ysc[:, blk, 0:nq], in0=ysb, scalar1=s.sel4_s[:, blk:blk + 1], scalar2=None, op0=ALU.mult),
                 reads=[t_ysb, s.t_sel4], writes=[t_ysc])
        YP = s.ypad.rearrange("(jt blk s d) t -> jt s d blk t", jt=4, blk=4, s=4, d=128)
        pos = q0
        while pos < q0 + nq:
            if pos < LC:
                jt, col, room = pos // 64, pos % 64, 64 - pos % 64
            else:
                tl = pos - LC
                jt, col, room = tl // 2048, 64 + tl % 2048, 2048 - tl % 2048
            n = min(room, q0 + nq - pos)
            o = pos - q0
            c.dma("sp", YP[jt][sidx][:, :, col:col + n], ysc[:, :, o:o + n], reads=[t_ysc], writes=[s.t_ypad])
            pos += n

    def alloc_scatter(self):
        s, c = self, self.c
        s.ysc = [c.sbuf(f"ysc{i}", [128, 4, 512], BF16) for i in range(2)]
        s.ysc_i = 0
        s.sel4_b, s.t_sel4b = c.sbuf("sel4b", [128, 4], BF16)
        c.op("dve", lambda e: e.tensor_copy(out=s.sel4_b[:], in_=s.sel4_s[:]), reads=[s.t_sel4], writes=[s.t_sel4b])

    def rope_ops(self, xn, t_xn, xr, t_xr, cs, sn, t_cs, nh, tmp, t_tmp):
        c = self.c
        cb = cs.unsqueeze(1).broadcast_to([128, nh, 64])
        sb = sn.unsqueeze(1).broadcast_to([128, nh, 64])
        x1 = xn[:, 0:nh, 0:64]
        x2 = xn[:, 0:nh, 64:128]
        t1, t2 = tmp[:, 0, 0:nh, :], tmp[:, 1, 0:nh, :]
        c.op("dve", lambda e: e.tensor_tensor(out=t1, in0=x1, in1=cb, op=ALU.mult), reads=[t_xn, t_cs], writes=[t_tmp])
        c.op("dve", lambda e: e.tensor_tensor(out=t2, in0=x2, in1=sb, op=ALU.mult), reads=[t_xn, t_cs], writes=[t_tmp])
        c.op("dve", lambda e: e.tensor_tensor(out=xr[:, 0:nh, 0:64], in0=t1, in1=t2, op=ALU.subtract), reads=[t_tmp], writes=[t_xr])
        c.op("dve", lambda e: e.tensor_tensor(out=t1, in0=x1, in1=sb, op=ALU.mult), reads=[t_xn, t_cs], writes=[t_tmp])
        c.op("dve", lambda e: e.tensor_tensor(out=t2, in0=x2, in1=cb, op=ALU.mult), reads=[t_xn, t_cs], writes=[t_tmp])
        c.op("dve", lambda e: e.tensor_tensor(out=xr[:, 0:nh, 64:128], in0=t1, in1=t2, op=ALU.add), reads=[t_tmp], writes=[t_xr])

    def phase_att(self, l):
        s, c = self, self.c
        s.alloc_scatter()
        QT = [c.sbuf(f"QT{i}", [128, T], BF16) for i in range(2)]
        KT, t_KT = c.sbuf("KT", [128, T], BF16)
        V, t_V = c.sbuf("V", [128, 66, 128], BF16)
        A = [c.sbuf(f"attA{i}", [128, 512], F32) for i in range(2)]
        CS = [c.sbuf(f"attCS{i}", [128, 2, 64], F32) for i in range(2)]
        ss, t_ss = c.sbuf("att_ss", [128, 4], F32)
        junk, t_junk = c.sbuf("att_junk", [128, 128], F32)
        xn, t_xn = c.sbuf("att_xn", [128, 3, 128], F32)
        xr, t_xr = c.sbuf("att_xr", [128, 3, 128], BF16)
        tmp, t_tmp = c.sbuf("att_tmp", [128, 2, 3, 64], F32)
        nwq = s.nw_s[:, (l * 3 + 1) * 128:(l * 3 + 2) * 128]
        nwk = s.nw_s[:, (l * 3 + 2) * 128:(l * 3 + 3) * 128]
        for n in range(66):
            a, ta = A[n % 2]
            c.dma("sp", a[:], s.PT[n * 128:(n + 1) * 128, 512:1024], reads=[s.t_PT], writes=[ta])
            if n >= 2:
                cs, tcs = CS[n % 2]
                c.dma("sp", cs[:, 0, :], s.COS[(n - 2) * 128:(n - 1) * 128, :], reads=[s.t_COS], writes=[tcs])
                c.dma("sp", cs[:, 1, :], s.SIN[(n - 2) * 128:(n - 1) * 128, :], reads=[s.t_COS], writes=[tcs])
            sub = s.debug.get("sub", 9)
            if sub < 1:
                continue
            for i in range(3):
                c.op("act", lambda e, a=a, i=i: e.activation(out=junk[:], in_=a[:, i * 128:(i + 1) * 128], func=AF.Square, accum_out=ss[:, i:i + 1]),
                     reads=[ta], writes=[t_junk, t_ss])
            c.op("dve", lambda e: e.tensor_scalar(out=ss[:, 0:3], in0=ss[:, 0:3], scalar1=1.0 / 128, scalar2=EPS, op0=ALU.mult, op1=ALU.add), reads=[], writes=[t_ss])
            c.op("act", lambda e: e.activation(out=ss[:, 0:3], in_=ss[:, 0:3], func=AF.Sqrt), reads=[], writes=[t_ss])
            c.op("dve", lambda e: e.reciprocal(out=ss[:, 0:3], in_=ss[:, 0:3]), reads=[], writes=[t_ss])
            if sub < 2:
                continue
            for i in range(3):
                wv = nwq if i < 2 else nwk
                c.op("dve", lambda e, a=a, i=i, wv=wv: e.scalar_tensor_tensor(out=xn[:, i, :], in0=a[:, i * 128:(i + 1) * 128], scalar=ss[:, i:i + 1],
                                                                              in1=wv, op0=ALU.mult, op1=ALU.mult), reads=[ta, t_ss, s.t_nw], writes=[t_xn])
            if sub < 3:
                continue
            if n >= 2:
                s.rope_ops(xn, t_xn, xr, t_xr, cs[:, 0, :], cs[:, 1, :], tcs, 3, tmp, t_tmp)
            else:
                c.op("dve", lambda e: e.tensor_copy(out=xr[:], in_=xn[:]), reads=[t_xn], writes=[t_xr])
            if sub < 4:
                continue
            pT, tpT = s.pst()
            for i in range(3):
                c.op("pe", lambda e, pT=pT, i=i: e.transpose(out=pT[:, i * 128:(i + 1) * 128], in_=xr[:, i, :], identity=s.ident_b[:]),
                     reads=[t_xr, s.t_ident_b], writes=[tpT])
            for i, (dst, tdst) in enumerate([QT[0], QT[1], (KT, t_KT)]):
                eng = "dve"
                if eng == "act":
                    c.op("act", lambda e, pT=pT, i=i, dst=dst, n=n: e.activation(out=dst[:, n * 128:(n + 1) * 128], in_=pT[:, i * 128:(i + 1) * 128], func=AF.Copy),
                         reads=[tpT], writes=[tdst])
                else:
                    c.op("dve", lambda e, pT=pT, i=i, dst=dst, n=n: e.tensor_copy(out=dst[:, n * 128:(n + 1) * 128], in_=pT[:, i * 128:(i + 1) * 128]),
                         reads=[tpT], writes=[tdst])
            c.op("act", lambda e, a=a, n=n: e.activation(out=V[:, n, :], in_=a[:, 384:512], func=AF.Copy), reads=[ta], writes=[t_V])
        if s.debug.get("att_stage") == 1:
            return
        Pb = [c.sbuf(f"attP{i}", [128, 512], BF16) for i in range(3)]
        rden, t_rden = c.sbuf("att_rden", [128, 512], F32)
        ysb = [c.sbuf(f"att_y{i}", [128, 512], BF16) for i in range(2)]
        sc = 128.0 ** -0.5
        pi = 0
        gi = 0
        groups = [(0, 256, [0, 1])] + [(LC + 512 * g, 512, list(range(66))) for g in range(16)]
        for hh in range(2):
            q, tq = QT[hh]
            for (q0, nq, blocks) in groups:
                psO, tpsO = s.ps()
                psD, tpsD = s.ps()
                pend = []

                def emit_S(kb, psO=psO, psD=psD, q=q, tq=tq, q0=q0, nq=nq, pend=pend):
                    psS, tpsS = s.ps()
                    while psS is psO or psS is psD:
                        psS, tpsS = s.ps()
                    c.op("pe", lambda e, psS=psS, kb=kb: e.matmul(psS[:, 0:nq], lhsT=KT[:, kb * 128:(kb + 1) * 128], rhs=q[:, q0:q0 + nq],
                                                                 start=True, stop=True), reads=[t_KT, tq], writes=[tpsS])
                    pend.append((psS, tpsS))

                LOOK = 2
                for kb in blocks[:LOOK]:
                    emit_S(kb)
                for bi, kb in enumerate(blocks):
                    if bi + LOOK < len(blocks):
                        emit_S(blocks[bi + LOOK])
                    psS, tpsS = pend[bi]
                    pb, tpb = Pb[pi % 3]
                    pi += 1
                    c.op("act", lambda e, psS=psS, pb=pb, nq=nq: e.activation(out=pb[:, 0:nq], in_=psS[:, 0:nq], func=AF.Exp, scale=sc), reads=[tpsS], writes=[tpb])
                    first, last = bi == 0, bi == len(blocks) - 1
                    c.op("pe", lambda e, psO=psO, kb=kb, pb=pb, nq=nq, first=first, last=last: e.matmul(psO[:, 0:nq], lhsT=V[:, kb, :], rhs=pb[:, 0:nq], start=first, stop=last),
                         reads=[t_V, tpb], writes=[tpsO])
                    c.op("pe", lambda e, psD=psD, pb=pb, nq=nq, first=first, last=last: e.matmul(psD[:, 0:nq], lhsT=s.ones_b[:], rhs=pb[:, 0:nq], start=first, stop=last),
                         reads=[s.t_ones_b, tpb], writes=[tpsD])
                c.op("dve", lambda e, psD=psD, nq=nq: e.reciprocal(out=rden[:, 0:nq], in_=psD[:, 0:nq]), reads=[tpsD], writes=[t_rden])
                y, ty = ysb[gi % 2]
                gi += 1
                c.op("dve", lambda e, psO=psO, y=y, nq=nq: e.tensor_tensor(out=y[:, 0:nq], in0=psO[:, 0:nq], in1=rden[:, 0:nq], op=ALU.mult), reads=[tpsO, t_rden], writes=[ty])
                if s.debug.get("att_stage") != 2:
                    s.scatter_y(y[:, 0:nq], ty, 2 + hh, q0, nq)

    def phase_ret(self, l):
        s, c = self, self.c
        s.alloc_scatter()
        sc = 128.0 ** -0.5
        lg, t_lg = c.sbuf("r_lg", [128, 2], F32)
        c.op("act", lambda e: e.activation(out=lg[:], in_=s.hp_s[:, l * 2:l * 2 + 2], func=AF.Exp, scale=-1.0), reads=[s.t_hp], writes=[t_lg])
        c.op("act", lambda e: e.activation(out=lg[:], in_=lg[:], func=AF.Ln, bias=1.0), reads=[], writes=[t_lg])
        c.op("dve", lambda e: e.tensor_scalar(out=lg[:], in0=lg[:], scalar1=-1.0, scalar2=None, op0=ALU.mult), reads=[], writes=[t_lg])
        ii, t_ii = c.sbuf("r_ii", [128, 128], I32)
        fi, t_fi = c.sbuf("r_fi", [128, 128], F32)
        DT = [c.sbuf(f"r_DT{d}", [128, 128], F32) for d in range(2)]
        RQ = [c.sbuf(f"r_RQ{d}", [128, 128], F32) for d in range(2)]
        kd, t_kd = c.sbuf("r_kd", [128, 2], F32)
        g128, t_g128 = c.sbuf("r_g128", [128, 2], F32)

        def iota_f(pattern, base, cm, n):
            c.op("pool", lambda e: e.iota(ii[:, 0:n], pattern=pattern, base=base, channel_multiplier=cm), reads=[t_fi], writes=[t_ii])
            c.op("dve", lambda e: e.tensor_copy(out=fi[:, 0:n], in_=ii[:, 0:n]), reads=[t_ii], writes=[t_fi])

        for d in range(2):
            dt_, tdt = DT[d]
            if d == 0:
                iota_f([[1, 128]], 0, -1, 128)
            else:
                iota_f([[-1, 128]], 0, 1, 128)
            c.op("dve", lambda e: e.tensor_scalar(out=fi[:], in0=fi[:], scalar1=0.0, scalar2=None, op0=ALU.max), reads=[], writes=[t_fi])
            c.op("act", lambda e, d=d, dt_=dt_: e.activation(out=dt_[:], in_=fi[:], func=AF.Exp, scale=lg[:, d:d + 1]), reads=[t_fi, t_lg], writes=[tdt])
            c.op("dve", lambda e, dt_=dt_: e.tensor_scalar(out=dt_[:], in0=dt_[:], scalar1=sc, scalar2=None, op0=ALU.mult), reads=[], writes=[tdt])
            if d == 0:
                c.op("pool", lambda e, dt_=dt_: e.affine_select(out=dt_[:], in_=dt_[:], pattern=[[1, 128]], compare_op=ALU.is_ge, fill=0.0, base=0, channel_multiplier=-1),
                     reads=[], writes=[tdt])
            else:
                c.op("pool", lambda e, dt_=dt_: e.affine_select(out=dt_[:], in_=dt_[:], pattern=[[-1, 128]], compare_op=ALU.is_ge, fill=0.0, base=0, channel_multiplier=1),
                     reads=[], writes=[tdt])
            rq, trq = RQ[d]
            if d == 0:
                iota_f([[1, 128]], 1, 0, 128)
            else:
                iota_f([[-1, 128]], 128, 0, 128)
            c.op("act", lambda e, d=d, rq=rq: e.activation(out=rq[:], in_=fi[:], func=AF.Exp, scale=lg[:, d:d + 1]), reads=[t_fi, t_lg], writes=[trq])
            if d == 0:
                iota_f([[0, 1]], 127, -1, 1)
            else:
                iota_f([[0, 1]], 0, 1, 1)
            c.op("act", lambda e, d=d: e.activation(out=kd[:, d:d + 1], in_=fi[:, 0:1], func=AF.Exp, scale=lg[:, d:d + 1]), reads=[t_fi, t_lg], writes=[t_kd])
        c.op("dve", lambda e: e.tensor_scalar(out=kd[:], in0=kd[:], scalar1=sc, scalar2=None, op0=ALU.mult), reads=[], writes=[t_kd])
        c.op("act", lambda e: e.activation(out=g128[:], in_=lg[:], func=AF.Exp, scale=128.0), reads=[t_lg], writes=[t_g128])
        QTa, t_QTa = c.sbuf("r_QT", [128, 66, 128], BF16)
        KTa, t_KTa = c.sbuf("r_KT", [128, 66, 128], BF16)
        Ka, t_Ka = c.sbuf("r_K", [128, 66, 128], BF16)
        Va, t_Va = c.sbuf("r_V", [128, 66, 128], BF16)
        SG, t_SG = c.sbuf("r_SG", [128, 66, 128], F32)
        Of, t_Of = c.sbuf("r_Of", [128, 66, 128], F32)
        A = [c.sbuf(f"r_A{i}", [128, 512], F32) for i in range(2)]
        CS = [c.sbuf(f"r_CS{i}", [128, 2, 64], F32) for i in range(2)]
        xr, t_xr = c.sbuf("r_xr", [128, 2, 128], BF16)
        tmp, t_tmp = c.sbuf("r_tmp", [128, 2, 2, 64], F32)
        for n in range(66):
            a, ta = A[n % 2]
            c.dma("sp", a[:], s.PT[n * 128:(n + 1) * 128, 0:512], reads=[s.t_PT], writes=[ta])
            av = a[:, 0:256].rearrange("p (h d) -> p h d", h=2)
            if n >= 2:
                cs, tcs = CS[n % 2]
                c.dma("sp", cs[:, 0, :], s.COS[(n - 2) * 128:(n - 1) * 128, :], reads=[s.t_COS], writes=[tcs])
                c.dma("sp", cs[:, 1, :], s.SIN[(n - 2) * 128:(n - 1) * 128, :], reads=[s.t_COS], writes=[tcs])
                s.rope_ops(av, ta, xr, t_xr, cs[:, 0, :], cs[:, 1, :], tcs, 2, tmp, t_tmp)
            else:
                c.op("dve", lambda e, av=av: e.tensor_copy(out=xr[:], in_=av), reads=[ta], writes=[t_xr])
            pT, tpT = s.pst()
            for i in range(2):
                c.op("pe", lambda e, pT=pT, i=i: e.transpose(out=pT[:, i * 128:(i + 1) * 128], in_=xr[:, i, :], identity=s.ident_b[:]),
                     reads=[t_xr, s.t_ident_b], writes=[tpT])
            c.op("dve", lambda e, pT=pT, n=n: e.tensor_copy(out=QTa[:, n, :], in_=pT[:, 0:128]), reads=[tpT], writes=[t_QTa])
            c.op("dve", lambda e, pT=pT, n=n: e.tensor_copy(out=KTa[:, n, :], in_=pT[:, 128:256]), reads=[tpT], writes=[t_KTa])
            c.op("act", lambda e, n=n: e.activation(out=Ka[:, n, :], in_=xr[:, 1, :], func=AF.Copy), reads=[t_xr], writes=[t_Ka])
            c.op("act", lambda e, a=a, n=n: e.activation(out=Va[:, n, :], in_=a[:, 256:384], func=AF.Copy), reads=[ta], writes=[t_Va])
            c.op("act", lambda e, a=a, n=n: e.activation(out=SG[:, n, :], in_=a[:, 384:512], func=AF.Silu), reads=[ta], writes=[t_SG])
        S, t_S = c.sbuf("r_S", [128, 128], F32)
        Sb, t_Sb = c.sbuf("r_Sb", [128, 128], BF16)
        ATb = [c.sbuf(f"r_AT{i}", [128, 128], BF16) for i in range(2)]
        QdT = [c.sbuf(f"r_QdT{i}", [128, 128], BF16) for i in range(2)]
        Vd = [c.sbuf(f"r_Vd{i}", [128, 128], BF16) for i in range(2)]
        osum, t_osum = c.sbuf("r_osum", [128, 128], F32)
        junk, t_junk = c.sbuf("r_junk", [128, 128], F32)
        ss, t_ss = c.sbuf("r_ss", [128, 1], F32)
        yb, t_yb = c.sbuf("r_yb", [128, 128], BF16)
        ysb = [c.sbuf(f"r_ysb{i}", [128, 128], BF16) for i in range(2)]
        orders = [[0, 1] + list(range(2, 66)), [1, 0] + list(range(65, 1, -1))]
        st = {}

        def r_prep(n, d, par):
            at, tat = ATb[par]
            qd, tqd = QdT[par]
            vd, tvd = Vd[par]
            psa, tpsa = s.ps()
            c.op("pe", lambda e: e.matmul(psa[:, 0:128], lhsT=KTa[:, n, :], rhs=QTa[:, n, :], start=True, stop=True), reads=[t_KTa, t_QTa], writes=[tpsa])
            c.op("dve", lambda e: e.tensor_tensor(out=at[:], in0=psa[:, 0:128], in1=DT[d][0][:], op=ALU.mult), reads=[tpsa, DT[d][1]], writes=[tat])
            c.op("pool", lambda e: e.tensor_tensor(out=qd[:], in0=QTa[:, n, :], in1=RQ[d][0][:], op=ALU.mult), reads=[t_QTa, RQ[d][1]], writes=[tqd])
            c.op("pool", lambda e: e.tensor_scalar(out=vd[:], in0=Va[:, n, :], scalar1=kd[:, d:d + 1], scalar2=None, op0=ALU.mult), reads=[t_Va, t_kd], writes=[tvd])
            pso, tpso = s.ps()
            c.op("pe", lambda e: e.matmul(pso[:, 0:128], lhsT=at[:], rhs=Va[:, n, :], start=True, stop=False), reads=[tat, t_Va], writes=[tpso])
            pss, tpss = s.ps()
            c.op("pe", lambda e: e.matmul(pss[:, 0:128], lhsT=Ka[:, n, :], rhs=vd[:], start=True, stop=True), reads=[t_Ka, tvd], writes=[tpss])
            st[par] = (pso, tpso, pss, tpss)

        def r_seq(n, d, par, it):
            qd, tqd = QdT[par]
            pso, tpso, pss, tpss = st[par]
            c.op("pe", lambda e: e.matmul(pso[:, 0:128], lhsT=qd[:], rhs=Sb[:], start=False, stop=True), reads=[tqd, t_Sb], writes=[tpso])
            c.op("dve", lambda e: e.scalar_tensor_tensor(out=S[:], in0=S[:], scalar=g128[:, d:d + 1], in1=pss[:, 0:128], op0=ALU.mult, op1=ALU.add),
                 reads=[tpss, t_g128], writes=[t_S])
            c.op("dve", lambda e: e.tensor_copy(out=Sb[:], in_=S[:]), reads=[t_S], writes=[t_Sb])
            if d == 0:
                c.op("act", lambda e: e.activation(out=Of[:, n, :], in_=pso[:, 0:128], func=AF.Copy), reads=[tpso], writes=[t_Of])
            else:
                c.op("dve", lambda e: e.tensor_tensor(out=osum[:], in0=pso[:, 0:128], in1=Of[:, n, :], op=ALU.add), reads=[tpso, t_Of], writes=[t_osum])
                c.op("act", lambda e: e.activation(out=junk[:], in_=osum[:], func=AF.Square, accum_out=ss[:]), reads=[t_osum], writes=[t_junk, t_ss])
                c.op("dve", lambda e: e.tensor_scalar(out=ss[:], in0=ss[:], scalar1=1.0 / 128, scalar2=EPS, op0=ALU.mult, op1=ALU.add), reads=[], writes=[t_ss])
                c.op("act", lambda e: e.activation(out=ss[:], in_=ss[:], func=AF.Sqrt), reads=[], writes=[t_ss])
                c.op("dve", lambda e: e.reciprocal(out=ss[:], in_=ss[:]), reads=[], writes=[t_ss])
                c.op("dve", lambda e: e.scalar_tensor_tensor(out=yb[:], in0=osum[:], scalar=ss[:, 0:1], in1=SG[:, n, :], op0=ALU.mult, op1=ALU.mult),
                     reads=[t_osum, t_ss, t_SG], writes=[t_yb])
                pT, tpT = s.pst()
                c.op("pe", lambda e: e.transpose(out=pT[:, 0:128], in_=yb[:], identity=s.ident_b[:]), reads=[t_yb, s.t_ident_b], writes=[tpT])
                y, ty = ysb[it % 2]
                c.op("dve", lambda e: e.tensor_copy(out=y[:], in_=pT[:, 0:128]), reads=[tpT], writes=[ty])
                s.scatter_y(y[:], ty, 0, n * 128, 128)

        it = 0
        for d in range(2):
            c.op("dve", lambda e: e.memset(S[:], 0.0), reads=[], writes=[t_S])
            c.op("dve", lambda e: e.memset(Sb[:], 0.0), reads=[], writes=[t_Sb])
            order = orders[d]
            r_prep(order[0], d, it % 2)
            for i, n in enumerate(order):
                if i + 1 < len(order):
                    r_prep(order[i + 1], d, (it + 1) % 2)
                r_seq(n, d, it % 2, it)
                it += 1

    def phase_dn(self, l):
        s, c = self, self.c
        s.alloc_scatter()
        cw0 = l * 15
        def mask(name, pattern, cm, op):
            t, tt = c.sbuf(name, [128, 128], F32)
            c.op("pool", lambda e: e.affine_select(out=t[:], in_=s.ones_f[:], pattern=pattern, compare_op=op, fill=0.0, base=0, channel_multiplier=cm),
                 reads=[s.t_ones_f], writes=[tt])
            return t, tt
        Mge = mask("d_Mge", [[1, 128]], -1, ALU.is_ge)
        Mgt = mask("d_Mgt", [[1, 128]], -1, ALU.is_gt)
        Mle = mask("d_Mle", [[-1, 128]], 1, ALU.is_ge)
        Mlt = mask("d_Mlt", [[-1, 128]], 1, ALU.is_gt)
        TRI = [Mge, Mle]
        STL = [Mlt, Mgt]
        QTd, t_QTd = c.sbuf("d_QT", [128, T], BF16)
        KTd, t_KTd = c.sbuf("d_KT", [128, T], BF16)
        Ktm, t_Ktm = c.sbuf("d_Ktm", [128, 66, 128], BF16)
        Vtm, t_Vtm = c.sbuf("d_Vtm", [128, 66, 128], BF16)
        Of, t_Of = c.sbuf("d_Of", [128, 66, 128], F32)
        xin = [c.sbuf(f"d_xin{i}", [128, 516], F32) for i in range(2)]
        acc = [c.sbuf(f"d_acc{i}", [128, 512], F32) for i in range(2)]
        sq, t_sq = c.sbuf("d_sq", [128, 512], F32)
        rn, t_rn = c.sbuf("d_rn", [128, 512], F32)
        vt, t_vt = c.sbuf("d_vt", [128, 512], BF16)
        PFv = s.PF.rearrange("(s p) t -> s p t", p=128)
        seqs = [(0, LC)] + [(LC, T)]
        tiles = []
        for (sa, sb_) in seqs:
            t0 = sa
            while t0 < sb_:
                n = min(512, sb_ - t0)
                tiles.append((t0, n, t0 == sa, t0 + n == sb_))
                t0 += n
        k = 0
        for q in range(3):
            for (t0, n, first, last) in tiles:
                xi, txi = xin[k % 2]
                ac, tac = acc[k % 2]
                k += 1
                lo = 0 if not first else 2
                hi = n + 4 if not last else n + 2
                if first:
                    c.op("pool", lambda e, xi=xi: e.memset(xi[:, 0:2], 0.0), reads=[], writes=[txi])
                if last:
                    c.op("pool", lambda e, xi=xi, n=n: e.memset(xi[:, n + 2:n + 4], 0.0), reads=[], writes=[txi])
                c.dma("sp", xi[:, lo:hi], PFv[q][:, t0 - 2 + lo:t0 - 2 + hi], reads=[s.t_PF], writes=[txi])
                wq = cw0 + q * 5
                c.op("dve", lambda e, xi=xi, ac=ac, n=n, wq=wq: e.tensor_scalar(out=ac[:, 0:n], in0=xi[:, 0:n], scalar1=s.convw_s[:, wq:wq + 1], scalar2=None, op0=ALU.mult),
                     reads=[txi, s.t_convw], writes=[tac])
                for tap in range(1, 5):
                    c.op("dve", lambda e, xi=xi, ac=ac, n=n, wq=wq, tap=tap: e.scalar_tensor_tensor(out=ac[:, 0:n], in0=xi[:, tap:tap + n], scalar=s.convw_s[:, wq + tap:wq + tap + 1],
                                                                                                  in1=ac[:, 0:n], op0=ALU.mult, op1=ALU.add), reads=[txi, s.t_convw], writes=[tac])
                c.op("act", lambda e, ac=ac, n=n: e.activation(out=ac[:, 0:n], in_=ac[:, 0:n], func=AF.Silu), reads=[], writes=[tac])
                if q < 2:
                    c.op("dve", lambda e, ac=ac, n=n: e.tensor_tensor(out=sq[:, 0:n], in0=ac[:, 0:n], in1=ac[:, 0:n], op=ALU.mult), reads=[tac], writes=[t_sq])
                    ps, tps = s.ps()
                    c.op("pe", lambda e, ps=ps, n=n: e.matmul(ps[:, 0:n], lhsT=s.ones_f[:], rhs=sq[:, 0:n], start=True, stop=True), reads=[t_sq, s.t_ones_f], writes=[tps])
                    c.op("act", lambda e, ps=ps, n=n: e.activation(out=rn[:, 0:n], in_=ps[:, 0:n], func=AF.Sqrt, bias=s.eps_t[:, 0:1]), reads=[tps, s.t_eps], writes=[t_rn])
                    c.op("dve", lambda e, n=n: e.reciprocal(out=rn[:, 0:n], in_=rn[:, 0:n]), reads=[], writes=[t_rn])
                    dst, tdst = (QTd, t_QTd) if q == 0 else (KTd, t_KTd)
                    scl = 128.0 ** -0.5 if q == 0 else 1.0
                    c.op("dve", lambda e, ac=ac, n=n, dst=dst, t0=t0, scl=scl: e.scalar_tensor_tensor(out=dst[:, t0:t0 + n], in0=ac[:, 0:n], scalar=scl, in1=rn[:, 0:n],
                                                                                                     op0=ALU.mult, op1=ALU.mult), reads=[tac, t_rn], writes=[tdst])
                    src, tsrc = dst, tdst
                    soff = t0
                else:
                    c.op("dve", lambda e, ac=ac, n=n: e.tensor_copy(out=vt[:, 0:n], in_=ac[:, 0:n]), reads=[tac], writes=[t_vt])
                    src, tsrc = vt, t_vt
                    soff = 0
                if q >= 1:
                    dtm, tdtm = (Ktm, t_Ktm) if q == 1 else (Vtm, t_Vtm)
                    for sub in range(n // 128):
                        pT, tpT = s.pst()
                        c.op("pe", lambda e, pT=pT, src=src, soff=soff, sub=sub: e.transpose(out=pT[:, 0:128], in_=src[:, soff + sub * 128:soff + (sub + 1) * 128], identity=s.ident_b[:]),
                             reads=[tsrc, s.t_ident_b], writes=[tpT])
                        nb_ = (t0 + sub * 128) // 128
                        c.op("dve", lambda e, pT=pT, dtm=dtm, nb_=nb_: e.tensor_copy(out=dtm[:, nb_, :], in_=pT[:, 0:128]), reads=[tpT], writes=[tdtm])
        AB, t_AB = c.sbuf("d_AB", [128, 66, 4], F32)
        c.dma("sp", AB[:], s.PT.rearrange("(n p) c -> p n c", p=128)[:, :, 1152:1156], reads=[s.t_PT], writes=[t_AB])
        nega, t_nega = c.sbuf("d_nega", [128, 2], F32)
        c.op("act", lambda e: e.activation(out=nega[:], in_=s.hp_s[:, 8 + l * 2:8 + l * 2 + 2], func=AF.Exp), reads=[s.t_hp], writes=[t_nega])
        c.op("dve", lambda e: e.tensor_scalar(out=nega[:], in0=nega[:], scalar1=-1.0, scalar2=None, op0=ALU.mult), reads=[], writes=[t_nega])
        G, t_G = c.sbuf("d_G", [128, 2, 66], F32)
        Bt, t_Bt = c.sbuf("d_B", [128, 2, 66], F32)
        NB, t_NB = c.sbuf("d_NB", [128, 2, 66], F32)
        for d in range(2):
            c.op("act", lambda e, d=d: e.activation(out=G[:, d, :], in_=AB[:, :, d], func=AF.Exp, bias=s.hp_s[:, 16 + l * 2 + d:16 + l * 2 + d + 1]),
                 reads=[t_AB, s.t_hp], writes=[t_G])
            c.op("act", lambda e, d=d: e.activation(out=G[:, d, :], in_=G[:, d, :], func=AF.Ln, bias=s.one_t[:, 0:1]), reads=[s.t_eps], writes=[t_G])
            c.op("dve", lambda e, d=d: e.tensor_scalar(out=G[:, d, :], in0=G[:, d, :], scalar1=nega[:, d:d + 1], scalar2=None, op0=ALU.mult), reads=[t_nega], writes=[t_G])
            c.op("act", lambda e, d=d: e.activation(out=Bt[:, d, :], in_=AB[:, :, 2 + d], func=AF.Sigmoid), reads=[t_AB], writes=[t_Bt])
        c.op("dve", lambda e: e.tensor_scalar(out=NB[:], in0=Bt[:], scalar1=-1.0, scalar2=None, op0=ALU.mult), reads=[t_Bt], writes=[t_NB])
        def T2(name, dt=F32, shape=(128, 128)):
            return [c.sbuf(f"{name}{i}", list(shape), dt) for i in range(2)]
        Xg = T2("d_Xg"); sm = T2("d_sm", F32, (128, 8)); Er = T2("d_Er"); DTm = T2("d_DTm"); DLm = T2("d_DLm")
        Pm = T2("d_P"); PTm = T2("d_PT"); Mi = T2("d_M"); Xa = T2("d_Xa"); Xb = T2("d_Xb"); XTa = T2("d_XTa"); XTb = T2("d_XTb")
        TTb = T2("d_TT", BF16); bv = T2("d_bv", BF16); kbg = T2("d_kbg", BF16)
        wv = T2("d_wv"); kcT = T2("d_kcT", BF16); qkT = T2("d_qkT", BF16); qgT = T2("d_qgT", BF16); kg = T2("d_kg", BF16)
        vnb, t_vnb = c.sbuf("d_vnb", [128, 128], BF16)
        S, t_S = c.sbuf("d_S", [128, 128], F32)
        Sb, t_Sb = c.sbuf("d_Sb", [128, 128], BF16)
        zt = [c.sbuf(f"d_z{i}", [128, 128], F32) for i in range(2)]
        osum, t_osum = c.sbuf("d_osum", [128, 128], F32)
        junk, t_junk = c.sbuf("d_junk", [128, 128], F32)
        ss, t_ss = c.sbuf("d_ss", [128, 1], F32)
        yb, t_yb = c.sbuf("d_yb", [128, 128], BF16)
        ysb = [c.sbuf(f"d_ysb{i}", [128, 128], BF16) for i in range(2)]
        nwd = s.nw_s[:, (l * 3) * 128:(l * 3 + 1) * 128]

        def evac(eng, dst, tdst, ps, tps):
            if eng == "act":
                c.op("act", lambda e: e.activation(out=dst[:], in_=ps[:, 0:128], func=AF.Copy), reads=[tps], writes=[tdst])
            else:
                c.op("dve", lambda e: e.tensor_copy(out=dst[:], in_=ps[:, 0:128]), reads=[tps], writes=[tdst])

        def block_prep(n, d, par):
            blk = slice(n * 128, (n + 1) * 128)
            tri, ttri = TRI[d]
            stl, tstl = STL[d]
            g = G[:, d, n:n + 1]
            xg, txg = Xg[par]
            smt, tsm = sm[par]
            c.op("dve", lambda e: e.tensor_scalar(out=xg[:], in0=tri[:], scalar1=g, scalar2=None, op0=ALU.mult), reads=[ttri, t_G], writes=[txg])
            psm, tpsm = s.ps()
            c.op("pe", lambda e: e.matmul(psm[:, 0:1], lhsT=tri[:], rhs=G[:, d, n:n + 1], start=True, stop=True), reads=[ttri, t_G], writes=[tpsm])
            c.op("pe", lambda e: e.matmul(psm[:, 2:3], lhsT=s.ones_f[:], rhs=G[:, d, n:n + 1], start=True, stop=True), reads=[s.t_ones_f, t_G], writes=[tpsm])
            psR, tpsR = s.ps()
            c.op("pe", lambda e: e.matmul(psR[:, 0:128], lhsT=s.ones_f[:], rhs=xg[:], start=True, stop=True), reads=[s.t_ones_f, txg], writes=[tpsR])
            c.op("dve", lambda e: e.tensor_copy(out=smt[:, 0:1], in_=psm[:, 0:1]), reads=[tpsm], writes=[tsm])
            c.op("dve", lambda e: e.tensor_copy(out=smt[:, 1:2], in_=psm[:, 2:3]), reads=[tpsm], writes=[tsm])
            c.op("act", lambda e: e.activation(out=smt[:, 2:3], in_=smt[:, 0:1], func=AF.Exp), reads=[], writes=[tsm])
            c.op("act", lambda e: e.activation(out=smt[:, 3:4], in_=smt[:, 0:1], func=AF.Exp, scale=-1.0, bias=smt[:, 1:2]), reads=[], writes=[tsm])
            c.op("act", lambda e: e.activation(out=smt[:, 4:5], in_=smt[:, 1:2], func=AF.Exp), reads=[], writes=[tsm])
            c.op("dve", lambda e: e.tensor_tensor(out=smt[:, 5:6], in0=smt[:, 2:3], in1=Bt[:, d, n:n + 1], op=ALU.mult), reads=[t_Bt], writes=[tsm])
            er, ter = Er[par]
            c.op("act", lambda e: e.activation(out=er[:], in_=psR[:, 0:128], func=AF.Exp), reads=[tpsR], writes=[ter])
            dtm, tdtm = DTm[par]
            c.op("dve", lambda e: e.tensor_scalar(out=dtm[:], in0=psR[:, 0:128], scalar1=smt[:, 0:1], scalar2=0.0, op0=ALU.subtract, op1=ALU.min), reads=[tpsR, tsm], writes=[tdtm])
            c.op("act", lambda e: e.activation(out=dtm[:], in_=dtm[:], func=AF.Exp), reads=[], writes=[tdtm])
            c.op("dve", lambda e: e.tensor_tensor(out=dtm[:], in0=dtm[:], in1=tri[:], op=ALU.mult), reads=[ttri], writes=[tdtm])
            dlm, tdlm = DLm[par]
            c.op("dve", lambda e: e.tensor_scalar(out=dlm[:], in0=psR[:, 0:128], scalar1=smt[:, 0:1], scalar2=0.0, op0=ALU.subtract, op1=ALU.max), reads=[tpsR, tsm], writes=[tdlm])
            c.op("act", lambda e: e.activation(out=dlm[:], in_=dlm[:], func=AF.Exp, scale=-1.0), reads=[], writes=[tdlm])
            c.op("pool", lambda e: e.tensor_tensor(out=dlm[:], in0=dlm[:], in1=stl[:], op=ALU.mult), reads=[tstl], writes=[tdlm])
            psK, tpsK = s.ps()
            c.op("pe", lambda e: e.matmul(psK[:, 0:128], lhsT=KTd[:, blk], rhs=KTd[:, blk], start=True, stop=True), reads=[t_KTd], writes=[tpsK])
            pm, tpm = Pm[par]
            c.op("dve", lambda e: e.scalar_tensor_tensor(out=pm[:], in0=psK[:, 0:128], scalar=NB[:, d, n:n + 1], in1=dlm[:], op0=ALU.mult, op1=ALU.mult),
                 reads=[tpsK, t_NB, tdlm], writes=[tpm])
            psT_, tpsT_ = s.ps()
            c.op("pe", lambda e: e.transpose(out=psT_[:, 0:128], in_=pm[:], identity=s.ident_f[:]), reads=[tpm, s.t_ident_f], writes=[tpsT_])
            ptm, tptm = PTm[par]
            evac("act", ptm, tptm, psT_, tpsT_)
            mi, tmi = Mi[par]
            c.op("dve", lambda e: e.tensor_tensor(out=mi[:], in0=ptm[:], in1=s.ident_f[:], op=ALU.add), reads=[tptm, s.t_ident_f], writes=[tmi])
            X, tX = ptm, tptm
            XT_, tXT = pm, tpm
            bufs = [(Xa[par], XTa[par]), (Xb[par], XTb[par])]
            for m in range(1, 7):
                (xn, txn), (xtn, txtn) = bufs[m % 2]
                if m < 6:
                    p1, tp1 = s.ps()
                    c.op("pe", lambda e, p1=p1, X=X, XT_=XT_: e.matmul(p1[:, 0:128], lhsT=XT_[:], rhs=X[:], start=True, stop=True), reads=[tX, tXT], writes=[tp1])
                    evac("act", xn, txn, p1, tp1)
                p2, tp2 = s.ps()
                c.op("pe", lambda e, p2=p2, X=X, XT_=XT_: e.matmul(p2[:, 0:128], lhsT=X[:], rhs=XT_[:], start=True, stop=True), reads=[tX, tXT], writes=[tp2])
                evac("dve", xtn, txtn, p2, tp2)
                p3, tp3 = s.ps()
                c.op("pe", lambda e, p3=p3, xtn=xtn: e.matmul(p3[:, 0:128], lhsT=xtn[:], rhs=mi[:], start=True, stop=True), reads=[txtn, tmi], writes=[tp3])
                c.op("dve", lambda e, p3=p3: e.tensor_tensor(out=mi[:], in0=mi[:], in1=p3[:, 0:128], op=ALU.add), reads=[tp3], writes=[tmi])
                X, tX, XT_, tXT = xn, txn, xtn, txtn
            tt, ttt = TTb[par]
            c.op("act", lambda e: e.activation(out=tt[:], in_=mi[:], func=AF.Copy), reads=[tmi], writes=[ttt])
            b_, tb_ = bv[par]
            c.op("pool", lambda e: e.tensor_scalar(out=b_[:], in0=Vtm[:, n, :], scalar1=Bt[:, d, n:n + 1], scalar2=None, op0=ALU.mult), reads=[t_Vtm, t_Bt], writes=[tb_])
            kb_, tkb_ = kbg[par]
            c.op("pool", lambda e: e.tensor_scalar(out=kb_[:], in0=Ktm[:, n, :], scalar1=smt[:, 5:6], scalar2=None, op0=ALU.mult), reads=[t_Ktm, tsm], writes=[tkb_])
            pw, tpw = s.ps()
            c.op("pe", lambda e: e.matmul(pw[:, 0:128], lhsT=tt[:], rhs=b_[:], start=True, stop=True), reads=[ttt, tb_], writes=[tpw])
            evac("act", wv[par][0], wv[par][1], pw, tpw)
            pk, tpk = s.ps()
            c.op("pe", lambda e: e.matmul(pk[:, 0:128], lhsT=kb_[:], rhs=tt[:], start=True, stop=True), reads=[tkb_, ttt], writes=[tpk])
            evac("dve", kcT[par][0], kcT[par][1], pk, tpk)
            pq, tpq = s.ps()
            c.op("pe", lambda e: e.matmul(pq[:, 0:128], lhsT=KTd[:, blk], rhs=QTd[:, blk], start=True, stop=True), reads=[t_KTd, t_QTd], writes=[tpq])
            c.op("dve", lambda e: e.tensor_tensor(out=qkT[par][0][:], in0=pq[:, 0:128], in1=dtm[:], op=ALU.mult), reads=[tpq, tdtm], writes=[qkT[par][1]])
            c.op("pool", lambda e: e.tensor_tensor(out=qgT[par][0][:], in0=QTd[:, blk], in1=er[:], op=ALU.mult), reads=[t_QTd, ter], writes=[qgT[par][1]])
            c.op("pool", lambda e: e.tensor_scalar(out=kg[par][0][:], in0=Ktm[:, n, :], scalar1=smt[:, 3:4], scalar2=None, op0=ALU.mult), reads=[t_Ktm, tsm], writes=[kg[par][1]])

        def block_seq(n, d, par, it):
            smt, tsm = sm[par]
            p1, tp1 = s.ps()
            c.op("pe", lambda e: e.matmul(p1[:, 0:128], lhsT=kcT[par][0][:], rhs=Sb[:], start=True, stop=True), reads=[kcT[par][1], t_Sb], writes=[tp1])
            c.op("dve", lambda e: e.tensor_tensor(out=vnb[:], in0=wv[par][0][:], in1=p1[:, 0:128], op=ALU.subtract), reads=[wv[par][1], tp1], writes=[t_vnb])
            po, tpo = s.ps()
            c.op("pe", lambda e: e.matmul(po[:, 0:128], lhsT=qgT[par][0][:], rhs=Sb[:], start=True, stop=False), reads=[qgT[par][1], t_Sb], writes=[tpo])
            c.op("pe", lambda e: e.matmul(po[:, 0:128], lhsT=qkT[par][0][:], rhs=vnb[:], start=False, stop=True), reads=[qkT[par][1], t_vnb], writes=[tpo])
            pS, tpS = s.ps()
            c.op("pe", lambda e: e.matmul(pS[:, 0:128], lhsT=kg[par][0][:], rhs=vnb[:], start=True, stop=True), reads=[kg[par][1], t_vnb], writes=[tpS])
            c.op("dve", lambda e: e.scalar_tensor_tensor(out=S[:], in0=S[:], scalar=smt[:, 4:5], in1=pS[:, 0:128], op0=ALU.mult, op1=ALU.add), reads=[tpS, tsm], writes=[t_S])
            c.op("dve", lambda e: e.tensor_copy(out=Sb[:], in_=S[:]), reads=[t_S], writes=[t_Sb])
            if d == 0:
                c.op("act", lambda e: e.activation(out=Of[:, n, :], in_=po[:, 0:128], func=AF.Copy), reads=[tpo], writes=[t_Of])
            else:
                z, tz = zt[it % 2]
                c.dma("sp", z[:], s.PT[n * 128:(n + 1) * 128, 1024:1152], reads=[s.t_PT], writes=[tz])
                c.op("act", lambda e: e.activation(out=z[:], in_=z[:], func=AF.Silu), reads=[], writes=[tz])
                c.op("dve", lambda e: e.tensor_tensor(out=osum[:], in0=po[:, 0:128], in1=Of[:, n, :], op=ALU.add), reads=[tpo, t_Of], writes=[t_osum])
                c.op("act", lambda e: e.activation(out=junk[:], in_=osum[:], func=AF.Square, accum_out=ss[:]), reads=[t_osum], writes=[t_junk, t_ss])
                c.op("dve", lambda e: e.tensor_scalar(out=ss[:], in0=ss[:], scalar1=1.0 / 128, scalar2=EPS, op0=ALU.mult, op1=ALU.add), reads=[], writes=[t_ss])
                c.op("act", lambda e: e.activation(out=ss[:], in_=ss[:], func=AF.Sqrt), reads=[], writes=[t_ss])
                c.op("dve", lambda e: e.reciprocal(out=ss[:], in_=ss[:]), reads=[], writes=[t_ss])
                c.op("dve", lambda e: e.scalar_tensor_tensor(out=osum[:], in0=osum[:], scalar=ss[:, 0:1], in1=nwd, op0=ALU.mult, op1=ALU.mult), reads=[t_ss, s.t_nw], writes=[t_osum])
                c.op("dve", lambda e: e.tensor_tensor(out=yb[:], in0=osum[:], in1=z[:], op=ALU.mult), reads=[tz, t_osum], writes=[t_yb])
                pT, tpT = s.pst()
                c.op("pe", lambda e: e.transpose(out=pT[:, 0:128], in_=yb[:], identity=s.ident_b[:]), reads=[t_yb, s.t_ident_b], writes=[tpT])
                y, ty = ysb[it % 2]
                c.op("dve", lambda e: e.tensor_copy(out=y[:], in_=pT[:, 0:128]), reads=[tpT], writes=[ty])
                s.scatter_y(y[:], ty, 1, n * 128, 128)

        orders = [[0, 1] + list(range(2, 66)), [1, 0] + list(range(65, 1, -1))]
        it = 0
        for d in range(2):
            c.op("dve", lambda e: e.memset(S[:], 0.0), reads=[], writes=[t_S])
            c.op("dve", lambda e: e.memset(Sb[:], 0.0), reads=[], writes=[t_Sb])
            order = orders[d]
            block_prep(order[0], d, it % 2)
            for i, n in enumerate(order):
                if i + 1 < len(order):
                    block_prep(order[i + 1], d, (it + 1) % 2)
                block_seq(n, d, it % 2, it)
                it += 1

    def layer_norm_fm(self, xb, t_xb, n, l, which_w, which_b):
        s, c = self, self.c
        ps1, tps1 = s.ps()
        ps2, tps2 = s.ps()
        for f in range(KC):
            zb, tzb = s.c_zb[f % 2]
            zq, tzq = s.c_zq[f % 2]
            c.op("dve", lambda e, zb=zb, f=f: e.tensor_copy(out=zb[:, 0:n], in_=xb[:, f, 0:n]), reads=[t_xb], writes=[tzb])
            c.op("act", lambda e, zq=zq, f=f: e.activation(out=zq[:, 0:n], in_=xb[:, f, 0:n], func=AF.Square), reads=[t_xb], writes=[tzq])
            c.op("pe", lambda e, zb=zb, f=f: e.matmul(ps1[:, 0:n], lhsT=s.ones_b[:], rhs=zb[:, 0:n], start=(f == 0), stop=(f == KC - 1)),
                 reads=[tzb, s.t_ones_b], writes=[tps1])
            c.op("pe", lambda e, zq=zq, f=f: e.matmul(ps2[:, 0:n], lhsT=s.ones_b[:], rhs=zq[:, 0:n], start=(f == 0), stop=(f == KC - 1)),
                 reads=[tzq, s.t_ones_b], writes=[tps2])
        mean, msq, rstd, nmr = s.c_ln
        t_ln = s.t_c_ln
        c.op("act", lambda e: e.activation(out=mean[:, 0:n], in_=ps1[:, 0:n], func=AF.Copy, scale=1.0 / D), reads=[tps1], writes=[t_ln])
        c.op("dve", lambda e: e.tensor_tensor(out=msq[:, 0:n], in0=mean[:, 0:n], in1=mean[:, 0:n], op=ALU.mult), reads=[], writes=[t_ln])
        c.op("dve", lambda e: e.scalar_tensor_tensor(out=rstd[:, 0:n], in0=ps2[:, 0:n], scalar=1.0 / D, in1=msq[:, 0:n], op0=ALU.mult, op1=ALU.subtract),
             reads=[tps2], writes=[t_ln])
        c.op("dve", lambda e: e.tensor_scalar(out=rstd[:, 0:n], in0=rstd[:, 0:n], scalar1=0.0, scalar2=EPS / (ALPHA * ALPHA), op0=ALU.max, op1=ALU.add), reads=[], writes=[t_ln])
        c.op("act", lambda e: e.activation(out=rstd[:, 0:n], in_=rstd[:, 0:n], func=AF.Sqrt), reads=[], writes=[t_ln])
        c.op("dve", lambda e: e.reciprocal(out=rstd[:, 0:n], in_=rstd[:, 0:n]), reads=[], writes=[t_ln])
        c.op("dve", lambda e: e.tensor_tensor(out=nmr[:, 0:n], in0=mean[:, 0:n], in1=rstd[:, 0:n], op=ALU.mult), reads=[], writes=[t_ln])
        for f in range(KC):
            eng = "dve" if f % 2 == 0 else "pool"
            c.op(eng, lambda e, f=f: e.tensor_tensor(out=xb[:, f, 0:n], in0=xb[:, f, 0:n], in1=rstd[:, 0:n], op=ALU.mult), reads=[t_ln], writes=[t_xb])
            c.op(eng, lambda e, f=f: e.tensor_tensor(out=xb[:, f, 0:n], in0=xb[:, f, 0:n], in1=nmr[:, 0:n], op=ALU.subtract), reads=[t_ln], writes=[t_xb])
            wi = (which_w * 4 + l) * 16 + f
            bi_ = (which_b * 4 + l) * 16 + f
            c.op("act", lambda e, f=f, wi=wi, bi_=bi_: e.activation(out=xb[:, f, 0:n], in_=xb[:, f, 0:n], func=AF.Identity,
                                                                   scale=s.lnp_s[:, wi:wi + 1], bias=s.lnp_s[:, bi_:bi_ + 1]),
                 reads=[s.t_lnp], writes=[t_xb])

    def phase_c(self, l):
        s, c = self, self.c
        xb, t_xb = c.sbuf("c_x", [128, KC, 512], F32)
        yb, t_yb = c.sbuf("c_y", [128, KC, 512], BF16)
        hid, t_hid = c.sbuf("c_hid", [128, 44, 512], BF16)
        s.c_zb = [c.sbuf(f"c_zb{i}", [128, 512], BF16) for i in range(2)]
        s.c_zq = [c.sbuf(f"c_zq{i}", [128, 512], BF16) for i in range(2)]
        lnt, s.t_c_ln = c.sbuf("c_ln", [128, 4, 512], F32)
        s.c_ln = [lnt[:, i, :] for i in range(4)]
        sg = [c.sbuf(f"c_sg{i}", [128, 512], F32) for i in range(2)]
        wo = [c.sbuf(f"c_wo{i}", [128, 2048], BF16) for i in range(2)]
        wf1 = [c.sbuf(f"c_wf1{i}", [128, 4096], BF16) for i in range(2)]
        wf2 = [c.sbuf(f"c_wf2{i}", [128, DFF], BF16) for i in range(2)]
        XTv = s.XT.rearrange("(kc p) t -> p kc t", p=128)
        YOv = s.yown.rearrange("(kc p) t -> p kc t", p=128)
        for (t0_, n_, w_) in s.own_tiles():
            s.phase_c_tile(l, t0_, n_, w_, xb, t_xb, yb, t_yb, hid, t_hid, sg, wo, wf1, wf2, XTv, YOv)

    def phase_c_tile(self, l, t0, n, w, xb, t_xb, yb, t_yb, hid, t_hid, sg, wo, wf1, wf2, XTv, YOv):
        s, c = self, self.c
        if True:
            c.dma("sp", yb[:, :, 0:n], YOv[:, :, t0:t0 + n], reads=[s.t_yown], writes=[t_yb])
            c.dma("sp", xb[:, :, 0:n], XTv[:, :, t0:t0 + n], reads=[s.t_XT], writes=[t_xb])
            for f in range(KC):
                wt, twt = wo[f % 2]
                c.dma("sp", wt[:], s.Wo[l][f * 128:(f + 1) * 128, :], reads=[s.t_W[l]], writes=[twt])
                ps, tps = s.ps()
                for kc in range(KC):
                    c.op("pe", lambda e, ps=ps, wt=wt, kc=kc: e.matmul(ps[:, 0:n], lhsT=wt[:, kc * 128:(kc + 1) * 128], rhs=yb[:, kc, 0:n],
                                                                      start=(kc == 0), stop=(kc == KC - 1)), reads=[twt, t_yb], writes=[tps], defer=(kc != KC - 1))
                c.op("dve", lambda e, ps=ps, f=f: e.scalar_tensor_tensor(out=xb[:, f, 0:n], in0=ps[:, 0:n], scalar=s.MM[:, l, 32 + f, w:w + 1],
                                                                        in1=xb[:, f, 0:n], op0=ALU.mult, op1=ALU.add), reads=[tps, s.t_MM], writes=[t_xb])
            s.layer_norm_fm(xb, t_xb, n, l, 0, 1)
            for f in range(KC):
                eng = "dve" if f % 2 == 0 else "pool"
                c.op(eng, lambda e, f=f: e.tensor_scalar(out=yb[:, f, 0:n], in0=xb[:, f, 0:n], scalar1=s.MM[:, l, 64 + f, w:w + 1],
                                                        scalar2=s.MM[:, l, 48 + f, w:w + 1], op0=ALU.mult, op1=ALU.add), reads=[t_xb, s.t_MM], writes=[t_yb])
            for hc in range(44):
                wt, twt = wf1[hc % 2]
                c.dma("sp", wt[:], s.Wf1[l][hc * 128:(hc + 1) * 128, :], reads=[s.t_W[l]], writes=[twt])
                psG, tpsG = s.ps()
                psU, tpsU = s.ps()
                for kc in range(KC):
                    c.op("pe", lambda e, psG=psG, wt=wt, kc=kc: e.matmul(psG[:, 0:n], lhsT=wt[:, kc * 256:kc * 256 + 128], rhs=yb[:, kc, 0:n],
                                                                        start=(kc == 0), stop=(kc == KC - 1)), reads=[twt, t_yb], writes=[tpsG], defer=(kc != KC - 1))
                for kc in range(KC):
                    c.op("pe", lambda e, psU=psU, wt=wt, kc=kc: e.matmul(psU[:, 0:n], lhsT=wt[:, kc * 256 + 128:kc * 256 + 256], rhs=yb[:, kc, 0:n],
                                                                        start=(kc == 0), stop=(kc == KC - 1)), reads=[twt, t_yb], writes=[tpsU], defer=(kc != KC - 1))
                g, tg = sg[hc % 2]
                c.op("act", lambda e, psG=psG, g=g: e.activation(out=g[:, 0:n], in_=psG[:, 0:n], func=AF.Silu), reads=[tpsG], writes=[tg])
                c.op("dve", lambda e, psU=psU, g=g, hc=hc: e.tensor_tensor(out=hid[:, hc, 0:n], in0=psU[:, 0:n], in1=g[:, 0:n], op=ALU.mult),
                     reads=[tpsU, tg], writes=[t_hid])
            for f in range(KC):
                wt, twt = wf2[f % 2]
                c.dma("sp", wt[:], s.Wf2[l][f * 128:(f + 1) * 128, :], reads=[s.t_W[l]], writes=[twt])
                ps, tps = s.ps()
                for hc in range(44):
                    c.op("pe", lambda e, ps=ps, wt=wt, hc=hc: e.matmul(ps[:, 0:n], lhsT=wt[:, hc * 128:(hc + 1) * 128], rhs=hid[:, hc, 0:n],
                                                                      start=(hc == 0), stop=(hc == 43)), reads=[twt, t_hid], writes=[tps], defer=(hc != 43))
                c.op("dve", lambda e, ps=ps, f=f: e.scalar_tensor_tensor(out=xb[:, f, 0:n], in0=ps[:, 0:n], scalar=s.MM[:, l, 80 + f, w:w + 1],
                                                                        in1=xb[:, f, 0:n], op0=ALU.mult, op1=ALU.add), reads=[tps, s.t_MM], writes=[t_xb])
            s.layer_norm_fm(xb, t_xb, n, l, 2, 3)
            c.dma("sp", XTv[:, :, t0:t0 + n], xb[:, :, 0:n], reads=[t_xb], writes=[s.t_XT])

    def dump_a1(self, l):
        s, c = self, self.c
        s.add_dbg("d_MM", s.MM[:].rearrange("p l c w -> p (l c w)"), [128, NL * 96 * 2], [s.t_MM])
        for i, r0 in enumerate([0, 256, 4096, 8320]):
            s.add_dbg(f"d_PT{i}", s.PT[r0:r0 + 128, :], [128, PW], [s.t_PT])
        s.add_dbg("d_PF0", s.PF[:, 0:512], [384, 512], [s.t_PF])
        s.add_dbg("d_PF1", s.PF[:, T - 512:T], [384, 512], [s.t_PF])
        s.add_dbg("d_COS", s.COS, [L, 64], [s.t_COS])
        s.add_dbg("d_SIN", s.SIN, [L, 64], [s.t_COS])

    def final_phase(self):
        s, c = self, self.c
        with contextlib.ExitStack() as stp:
            c.stack = stp
            xin = [c.sbuf(f"fx{i}", [128, KC, 128], F32) for i in range(2)]
            xo = [c.sbuf(f"fo{i}", [128, D], F32) for i in range(2)]
            XTv = s.XT.rearrange("(kc p) t -> p kc t", p=128)
            t_out = Trk("out")
            for ti in range(16):
                a, ta = xin[ti % 2]
                o, to = xo[ti % 2]
                c.dma("sp", a[:], XTv[:, :, 64 + ti * 128:64 + (ti + 1) * 128], reads=[s.t_XT], writes=[ta])
                for g in range(4):
                    ps, tps = s.ps()
                    for q in range(4):
                        kc = g * 4 + q
                        c.op("pe", lambda e, a=a, ps=ps, kc=kc, q=q: e.transpose(out=ps[:, q * 128:(q + 1) * 128], in_=a[:, kc, :], identity=s.ident_f[:]),
                             reads=[ta, s.t_ident_f], writes=[tps])
                    c.op("dve", lambda e, o=o, ps=ps, g=g: e.tensor_copy(out=o[:, g * 512:(g + 1) * 512], in_=ps[:, :]), reads=[tps], writes=[to])
                c.dma("sp", s.out[ti * 128:(ti + 1) * 128, :], o[:], reads=[to], writes=[t_out])
            c.barrier()


def _in_tile_cols(j):
    hk = j // 2
    return [
        (0 + j * 128, 128), (512 + j * 128, 128), (1024 + j * 128, 128), (1536 + j * 128, 128),
        (4112 + (2 * j) * 128, 128), (4112 + (2 * j + 1) * 128, 128), (5136 + hk * 128, 128), (5392 + hk * 128, 128),
        (3584 + j * 128, 128), ([4096 + j, 4100 + j, 4104 + j, 4108 + j], 4),
        (2048 + j * 128, 128), (2560 + j * 128, 128), (3072 + j * 128, 128),
    ]


def make_in_maps(inp, n_layers=NL):
    x, cvec, ctx, c_ctx = inp["x"], inp["c"], inp["ctx"], inp["c_ctx"]
    f32 = np.float32
    lnp = np.stack([inp["ln1_w"], inp["ln1_b"], inp["ln2_w"], inp["ln2_b"]], 0)
    lnp = lnp.reshape(4, NL, KC, 128).transpose(3, 0, 1, 2).reshape(128, 256)
    w_o, w_f1, w_f2 = inp["w_o"], inp["w_ffn_in"], inp["w_ffn_out"]
    Wot = np.empty((NL, 2048, 2048), f32)
    Wf1t = np.empty((NL, DFF, 4096), f32)
    Wf2t = np.empty((NL, 2048, DFF), f32)
    rows = []
    for blk in range(4):
        rows += [blk * 128, 512 + blk * 128, 1024 + (2 * blk) * 128, 1024 + (2 * blk + 1) * 128]
    for l in range(NL):
        wo_perm = np.concatenate([w_o[l, r:r + 128, :] for r in rows], 0)
        Wot[l] = wo_perm.reshape(KC, 128, KC, 128).transpose(2, 1, 0, 3).reshape(2048, 2048)
        Wf1t[l] = w_f1[l].reshape(KC, 128, 2, 44, 128).transpose(3, 1, 0, 2, 4).reshape(DFF, 4096)
        Wf2t[l] = w_f2[l].reshape(44, 128, KC, 128).transpose(2, 1, 0, 3).reshape(2048, DFF)
    w_ada = inp["w_ada"]
    maps = []
    for r in range(NCORE):
        b, j = r // 4, r % 4
        m = {}
        m["x_own"] = np.ascontiguousarray(np.concatenate([ctx[b, 64 * j:64 * j + 64], x[b, 2048 * j:2048 * (j + 1)]], 0))
        m["cT"] = np.ascontiguousarray(np.stack([cvec[b], c_ctx], -1).reshape(KC, 128, 2).transpose(1, 0, 2).reshape(128, 32))
        wa = w_ada[:, :, 3072 * j:3072 * (j + 1)].reshape(NL, KC, 128, 24, 128).transpose(0, 3, 2, 1, 4)
        m["w_ada_t"] = np.ascontiguousarray(wa).reshape(NL * 24 * 128, 2048)[:n_layers * 24 * 128]
        m["b_ada_s"] = np.ascontiguousarray(inp["b_ada"][:, 3072 * j:3072 * (j + 1)].reshape(NL, 24, 128).transpose(2, 0, 1).reshape(128, 96))
        sb = np.zeros((128, 2), f32); sb[:, b] = 1.0
        m["selb"] = sb
        s4 = np.zeros((128, 4), f32); s4[:, j] = 1.0
        m["sel4"] = s4
        wsel = np.zeros((NL, 13, 128, KC, 128), f32)
        for ti, (c0, nc_) in enumerate(_in_tile_cols(j)):
            idx = np.asarray(c0) if isinstance(c0, list) else np.arange(c0, c0 + nc_)
            wsel[:, ti, :, :, :nc_] = inp["w_in"][:, :, idx].reshape(NL, KC, 128, nc_).transpose(0, 2, 1, 3)
        m["w_in_sel"] = np.ascontiguousarray(wsel.transpose(0, 2, 1, 3, 4)).reshape(NL * 128, 13 * 2048)[:n_layers * 128]
        m["w_o_s"] = np.ascontiguousarray(Wot.reshape(NL, 4, 4, 128, 2048)[:, :, j]).reshape(NL * 128, 8192)[:n_layers * 128]
        m["w_f1_s"] = np.ascontiguousarray(Wf1t.reshape(NL, 11, 4, 128, 4096)[:, :, j]).reshape(NL * 128, 45056)[:n_layers * 128]
        m["w_f2_s"] = np.ascontiguousarray(Wf2t.reshape(NL, 8, 4, 64, DFF)[:, :, j]).reshape(NL * 128, 22528)[:n_layers * 128]
        m["lnp"] = np.ascontiguousarray(lnp)
        hp = np.concatenate([inp["ret_decay_logit"][:, :, j].reshape(-1), inp["dn_a_log"][:, :, j].reshape(-1),
                             inp["dn_dt_bias"][:, :, j].reshape(-1)]).astype(f32)
        m["hp"] = np.ascontiguousarray(np.broadcast_to(hp[None, :], (128, 24)))
        cw = inp["dn_conv_w"]
        cwj = np.stack([cw[:, :, s_ * 512 + j * 128: s_ * 512 + (j + 1) * 128] for s_ in range(3)], 1)
        m["convw"] = np.ascontiguousarray(cwj.transpose(3, 0, 1, 2).reshape(128, 60))
        nw = np.stack([inp["dn_norm_w"], inp["att_qn_w"], inp["att_kn_w"]], 1).reshape(-1)
        m["nw"] = np.ascontiguousarray(np.broadcast_to(nw[None, :], (128, NL * 3 * 128)))
        maps.append(m)
    return maps


_CACHE = {}


def kernel(**inputs):
    inp = {k: np.asarray(v) for k, v in inputs.items()}
    if "nc" not in _CACHE:
        _CACHE["nc"] = Builder().build()
    nc = _CACHE["nc"]
    maps = make_in_maps(inp)
    res = run_bass_kernel_spmd(nc, maps, core_ids=list(range(NCORE)))
    out = np.empty((2, L, D), np.float32)
    for r in range(NCORE):
        b, j = r // 4, r % 4
        out[b, 2048 * j:2048 * (j + 1)] = res.results[r]["out"]
    return out
```

```python
import contextlib
import math
import numpy as np
import ml_dtypes
import concourse.bass as bass
import concourse.mybir as mybir
from concourse.bass_utils import run_bass_kernel_spmd

F32 = mybir.dt.float32
BF16 = mybir.dt.bfloat16
I32 = mybir.dt.int32
AF = mybir.ActivationFunctionType
ALU = mybir.AluOpType

NCORE = 8
NL = 4
D = 2048
L = 8192
LC = 256
T = L + LC
TS = 2112
DFF = 5632
KC = 16
PW = 1168
ALPHA = (2 * NL) ** 0.25
EPS = 1e-6
GROUPS4 = [[0, 1, 2, 3], [4, 5, 6, 7]]
GROUPS8 = [[0, 1, 2, 3, 4, 5, 6, 7]]


class Trk:
    __slots__ = ("name", "w", "r", "chan")

    def __init__(self, name=""):
        self.name = name
        self.w = None
        self.r = {}
        self.chan = None


class Chan:
    __slots__ = ("sem", "cnt")

    def __init__(self, sem):
        self.sem = sem
        self.cnt = 0


class Ctx:
    ENG = ("pe", "act", "dve", "pool", "sp")

    def __init__(self, nc, stack):
        self.nc = nc
        self.stack = stack
        self.sem = {e: stack.enter_context(nc.semaphore("s_" + e)) for e in self.ENG}
        self.cnt = {e: 0 for e in self.ENG}
        self.seen = {e: {} for e in self.ENG}
        self.prog = {e: [] for e in self.ENG}
        self.chans = []
        self.free_chans = []
        self.phase_chans = []
        self.gstack = stack
        self.uid = 0
        self.same_engine_sync = True

    def sbuf(self, name, shape, dt):
        self.uid += 1
        t = self.stack.enter_context(self.nc.sbuf_tensor(f"{name}_{self.uid}", shape, dt))
        return t, Trk(name)

    def psum(self, name, shape, dt=F32):
        self.uid += 1
        t = self.stack.enter_context(self.nc.psum_tensor(f"{name}_{self.uid}", shape, dt))
        return t, Trk(name)

    def new_chan(self, name):
        if self.free_chans:
            c = self.free_chans.pop()
        else:
            self.uid += 1
            c = Chan(self.gstack.enter_context(self.nc.semaphore(f"c_{name}_{self.uid}")))
            self.chans.append(c)
        self.phase_chans.append(c)
        return c

    def release_phase_chans(self):
        self.free_chans.extend(self.phase_chans)
        self.phase_chans = []

    def _deps(self, E, reads, writes, extra=()):
        deps = {}

        def add(d):
            if d is None:
                return
            k = d[0]
            if k not in deps or deps[k][2] < d[2]:
                deps[k] = d

        for t in reads:
            add(t.w)
        for t in writes:
            add(t.w)
            for d in t.r.values():
                add(d)
        for d in extra:
            add(d)
        out = []
        for k, (kk, s, v) in deps.items():
            if k == E and (E == "pe" or not self.same_engine_sync):
                continue
            if self.seen[E].get(k, 0) >= v:
                continue
            self.seen[E][k] = v
            out.append((s, v))
        return out

    def op(self, E, fn, reads=(), writes=(), defer=False):
        for s, v in self._deps(E, reads, writes):
            self.prog[E].append(("wait", s, v))
        if defer:
            assert E == "pe"
            me = (E, self.sem[E], self.cnt[E] + 1)
            self.prog[E].append(("opq", fn))
        else:
            self.cnt[E] += 1
            me = (E, self.sem[E], self.cnt[E])
            self.prog[E].append(("op", fn, self.sem[E]))
        for t in writes:
            t.w = me
            t.r = {}
        for t in reads:
            if t not in writes:
                t.r[E] = me

    def _chan_for(self, reads, writes, chan):
        if chan is not None:
            return chan
        for t in list(writes) + list(reads):
            if t.chan is not None:
                return t.chan
        t = (list(writes) + list(reads))[0]
        t.chan = self.new_chan(t.name or "x")
        return t.chan

    def _async(self, Q, item_fn, inc, reads, writes, chan, serialize=True):
        chan = self._chan_for(reads, writes, chan)
        key = ("c", id(chan))
        extra = [(key, chan.sem, chan.cnt)] if (chan.cnt > 0 and serialize) else []
        for s, v in self._deps(Q, reads, writes, extra):
            self.prog[Q].append(("wait", s, v))
        chan.cnt += inc
        me = (key, chan.sem, chan.cnt)
        self.prog[Q].append(item_fn(chan.sem))
        for t in writes:
            t.w = me
            t.r = {}
        for t in reads:
            if t not in writes:
                t.r[key] = me

    def dma(self, Q, out, in_, reads=(), writes=(), chan=None):
        self._async(Q, lambda sem: ("dma", out, in_, sem), 16, reads, writes, chan)

    def cc(self, kind, groups, in_ap, out_ap, reads=(), writes=(), op=None, chan=None):
        if getattr(self, "no_cc", False):
            return
        if chan is None:
            chan = self.new_chan("cc")
        self._async("pool", lambda sem: ("cc", kind, groups, in_ap, out_ap, sem, op), 1, reads, writes, chan, serialize=False)

    def barrier(self):
        for E in self.ENG:
            for E2 in self.ENG:
                if E2 != E and self.cnt[E2] > self.seen[E].get(E2, 0):
                    self.prog[E].append(("wait", self.sem[E2], self.cnt[E2]))
                    self.seen[E][E2] = self.cnt[E2]
            for ch in self.chans:
                k = ("c", id(ch))
                if ch.cnt > self.seen[E].get(k, 0):
                    self.prog[E].append(("wait", ch.sem, ch.cnt))
                    self.seen[E][k] = ch.cnt
        self.release_phase_chans()

    def finish(self):
        for ch in self.chans:
            k = ("c", id(ch))
            if ch.cnt > self.seen["sp"].get(k, 0):
                self.prog["sp"].append(("wait", ch.sem, ch.cnt))

    def emit(self):
        nc = self.nc
        with nc.Block() as block:
            def make(E):
                def body(eng):
                    for item in self.prog[E]:
                        kind = item[0]
                        if kind == "wait":
                            eng.wait_ge(item[1], item[2])
                        elif kind == "op":
                            item[1](eng).then_inc(item[2], 1)
                        elif kind == "opq":
                            item[1](eng)
                        elif kind == "cc":
                            _, k, groups, in_ap, out_ap, sem, ccop = item
                            eng.collective_compute(k, ccop if ccop is not None else ALU.bypass,
                                                   replica_groups=groups, ins=[in_ap], outs=[out_ap]).then_inc(sem, 1)
                        else:
                            _, out, in_, sem = item
                            eng.dma_start(out=out, in_=in_).then_inc(sem, 16)
                return body
            block.tensor(make("pe"))
            block.scalar(make("act"))
            block.vector(make("dve"))
            block.gpsimd(make("pool"))
            block.sync(make("sp"))


def dap(t, offset, dims):
    h = t.tensor if hasattr(t, "tensor") else t
    return bass.AP(tensor=h, offset=offset, ap=[[int(s), int(n)] for s, n in dims])


class Builder:
    def __init__(self, n_layers=NL, debug=None):
        self.n_layers = n_layers
        self.debug = debug or {}
        self.nc = bass.Bass("TRN2", target_bir_lowering=False)
        self.dbg_outs = []

    def din(self, name, shape, dt=F32):
        return self.nc.dram_tensor(name, list(shape), dt, kind="ExternalInput").ap()

    def dout(self, name, shape, dt=F32):
        return self.nc.dram_tensor(name, list(shape), dt, kind="ExternalOutput").ap()

    def dscr(self, name, shape, dt):
        return self.nc.dram_tensor(name, list(shape), dt).ap()

    def declare(self):
        s = self
        NLW = s.n_layers
        if s.debug.get("mixtest"):
            s.PT_in = s.din("PT_in", [T, PW])
            s.PF_in = s.din("PF_in", [384, T])
        else:
            s.x_own = s.din("x_own", [TS, D])
            s.cT = s.din("cT", [128, 32])
            s.w_ada_t = s.din("w_ada_t", [NLW * 24 * 128, 2048])
            s.b_ada_s = s.din("b_ada_s", [128, 96])
            s.w_in_sel = s.din("w_in_sel", [NLW * 128, 13 * 2048])
            s.w_o_s = s.din("w_o_s", [NLW * 128, 8192])
            s.w_f1_s = s.din("w_f1_s", [NLW * 128, 45056])
            s.w_f2_s = s.din("w_f2_s", [NLW * 128, 22528])
        s.selb = s.din("selb", [128, 2])
        s.sel4 = s.din("sel4", [128, 4])
        s.lnp = s.din("lnp", [128, 256])
        s.hp = s.din("hp", [128, 24])
        s.convw = s.din("convw", [128, 60])
        s.nw = s.din("nw", [128, NL * 3 * 128])
        s.out = s.dout("out", [L // 4, D])
        s.Wsel = [s.dscr(f"Wsel{l}", [128, 13 * 2048], BF16) for l in range(NL)]
        s.wpo = [s.dscr(f"wpo{l}", [128, 8192], BF16) for l in range(NL)]
        s.wpf1 = [s.dscr(f"wpf1{l}", [128, 45056], BF16) for l in range(NL)]
        s.wpf2 = [s.dscr(f"wpf2{l}", [128, 22528], BF16) for l in range(NL)]
        s.Wo = [s.dscr(f"Wo{l}", [2048, 2048], BF16) for l in range(NL)]
        s.Wf1 = [s.dscr(f"Wf1{l}", [DFF, 4096], BF16) for l in range(NL)]
        s.Wf2 = [s.dscr(f"Wf2{l}", [2048, DFF], BF16) for l in range(NL)]
        s.mpart = s.dscr("mpart", [128, 192], F32)
        s.MG = s.dscr("MG", [512, 192], F32)
        s.COS = s.dscr("COS", [L, 64], F32)
        s.SIN = s.dscr("SIN", [L, 64], F32)
        s.XT = s.dscr("XT", [D, TS], F32)
        s.hpart = s.dscr("hpart", [D, TS], BF16)
        s.HG = s.dscr("HG", [4 * D, TS], BF16)
        s.PT = s.dscr("PT", [T, PW], F32)
        s.PF = s.dscr("PF", [3 * 128, T], F32)
        s.ypad = s.dscr("ypad", [16 * 512, TS], BF16)
        s.yown = s.dscr("yown", [D, TS], BF16)
        s.t_XT = Trk("XT"); s.t_hpart = Trk("hpart"); s.t_HG = Trk("HG")
        s.t_PT = Trk("PT"); s.t_PF = Trk("PF"); s.t_ypad = Trk("ypad"); s.t_yown = Trk("yown")
        s.t_COS = Trk("COS")
        s.t_W = [Trk(f"W{l}") for l in range(NL)]
        s.t_Wsel = [Trk(f"Wsel{l}") for l in range(NL)]

    def add_dbg(self, name, src_ap, shape, reads, dt=F32):
        if self.debug.get("no_dbg"):
            return
        o = self.dout(name, shape, dt)
        self.dbg_outs.append(name)
        self.c.dma("sp", o, src_ap, reads=reads, writes=[Trk(name)])

    def ps(self):
        i = self.ps_i
        self.ps_i = (i + 1) % 6
        return self.psb[i]

    def pst(self):
        i = self.psT_i
        self.psT_i = (i + 1) % 2
        return self.psT[i]

    def build(self):
        s = self
        nc = s.nc
        s.declare()
        with contextlib.ExitStack() as st:
            c = s.c = Ctx(nc, st)
            c.no_cc = bool(s.debug.get("no_cc"))
            s.psb = [c.psum(f"psb{i}", [128, 512]) for i in range(6)]
            s.ps_i = 0
            s.psT = [c.psum(f"psT{i}", [128, 1024], BF16) for i in range(2)]
            s.psT_i = 0
            with contextlib.ExitStack() as st_const:
                c.stack = st_const
                s.consts()
                if s.debug.get("mixtest"):
                    with contextlib.ExitStack() as stp:
                        c.stack = stp
                        s.setup_rope()
                        tcp = [Trk("cp0"), Trk("cp1"), Trk("cp2"), Trk("cp3")]
                        for n in range(66):
                            c.dma("sp", s.PT[n * 128:(n + 1) * 128, :], s.PT_in[n * 128:(n + 1) * 128, :], writes=[tcp[n % 4]])
                        for q in range(3):
                            for k in range(4):
                                c.dma("sp", s.PF[q * 128:(q + 1) * 128, k * 2112:(k + 1) * 2112], s.PF_in[q * 128:(q + 1) * 128, k * 2112:(k + 1) * 2112], writes=[tcp[k]])
                        c.barrier()
                else:
                    s.setup_phase()
                for l in range(s.n_layers):
                    s.layer(l)
                    if s.debug.get("stop_after"):
                        break
                if not s.debug.get("stop_after"):
                    s.final_phase()
                c.barrier()
            c.finish()
            c.emit()
        return nc

    def consts(self):
        s, c = self, self.c
        s.ones_f, s.t_ones_f = c.sbuf("ones_f", [128, 128], F32)
        c.op("pool", lambda e: e.memset(s.ones_f[:], 1.0), writes=[s.t_ones_f])
        s.ident_f, s.t_ident_f = c.sbuf("ident_f", [128, 128], F32)
        c.op("pool", lambda e: e.affine_select(out=s.ident_f[:], in_=s.ones_f[:], pattern=[[-1, 128]],
                                               compare_op=ALU.is_equal, fill=0.0, base=0, channel_multiplier=1),
             reads=[s.t_ones_f], writes=[s.t_ident_f])
        s.ident_b, s.t_ident_b = c.sbuf("ident_b", [128, 128], BF16)
        c.op("dve", lambda e: e.tensor_copy(out=s.ident_b[:], in_=s.ident_f[:]), reads=[s.t_ident_f], writes=[s.t_ident_b])
        s.ones_b, s.t_ones_b = c.sbuf("ones_b", [128, 128], BF16)
        c.op("dve", lambda e: e.tensor_copy(out=s.ones_b[:], in_=s.ones_f[:]), reads=[s.t_ones_f], writes=[s.t_ones_b])
        s.lnp_s, s.t_lnp = c.sbuf("lnp", [128, 256], F32)
        c.dma("sp", s.lnp_s[:], s.lnp, writes=[s.t_lnp])
        s.hp_s, s.t_hp = c.sbuf("hp", [128, 24], F32)
        c.dma("sp", s.hp_s[:], s.hp, writes=[s.t_hp])
        s.convw_s, s.t_convw = c.sbuf("convw", [128, 60], F32)
        c.dma("sp", s.convw_s[:], s.convw, writes=[s.t_convw])
        s.nw_s, s.t_nw = c.sbuf("nw", [128, NL * 3 * 128], F32)
        c.dma("sp", s.nw_s[:], s.nw, writes=[s.t_nw])
        s.selb_s, s.t_selb = c.sbuf("selb", [128, 2], F32)
        c.dma("sp", s.selb_s[:], s.selb, writes=[s.t_selb])
        s.sel4_s, s.t_sel4 = c.sbuf("sel4", [128, 4], F32)
        c.dma("sp", s.sel4_s[:], s.sel4, writes=[s.t_sel4])
        s.eps_t, s.t_eps = c.sbuf("eps_t", [128, 1], F32)
        c.op("pool", lambda e: e.memset(s.eps_t[:], EPS), writes=[s.t_eps])
        s.one_t, _ = c.sbuf("one_t", [128, 1], F32)
        c.op("pool", lambda e: e.memset(s.one_t[:], 1.0), writes=[s.t_eps])
        s.MM, s.t_MM = c.sbuf("MM", [128, NL, 96, 2], F32)

    def setup_phase(self):
        s, c = self, self.c
        with contextlib.ExitStack() as stp:
            c.stack = stp
            s.setup_weights()
            s.setup_ada()
            s.setup_rope()
            s.setup_xt()
            c.barrier()

    def convert(self, src, dst, X, reads_dst_trk):
        s, c = self, self.c
        CH = 4096
        for i, c0 in enumerate(range(0, X, CH)):
            n = min(CH, X - c0)
            sl = s.cv_i % 2
            s.cv_i += 1
            (a, ta), (b, tb) = s.cv_a[sl], s.cv_b[sl]
            c.dma("sp", a[:, 0:n], src[:, c0:c0 + n], writes=[ta])
            eng = ("dve", "act", "pool")[s.cv_i % 3]
            if eng == "act":
                c.op("act", lambda e, a=a, b=b, n=n: e.activation(out=b[:, 0:n], in_=a[:, 0:n], func=AF.Copy), reads=[ta], writes=[tb])
            else:
                c.op(eng, lambda e, a=a, b=b, n=n: e.tensor_copy(out=b[:, 0:n], in_=a[:, 0:n]), reads=[ta], writes=[tb])
            c.dma("sp", dst[:, c0:c0 + n], b[:, 0:n], reads=[tb], writes=[reads_dst_trk])

    def setup_weights(self):
        s, c = self, self.c
        s.cv_a = [c.sbuf(f"cva{i}", [128, 4096], F32) for i in range(2)]
        s.cv_b = [c.sbuf(f"cvb{i}", [128, 4096], BF16) for i in range(2)]
        s.cv_i = 0
        s.ch_w = c.new_chan("ccw")
        for l in range(s.n_layers):
            tp = [Trk(f"wpo{l}"), Trk(f"wpf1{l}"), Trk(f"wpf2{l}")]
            s.convert(s.w_in_sel[l * 128:(l + 1) * 128, :], s.Wsel[l], 13 * 2048, s.t_Wsel[l])
            s.convert(s.w_o_s[l * 128:(l + 1) * 128, :], s.wpo[l], 8192, tp[0])
            s.convert(s.w_f1_s[l * 128:(l + 1) * 128, :], s.wpf1[l], 45056, tp[1])
            s.convert(s.w_f2_s[l * 128:(l + 1) * 128, :], s.wpf2[l], 22528, tp[2])
            for (part, full, C, rb, nch, tpi) in ((s.wpo[l], s.Wo[l], 2048, 128, 4, tp[0]), (s.wpf1[l], s.Wf1[l], 4096, 128, 11, tp[1]),
                                                  (s.wpf2[l], s.Wf2[l], DFF, 64, 8, tp[2])):
                for ch in range(nch):
                    c.cc("AllGather", GROUPS4, dap(part, ch * rb * C, [(C, rb), (1, C)]), full[ch * 4 * rb:(ch + 1) * 4 * rb, :],
                         reads=[tpi], writes=[s.t_W[l]], chan=s.ch_w)

    def setup_ada(self):
        s, c = self, self.c
        cond, t_cond = c.sbuf("cond", [128, 32], F32)
        c.dma("sp", cond[:], s.cT, writes=[t_cond])
        c.op("act", lambda e: e.activation(out=cond[:], in_=cond[:], func=AF.Silu), reads=[t_cond], writes=[t_cond])
        bada, t_bada = c.sbuf("bada", [128, 96], F32)
        c.dma("sp", bada[:], s.b_ada_s, writes=[t_bada])
        mp, t_mp = c.sbuf("mp", [128, 192], F32)
        c.op("pool", lambda e: e.memset(mp[:], 0.0), writes=[t_mp])
        wt = [c.sbuf(f"adaw{i}", [128, 2048], F32) for i in range(2)]
        for lt in range(s.n_layers * 24):
            w, tw = wt[lt % 2]
            c.dma("sp", w[:], s.w_ada_t[lt * 128:(lt + 1) * 128, :], writes=[tw])
            ps, tps = s.ps()
            for kc in range(KC):
                c.op("pe", lambda e, w=w, ps=ps, kc=kc: e.matmul(ps[:, 0:2], lhsT=w[:, kc * 128:(kc + 1) * 128],
                                                                rhs=cond[:, kc * 2:(kc + 1) * 2], start=(kc == 0), stop=(kc == KC - 1)),
                     reads=[tw, t_cond], writes=[tps])
            c.op("dve", lambda e, ps=ps, lt=lt: e.tensor_scalar(out=mp[:, lt * 2:(lt + 1) * 2], in0=ps[:, 0:2],
                                                               scalar1=bada[:, lt:lt + 1], scalar2=None, op0=ALU.add),
                 reads=[tps, t_bada], writes=[t_mp])
        t_mpart = Trk("mpart"); t_MG = Trk("MG")
        c.dma("sp", s.mpart, mp[:], reads=[t_mp], writes=[t_mpart])
        c.cc("AllGather", GROUPS4, s.mpart, s.MG, reads=[t_mpart], writes=[t_MG])
        Mg, t_Mg = c.sbuf("Mg", [128, 4, 192], F32)
        c.dma("sp", Mg[:], s.MG.rearrange("(r p) x -> p r x", p=128), reads=[t_MG], writes=[t_Mg])
        Mv = Mg[:].rearrange("p r (l t i) -> p r l t i", l=NL, t=24, i=2)
        for l in range(s.n_layers):
            dst = s.MM[:, l, :, :].rearrange("p (r t) w -> p r t w", r=4, t=24)
            for w_ in range(2):
                c.op("dve", lambda e, l=l, dst=dst, w_=w_: e.tensor_copy(out=dst[:, :, :, w_], in_=Mv[:, :, l, :, w_]), reads=[t_Mg], writes=[s.t_MM])
            for comp in (1, 4):
                v = s.MM[:, l, comp * 16:(comp + 1) * 16, :]
                c.op("dve", lambda e, v=v: e.tensor_scalar(out=v, in0=v, scalar1=1.0, scalar2=None, op0=ALU.add), reads=[], writes=[s.t_MM])
            for comp in (2, 5):
                v = s.MM[:, l, comp * 16:(comp + 1) * 16, :]
                c.op("dve", lambda e, v=v: e.tensor_scalar(out=v, in0=v, scalar1=1.0 / ALPHA, scalar2=None, op0=ALU.mult), reads=[], writes=[s.t_MM])

    def setup_rope(self):
        s, c = self, self.c
        io, t_io = c.sbuf("io", [128, 64], I32)
        c.op("pool", lambda e: e.iota(io[:], pattern=[[128, 64]], base=0, channel_multiplier=1), writes=[t_io])
        ri, t_ri = c.sbuf("ri", [128, 64], I32)
        ci, t_ci = c.sbuf("ci", [128, 64], I32)
        c.op("dve", lambda e: e.tensor_single_scalar(out=ri[:], in_=io[:], scalar=6, op=ALU.arith_shift_right), reads=[t_io], writes=[t_ri])
        c.op("dve", lambda e: e.tensor_single_scalar(out=ci[:], in_=io[:], scalar=63, op=ALU.bitwise_and), reads=[t_io], writes=[t_ci])
        rf, t_rf = c.sbuf("rf", [128, 64], F32)
        cf, t_cf = c.sbuf("cf", [128, 64], F32)
        c.op("dve", lambda e: e.tensor_copy(out=rf[:], in_=ri[:]), reads=[t_ri], writes=[t_rf])
        c.op("dve", lambda e: e.tensor_copy(out=cf[:], in_=ci[:]), reads=[t_ci], writes=[t_cf])
        ii, t_ii = c.sbuf("ii", [128, 32], I32)
        c.op("pool", lambda e: e.iota(ii[:], pattern=[[1, 32]], base=0, channel_multiplier=0), writes=[t_ii])
        inv, t_inv = c.sbuf("inv", [128, 32], F32)
        c.op("dve", lambda e: e.tensor_copy(out=inv[:], in_=ii[:]), reads=[t_ii], writes=[t_inv])
        c.op("act", lambda e: e.activation(out=inv[:], in_=inv[:], func=AF.Exp, scale=-math.log(10000.0) / 32.0), reads=[t_inv], writes=[t_inv])
        ang, t_ang = c.sbuf("ang", [128, 64, 64], F32)
        invb = inv[:].unsqueeze(1).broadcast_to([128, 64, 32])
        c.op("dve", lambda e: e.tensor_tensor(out=ang[:, :, 0:32], in0=rf[:].unsqueeze(2).broadcast_to([128, 64, 32]), in1=invb, op=ALU.mult),
             reads=[t_rf, t_inv], writes=[t_ang])
        c.op("dve", lambda e: e.tensor_tensor(out=ang[:, :, 32:64], in0=cf[:].unsqueeze(2).broadcast_to([128, 64, 32]), in1=invb, op=ALU.mult),
             reads=[t_cf, t_inv], writes=[t_ang])
        ki, t_ki = c.sbuf("ki", [128, 64, 64], I32)
        kf, t_kf = c.sbuf("kf", [128, 64, 64], F32)
        sn, t_sn = c.sbuf("sn", [128, 64, 64], F32)
        cs, t_cs = c.sbuf("cs", [128, 64, 64], F32)
        for (dst, t_dst, shift) in ((sn, t_sn, 0.0), (cs, t_cs, math.pi / 2)):
            c.op("dve", lambda e, dst=dst, shift=shift: e.tensor_scalar(out=dst[:], in0=ang[:], scalar1=shift, scalar2=None, op0=ALU.add),
                 reads=[t_ang], writes=[t_dst])
            c.op("dve", lambda e, dst=dst: e.tensor_scalar(out=ki[:], in0=dst[:], scalar1=1.0 / (2 * math.pi), scalar2=None, op0=ALU.mult),
                 reads=[t_dst], writes=[t_ki])
            c.op("dve", lambda e: e.tensor_copy(out=kf[:], in_=ki[:]), reads=[t_ki], writes=[t_kf])
            c.op("dve", lambda e, dst=dst: e.scalar_tensor_tensor(out=dst[:], in0=kf[:], scalar=-2 * math.pi, in1=dst[:], op0=ALU.mult, op1=ALU.add),
                 reads=[t_kf], writes=[t_dst])
            c.op("dve", lambda e, dst=dst: e.tensor_scalar(out=dst[:], in0=dst[:], scalar1=math.pi, scalar2=-math.pi, op0=ALU.min, op1=ALU.max),
                 reads=[], writes=[t_dst])
            c.op("act", lambda e, dst=dst: e.activation(out=dst[:], in_=dst[:], func=AF.Sin), reads=[], writes=[t_dst])
        c.dma("sp", s.SIN.rearrange("(n p) d -> p n d", p=128), sn[:], reads=[t_sn], writes=[s.t_COS])
        c.dma("sp", s.COS.rearrange("(n p) d -> p n d", p=128), cs[:], reads=[t_cs], writes=[s.t_COS])

    def setup_xt(self):
        s, c = self, self.c
        xin = [c.sbuf(f"xin{i}", [128, D], F32) for i in range(2)]
        xo = [c.sbuf(f"xo{i}", [128, KC, 128], F32) for i in range(2)]
        tiles = [(0, 64)] + [(64 + 128 * i, 128) for i in range(16)]
        for ti, (r0, m) in enumerate(tiles):
            a, ta = xin[ti % 2]
            o, to = xo[ti % 2]
            c.dma("sp", a[0:m, :], s.x_own[r0:r0 + m, :], writes=[ta])
            for g in range(4):
                ps, tps = s.ps()
                for q in range(4):
                    kc = g * 4 + q
                    c.op("pe", lambda e, a=a, ps=ps, kc=kc, q=q, m=m: e.transpose(out=ps[:, q * 128:q * 128 + m], in_=a[0:m, kc * 128:(kc + 1) * 128],
                                                                                  identity=s.ident_f[0:m, 0:m]),
                         reads=[ta, s.t_ident_f], writes=[tps])
                eng = "act" if g % 2 else "dve"
                src = ps[:, :].rearrange("p (q t) -> p q t", q=4)[:, :, 0:m]
                if eng == "act":
                    c.op("act", lambda e, o=o, src=src, g=g, m=m: e.activation(out=o[:, g * 4:(g + 1) * 4, 0:m], in_=src, func=AF.Copy), reads=[tps], writes=[to])
                else:
                    c.op("dve", lambda e, o=o, src=src, g=g, m=m: e.tensor_copy(out=o[:, g * 4:(g + 1) * 4, 0:m], in_=src), reads=[tps], writes=[to])
            c.dma("sp", s.XT.rearrange("(kc p) t -> p kc t", p=128)[:, :, r0:r0 + m], o[:, :, 0:m], reads=[to], writes=[s.t_XT])

    def layer(self, l):
        s, c = self, self.c
        if not s.debug.get("mixtest"):
            with contextlib.ExitStack() as stp:
                c.stack = stp
                s.wsel, s.t_wsel = c.sbuf("wsel", [128, 13, 2048], BF16)
                c.dma("sp", s.wsel[:], s.Wsel[l].rearrange("p (t c) -> p t c", t=13), reads=[s.t_Wsel[l]], writes=[s.t_wsel])
                s.phase_a0(l)
                s.phase_a1(l)
                c.barrier()
        if self.debug.get("stop_after") == "a1":
            self.dump_a1(l)
            return
        for ph in (s.phase_att, s.phase_ret, s.phase_dn):
            if ph.__name__ in self.debug.get("skip", ()):
                continue
            with contextlib.ExitStack() as stp:
                c.stack = stp
                ph(l)
                c.barrier()
        if s.debug.get("zero_ypad"):
            with contextlib.ExitStack() as stp:
                c.stack = stp
                zt_, tzt_ = c.sbuf("zpad", [128, TS], BF16)
                c.op("dve", lambda e: e.memset(zt_[:], 0.0), writes=[tzt_])
                for i in range(64):
                    c.dma("sp", s.ypad[i * 128:(i + 1) * 128, :], zt_[:], reads=[tzt_], writes=[s.t_ypad])
                c.barrier()
        if not s.debug.get("no_rs"):
            c.cc("ReduceScatter", GROUPS4, s.ypad, s.yown, reads=[s.t_ypad], writes=[s.t_yown], op=ALU.add)
        if self.debug.get("stop_after") == "mix":
            s.add_dbg("d_yown", s.yown, [D, TS], [s.t_yown], dt=BF16)
            return
        with contextlib.ExitStack() as stp:
            c.stack = stp
            s.phase_c(l)
            c.barrier()
        if self.debug.get("stop_after") == "c":
            s.add_dbg("d_yown", s.yown, [D, TS], [s.t_yown], dt=BF16)
            s.add_dbg("d_XT", s.XT, [D, TS], [s.t_XT])
            return

    def own_tiles(self):
        return [(0, 64, 1)] + [(64 + 512 * i, 512, 0) for i in range(4)]

    def phase_a0(self, l):
        s, c = self, self.c
        xt = [c.sbuf(f"a0x{i}", [128, TS], F32) for i in range(2)]
        hb = [c.sbuf(f"a0h{i}", [128, TS], BF16) for i in range(2)]
        ch_h = c.new_chan("cch")
        t_hp = [Trk(f"hpart{kc}") for kc in range(KC)]
        for kc in range(KC):
            a, ta = xt[kc % 2]
            h, th = hb[kc % 2]
            c.dma("sp", a[:], s.XT[kc * 128:(kc + 1) * 128, :], reads=[s.t_XT], writes=[ta])
            c.op("act", lambda e, a=a, h=h, kc=kc: e.activation(out=h[:, 0:64], in_=a[:, 0:64], func=AF.Identity,
                                                                scale=s.MM[:, l, 16 + kc, 1:2], bias=s.MM[:, l, kc, 1:2]),
                 reads=[ta, s.t_MM], writes=[th])
            c.op("dve", lambda e, a=a, h=h, kc=kc: e.tensor_scalar(out=h[:, 64:TS], in0=a[:, 64:TS], scalar1=s.MM[:, l, 16 + kc, 0:1],
                                                                   scalar2=s.MM[:, l, kc, 0:1], op0=ALU.mult, op1=ALU.add),
                 reads=[ta, s.t_MM], writes=[th])
            c.dma("sp", s.hpart[kc * 128:(kc + 1) * 128, :], h[:], reads=[th], writes=[t_hp[kc]])
            c.cc("AllGather", GROUPS4, s.hpart[kc * 128:(kc + 1) * 128, :], s.HG[kc * 512:(kc + 1) * 512, :],
                 reads=[t_hp[kc]], writes=[s.t_HG], chan=ch_h)

    def phase_a1(self, l):
        s, c = self, self.c
        wsel, t_wsel = s.wsel, s.t_wsel
        ht = [c.sbuf(f"a1h{i}", [128, KC, 512], BF16) for i in range(2)]
        stg = [c.sbuf(f"a1s{i}", [128, PW], F32) for i in range(2)]
        stf = [c.sbuf(f"a1f{i}", [128, 512], F32) for i in range(2)]
        HGv = s.HG.rearrange("(kc a p) t -> a p kc t", a=4, kc=KC, p=128)
        PFv = s.PF.rearrange("(s p) t -> s p t", p=128)
        it = 0
        si = 0
        fi = 0
        for js in range(4):
            for (t0, n, w) in s.own_tiles():
                tok0 = (64 * js) if w == 1 else (LC + 2048 * js + (t0 - 64))
                h, th = ht[it % 2]
                it += 1
                c.dma("sp", h[:, :, 0:n], HGv[js][:, :, t0:t0 + n], reads=[s.t_HG], writes=[th])
                for sub in range((n + 127) // 128):
                    m = min(128, n - sub * 128)
                    sg, tsg = stg[si % 2]
                    si += 1
                    for gi, (tl0, ntl, c0, ncols) in enumerate([(0, 4, 0, 512), (4, 4, 512, 512), (8, 2, 1024, 144)]):
                        ps, tps = s.ps()
                        for kc in range(KC):
                            c.op("pe", lambda e, ps=ps, h=h, kc=kc, sub=sub, m=m, tl0=tl0, ntl=ntl: e.matmul(
                                ps[0:m, 0:ntl * 128], lhsT=h[:, kc, sub * 128:sub * 128 + m],
                                rhs=wsel[:, tl0:tl0 + ntl, kc * 128:(kc + 1) * 128], start=(kc == 0), stop=(kc == KC - 1)),
                                 reads=[th, t_wsel], writes=[tps], defer=(kc != KC - 1))
                        if gi == 1:
                            c.op("act", lambda e, ps=ps, sg=sg, m=m, c0=c0, ncols=ncols: e.activation(out=sg[0:m, c0:c0 + ncols], in_=ps[0:m, 0:ncols], func=AF.Copy),
                                 reads=[tps], writes=[tsg])
                        else:
                            c.op("dve", lambda e, ps=ps, sg=sg, m=m, c0=c0, ncols=ncols: e.tensor_copy(out=sg[0:m, c0:c0 + ncols], in_=ps[0:m, 0:ncols]),
                                 reads=[tps], writes=[tsg])
                    r0 = tok0 + sub * 128
                    c.dma("sp", s.PT[r0:r0 + m, :], sg[0:m, :], reads=[tsg], writes=[s.t_PT])
                for q in range(3):
                    ps, tps = s.ps()
                    for kc in range(KC):
                        c.op("pe", lambda e, ps=ps, h=h, kc=kc, q=q, n=n: e.matmul(
                            ps[:, 0:n], lhsT=wsel[:, 10 + q, kc * 128:(kc + 1) * 128], rhs=h[:, kc, 0:n],
                            start=(kc == 0), stop=(kc == KC - 1)), reads=[th, t_wsel], writes=[tps], defer=(kc != KC - 1))
                    sf, tsf = stf[fi % 2]
                    fi += 1
                    c.op("act", lambda e, ps=ps, sf=sf, n=n: e.activation(out=sf[:, 0:n], in_=ps[:, 0:n], func=AF.Copy), reads=[tps], writes=[tsf])
                    c.dma("sp", PFv[q][:, tok0:tok0 + n], sf[:, 0:n], reads=[tsf], writes=[s.t_PF])


    def scatter_y(self, ysb, t_ysb, sidx, q0, nq):
        s, c = self, self.c
        ysc, t_ysc = s.ysc[s.ysc_i % 2]
        s.ysc_i += 1
        for blk in range(4):
            eng = "dve" if blk % 2 == 0 else "pool"
            c.op(eng, lambda e, blk=blk: e.tensor_scalar(out=ysc[:, blk, 0:nq], in0=ysb, scalar1=s.sel4_s[:, blk:blk + 1], scalar2=None, op0=ALU.mult),
                 reads=[t_ysb, s.t_sel4], writes=[t_ysc])
        YP = s.ypad.rearrange("(jt blk s d) t -> jt s d blk t", jt=4, blk=4, s=4, d=128)
        pos = q0
        while pos < q0 + nq:
            if pos < LC:
                jt, col, room = pos // 64, pos % 64, 64 - pos % 64
            else:
                tl = pos - LC
                jt, col, room = tl // 2048, 64 + tl % 2048, 2048 - tl % 2048
            n = min(room, q0 + nq - pos)
            o = pos - q0
            c.dma("sp", YP[jt][sidx][:, :, col:col + n], ysc[:, :, o:o + n], reads=[t_ysc], writes=[s.t_ypad])
            pos += n

    def alloc_scatter(self):
        s, c = self, self.c
        s.ysc = [c.sbuf(f"ysc{i}", [128, 4, 512], BF16) for i in range(2)]
        s.ysc_i = 0
        s.sel4_b, s.t_sel4b = c.sbuf("sel4b", [128, 4], BF16)
        c.op("dve", lambda e: e.tensor_copy(out=s.sel4_b[:], in_=s.sel4_s[:]), reads=[s.t_sel4], writes=[s.t_sel4b])

    def rope_ops(self, xn, t_xn, xr, t_xr, cs, sn, t_cs, nh, tmp, t_tmp):
        c = self.c
        cb = cs.unsqueeze(1).broadcast_to([128, nh, 64])
        sb = sn.unsqueeze(1).broadcast_to([128, nh, 64])
        x1 = xn[:, 0:nh, 0:64]
        x2 = xn[:, 0:nh, 64:128]
        t1, t2 = tmp[:, 0, 0:nh, :], tmp[:, 1, 0:nh, :]
        c.op("dve", lambda e: e.tensor_tensor(out=t1, in0=x1, in1=cb, op=ALU.mult), reads=[t_xn, t_cs], writes=[t_tmp])
        c.op("dve", lambda e: e.tensor_tensor(out=t2, in0=x2, in1=sb, op=ALU.mult), reads=[t_xn, t_cs], writes=[t_tmp])
        c.op("dve", lambda e: e.tensor_tensor(out=xr[:, 0:nh, 0:64], in0=t1, in1=t2, op=ALU.subtract), reads=[t_tmp], writes=[t_xr])
        c.op("dve", lambda e: e.tensor_tensor(out=t1, in0=x1, in1=sb, op=ALU.mult), reads=[t_xn, t_cs], writes=[t_tmp])
        c.op("dve", lambda e: e.tensor_tensor(out=t2, in0=x2, in1=cb, op=ALU.mult), reads=[t_xn, t_cs], writes=[t_tmp])
        c.op("dve", lambda e: e.tensor_tensor(out=xr[:, 0:nh, 64:128], in0=t1, in1=t2, op=ALU.add), reads=[t_tmp], writes=[t_xr])

    def phase_att(self, l):
        s, c = self, self.c
        s.alloc_scatter()
        QT = [c.sbuf(f"QT{i}", [128, T], BF16) for i in range(2)]
        KT, t_KT = c.sbuf("KT", [128, T], BF16)
        V, t_V = c.sbuf("V", [128, 66, 128], BF16)
        A = [c.sbuf(f"attA{i}", [128, 512], F32) for i in range(2)]
        CS = [c.sbuf(f"attCS{i}", [128, 2, 64], F32) for i in range(2)]
        ss, t_ss = c.sbuf("att_ss", [128, 4], F32)
        junk, t_junk = c.sbuf("att_junk", [128, 128], F32)
        xn, t_xn = c.sbuf("att_xn", [128, 3, 128], F32)
        xr, t_xr = c.sbuf("att_xr", [128, 3, 128], BF16)
        tmp, t_tmp = c.sbuf("att_tmp", [128, 2, 3, 64], F32)
        nwq = s.nw_s[:, (l * 3 + 1) * 128:(l * 3 + 2) * 128]
        nwk = s.nw_s[:, (l * 3 + 2) * 128:(l * 3 + 3) * 128]
        for n in range(66):
            a, ta = A[n % 2]
            c.dma("sp", a[:], s.PT[n * 128:(n + 1) * 128, 512:1024], reads=[s.t_PT], writes=[ta])
            if n >= 2:
                cs, tcs = CS[n % 2]
                c.dma("sp", cs[:, 0, :], s.COS[(n - 2) * 128:(n - 1) * 128, :], reads=[s.t_COS], writes=[tcs])
                c.dma("sp", cs[:, 1, :], s.SIN[(n - 2) * 128:(n - 1) * 128, :], reads=[s.t_COS], writes=[tcs])
            sub = s.debug.get("sub", 9)
            if sub < 1:
                continue
            for i in range(3):
                c.op("act", lambda e, a=a, i=i: e.activation(out=junk[:], in_=a[:, i * 128:(i + 1) * 128], func=AF.Square, accum_out=ss[:, i:i + 1]),
                     reads=[ta], writes=[t_junk, t_ss])
            c.op("dve", lambda e: e.tensor_scalar(out=ss[:, 0:3], in0=ss[:, 0:3], scalar1=1.0 / 128, scalar2=EPS, op0=ALU.mult, op1=ALU.add), reads=[], writes=[t_ss])
            c.op("act", lambda e: e.activation(out=ss[:, 0:3], in_=ss[:, 0:3], func=AF.Sqrt), reads=[], writes=[t_ss])
            c.op("dve", lambda e: e.reciprocal(out=ss[:, 0:3], in_=ss[:, 0:3]), reads=[], writes=[t_ss])
            if sub < 2:
                continue
            for i in range(3):
                wv = nwq if i < 2 else nwk
                c.op("dve", lambda e, a=a, i=i, wv=wv: e.scalar_tensor_tensor(out=xn[:, i, :], in0=a[:, i * 128:(i + 1) * 128], scalar=ss[:, i:i + 1],
                                                                              in1=wv, op0=ALU.mult, op1=ALU.mult), reads=[ta, t_ss, s.t_nw], writes=[t_xn])
            if sub < 3:
                continue
            if n >= 2:
                s.rope_ops(xn, t_xn, xr, t_xr, cs[:, 0, :], cs[:, 1, :], tcs, 3, tmp, t_tmp)
            else:
                c.op("dve", lambda e: e.tensor_copy(out=xr[:], in_=xn[:]), reads=[t_xn], writes=[t_xr])
            if sub < 4:
                continue
            pT, tpT = s.pst()
            for i in range(3):
                c.op("pe", lambda e, pT=pT, i=i: e.transpose(out=pT[:, i * 128:(i + 1) * 128], in_=xr[:, i, :], identity=s.ident_b[:]),
                     reads=[t_xr, s.t_ident_b], writes=[tpT])
            for i, (dst, tdst) in enumerate([QT[0], QT[1], (KT, t_KT)]):
                eng = "dve"
                if eng == "act":
                    c.op("act", lambda e, pT=pT, i=i, dst=dst, n=n: e.activation(out=dst[:, n * 128:(n + 1) * 128], in_=pT[:, i * 128:(i + 1) * 128], func=AF.Copy),
                         reads=[tpT], writes=[tdst])
                else:
                    c.op("dve", lambda e, pT=pT, i=i, dst=dst, n=n: e.tensor_copy(out=dst[:, n * 128:(n + 1) * 128], in_=pT[:, i * 128:(i + 1) * 128]),
                         reads=[tpT], writes=[tdst])
            c.op("act", lambda e, a=a, n=n: e.activation(out=V[:, n, :], in_=a[:, 384:512], func=AF.Copy), reads=[ta], writes=[t_V])
        if s.debug.get("att_stage") == 1:
            return
        Pb = [c.sbuf(f"attP{i}", [128, 512], BF16) for i in range(3)]
        rden, t_rden = c.sbuf("att_rden", [128, 512], F32)
        ysb = [c.sbuf(f"att_y{i}", [128, 512], BF16) for i in range(2)]
        sc = 128.0 ** -0.5
        pi = 0
        gi = 0
        groups = [(0, 256, [0, 1])] + [(LC + 512 * g, 512, list(range(66))) for g in range(16)]
        for hh in range(2):
            q, tq = QT[hh]
            for (q0, nq, blocks) in groups:
                psO, tpsO = s.ps()
                psD, tpsD = s.ps()
                pend = []

                def emit_S(kb, psO=psO, psD=psD, q=q, tq=tq, q0=q0, nq=nq, pend=pend):
                    psS, tpsS = s.ps()
                    while psS is psO or psS is psD:
                        psS, tpsS = s.ps()
                    c.op("pe", lambda e, psS=psS, kb=kb: e.matmul(psS[:, 0:nq], lhsT=KT[:, kb * 128:(kb + 1) * 128], rhs=q[:, q0:q0 + nq],
                                                                 start=True, stop=True), reads=[t_KT, tq], writes=[tpsS])
                    pend.append((psS, tpsS))

                LOOK = 2
                for kb in blocks[:LOOK]:
                    emit_S(kb)
                for bi, kb in enumerate(blocks):
                    if bi + LOOK < len(blocks):
                        emit_S(blocks[bi + LOOK])
                    psS, tpsS = pend[bi]
                    pb, tpb = Pb[pi % 3]
                    pi += 1
                    c.op("act", lambda e, psS=psS, pb=pb, nq=nq: e.activation(out=pb[:, 0:nq], in_=psS[:, 0:nq], func=AF.Exp, scale=sc), reads=[tpsS], writes=[tpb])
                    first, last = bi == 0, bi == len(blocks) - 1
                    c.op("pe", lambda e, psO=psO, kb=kb, pb=pb, nq=nq, first=first, last=last: e.matmul(psO[:, 0:nq], lhsT=V[:, kb, :], rhs=pb[:, 0:nq], start=first, stop=last),
                         reads=[t_V, tpb], writes=[tpsO])
                    c.op("pe", lambda e, psD=psD, pb=pb, nq=nq, first=first, last=last: e.matmul(psD[:, 0:nq], lhsT=s.ones_b[:], rhs=pb[:, 0:nq], start=first, stop=last),
                         reads=[s.t_ones_b, tpb], writes=[tpsD])
                c.op("dve", lambda e, psD=psD, nq=nq: e.reciprocal(out=rden[:, 0:nq], in_=psD[:, 0:nq]), reads=[tpsD], writes=[t_rden])
                y, ty = ysb[gi % 2]
                gi += 1
                c.op("dve", lambda e, psO=psO, y=y, nq=nq: e.tensor_tensor(out=y[:, 0:nq], in0=psO[:, 0:nq], in1=rden[:, 0:nq], op=ALU.mult), reads=[tpsO, t_rden], writes=[ty])
                if s.debug.get("att_stage") != 2:
                    s.scatter_y(y[:, 0:nq], ty, 2 + hh, q0, nq)

    def phase_ret(self, l):
        s, c = self, self.c
        s.alloc_scatter()
        sc = 128.0 ** -0.5
        lg, t_lg = c.sbuf("r_lg", [128, 2], F32)
        c.op("act", lambda e: e.activation(out=lg[:], in_=s.hp_s[:, l * 2:l * 2 + 2], func=AF.Exp, scale=-1.0), reads=[s.t_hp], writes=[t_lg])
        c.op("act", lambda e: e.activation(out=lg[:], in_=lg[:], func=AF.Ln, bias=1.0), reads=[], writes=[t_lg])
        c.op("dve", lambda e: e.tensor_scalar(out=lg[:], in0=lg[:], scalar1=-1.0, scalar2=None, op0=ALU.mult), reads=[], writes=[t_lg])
        ii, t_ii = c.sbuf("r_ii", [128, 128], I32)
        fi, t_fi = c.sbuf("r_fi", [128, 128], F32)
        DT = [c.sbuf(f"r_DT{d}", [128, 128], F32) for d in range(2)]
        RQ = [c.sbuf(f"r_RQ{d}", [128, 128], F32) for d in range(2)]
        kd, t_kd = c.sbuf("r_kd", [128, 2], F32)
        g128, t_g128 = c.sbuf("r_g128", [128, 2], F32)

        def iota_f(pattern, base, cm, n):
            c.op("pool", lambda e: e.iota(ii[:, 0:n], pattern=pattern, base=base, channel_multiplier=cm), reads=[t_fi], writes=[t_ii])
            c.op("dve", lambda e: e.tensor_copy(out=fi[:, 0:n], in_=ii[:, 0:n]), reads=[t_ii], writes=[t_fi])

        for d in range(2):
            dt_, tdt = DT[d]
            if d == 0:
                iota_f([[1, 128]], 0, -1, 128)
            else:
                iota_f([[-1, 128]], 0, 1, 128)
            c.op("dve", lambda e: e.tensor_scalar(out=fi[:], in0=fi[:], scalar1=0.0, scalar2=None, op0=ALU.max), reads=[], writes=[t_fi])
            c.op("act", lambda e, d=d, dt_=dt_: e.activation(out=dt_[:], in_=fi[:], func=AF.Exp, scale=lg[:, d:d + 1]), reads=[t_fi, t_lg], writes=[tdt])
            c.op("dve", lambda e, dt_=dt_: e.tensor_scalar(out=dt_[:], in0=dt_[:], scalar1=sc, scalar2=None, op0=ALU.mult), reads=[], writes=[tdt])
            if d == 0:
                c.op("pool", lambda e, dt_=dt_: e.affine_select(out=dt_[:], in_=dt_[:], pattern=[[1, 128]], compare_op=ALU.is_ge, fill=0.0, base=0, channel_multiplier=-1),
                     reads=[], writes=[tdt])
            else:
                c.op("pool", lambda e, dt_=dt_: e.affine_select(out=dt_[:], in_=dt_[:], pattern=[[-1, 128]], compare_op=ALU.is_ge, fill=0.0, base=0, channel_multiplier=1),
                     reads=[], writes=[tdt])
            rq, trq = RQ[d]
            if d == 0:
                iota_f([[1, 128]], 1, 0, 128)
            else:
                iota_f([[-1, 128]], 128, 0, 128)
            c.op("act", lambda e, d=d, rq=rq: e.activation(out=rq[:], in_=fi[:], func=AF.Exp, scale=lg[:, d:d + 1]), reads=[t_fi, t_lg], writes=[trq])
            if d == 0:
                iota_f([[0, 1]], 127, -1, 1)
            else:
                iota_f([[0, 1]], 0, 1, 1)
            c.op("act", lambda e, d=d: e.activation(out=kd[:, d:d + 1], in_=fi[:, 0:1], func=AF.Exp, scale=lg[:, d:d + 1]), reads=[t_fi, t_lg], writes=[t_kd])
        c.op("dve", lambda e: e.tensor_scalar(out=kd[:], in0=kd[:], scalar1=sc, scalar2=None, op0=ALU.mult), reads=[], writes=[t_kd])
        c.op("act", lambda e: e.activation(out=g128[:], in_=lg[:], func=AF.Exp, scale=128.0), reads=[t_lg], writes=[t_g128])
        QTa, t_QTa = c.sbuf("r_QT", [128, 66, 128], BF16)
        KTa, t_KTa = c.sbuf("r_KT", [128, 66, 128], BF16)
        Ka, t_Ka = c.sbuf("r_K", [128, 66, 128], BF16)
        Va, t_Va = c.sbuf("r_V", [128, 66, 128], BF16)
        SG, t_SG = c.sbuf("r_SG", [128, 66, 128], F32)
        Of, t_Of = c.sbuf("r_Of", [128, 66, 128], F32)
        A = [c.sbuf(f"r_A{i}", [128, 512], F32) for i in range(2)]
        CS = [c.sbuf(f"r_CS{i}", [128, 2, 64], F32) for i in range(2)]
        xr, t_xr = c.sbuf("r_xr", [128, 2, 128], BF16)
        tmp, t_tmp = c.sbuf("r_tmp", [128, 2, 2, 64], F32)
        for n in range(66):
            a, ta = A[n % 2]
            c.dma("sp", a[:], s.PT[n * 128:(n + 1) * 128, 0:512], reads=[s.t_PT], writes=[ta])
            av = a[:, 0:256].rearrange("p (h d) -> p h d", h=2)
            if n >= 2:
                cs, tcs = CS[n % 2]
                c.dma("sp", cs[:, 0, :], s.COS[(n - 2) * 128:(n - 1) * 128, :], reads=[s.t_COS], writes=[tcs])
                c.dma("sp", cs[:, 1, :], s.SIN[(n - 2) * 128:(n - 1) * 128, :], reads=[s.t_COS], writes=[tcs])
                s.rope_ops(av, ta, xr, t_xr, cs[:, 0, :], cs[:, 1, :], tcs, 2, tmp, t_tmp)
            else:
                c.op("dve", lambda e, av=av: e.tensor_copy(out=xr[:], in_=av), reads=[ta], writes=[t_xr])
            pT, tpT = s.pst()
            for i in range(2):
                c.op("pe", lambda e, pT=pT, i=i: e.transpose(out=pT[:, i * 128:(i + 1) * 128], in_=xr[:, i, :], identity=s.ident_b[:]),
                     reads=[t_xr, s.t_ident_b], writes=[tpT])
            c.op("dve", lambda e, pT=pT, n=n: e.tensor_copy(out=QTa[:, n, :], in_=pT[:, 0:128]), reads=[tpT], writes=[t_QTa])
            c.op("dve", lambda e, pT=pT, n=n: e.tensor_copy(out=KTa[:, n, :], in_=pT[:, 128:256]), reads=[tpT], writes=[t_KTa])
            c.op("act", lambda e, n=n: e.activation(out=Ka[:, n, :], in_=xr[:, 1, :], func=AF.Copy), reads=[t_xr], writes=[t_Ka])
            c.op("act", lambda e, a=a, n=n: e.activation(out=Va[:, n, :], in_=a[:, 256:384], func=AF.Copy), reads=[ta], writes=[t_Va])
            c.op("act", lambda e, a=a, n=n: e.activation(out=SG[:, n, :], in_=a[:, 384:512], func=AF.Silu), reads=[ta], writes=[t_SG])
        S, t_S = c.sbuf("r_S", [128, 128], F32)
        Sb, t_Sb = c.sbuf("r_Sb", [128, 128], BF16)
        ATb = [c.sbuf(f"r_AT{i}", [128, 128], BF16) for i in range(2)]
        QdT = [c.sbuf(f"r_QdT{i}", [128, 128], BF16) for i in range(2)]
        Vd = [c.sbuf(f"r_Vd{i}", [128, 128], BF16) for i in range(2)]
        osum, t_osum = c.sbuf("r_osum", [128, 128], F32)
        junk, t_junk = c.sbuf("r_junk", [128, 128], F32)
        ss, t_ss = c.sbuf("r_ss", [128, 1], F32)
        yb, t_yb = c.sbuf("r_yb", [128, 128], BF16)
        ysb = [c.sbuf(f"r_ysb{i}", [128, 128], BF16) for i in range(2)]
        orders = [[0, 1] + list(range(2, 66)), [1, 0] + list(range(65, 1, -1))]
        st = {}

        def r_prep(n, d, par):
            at, tat = ATb[par]
            qd, tqd = QdT[par]
            vd, tvd = Vd[par]
            psa, tpsa = s.ps()
            c.op("pe", lambda e: e.matmul(psa[:, 0:128], lhsT=KTa[:, n, :], rhs=QTa[:, n, :], start=True, stop=True), reads=[t_KTa, t_QTa], writes=[tpsa])
            c.op("dve", lambda e: e.tensor_tensor(out=at[:], in0=psa[:, 0:128], in1=DT[d][0][:], op=ALU.mult), reads=[tpsa, DT[d][1]], writes=[tat])
            c.op("pool", lambda e: e.tensor_tensor(out=qd[:], in0=QTa[:, n, :], in1=RQ[d][0][:], op=ALU.mult), reads=[t_QTa, RQ[d][1]], writes=[tqd])
            c.op("pool", lambda e: e.tensor_scalar(out=vd[:], in0=Va[:, n, :], scalar1=kd[:, d:d + 1], scalar2=None, op0=ALU.mult), reads=[t_Va, t_kd], writes=[tvd])
            pso, tpso = s.ps()
            c.op("pe", lambda e: e.matmul(pso[:, 0:128], lhsT=at[:], rhs=Va[:, n, :], start=True, stop=False), reads=[tat, t_Va], writes=[tpso])
            pss, tpss = s.ps()
            c.op("pe", lambda e: e.matmul(pss[:, 0:128], lhsT=Ka[:, n, :], rhs=vd[:], start=True, stop=True), reads=[t_Ka, tvd], writes=[tpss])
            st[par] = (pso, tpso, pss, tpss)

        def r_seq(n, d, par, it):
            qd, tqd = QdT[par]
            pso, tpso, pss, tpss = st[par]
            c.op("pe", lambda e: e.matmul(pso[:, 0:128], lhsT=qd[:], rhs=Sb[:], start=False, stop=True), reads=[tqd, t_Sb], writes=[tpso])
            c.op("dve", lambda e: e.scalar_tensor_tensor(out=S[:], in0=S[:], scalar=g128[:, d:d + 1], in1=pss[:, 0:128], op0=ALU.mult, op1=ALU.add),
                 reads=[tpss, t_g128], writes=[t_S])
            c.op("dve", lambda e: e.tensor_copy(out=Sb[:], in_=S[:]), reads=[t_S], writes=[t_Sb])
            if d == 0:
                c.op("act", lambda e: e.activation(out=Of[:, n, :], in_=pso[:, 0:128], func=AF.Copy), reads=[tpso], writes=[t_Of])
            else:
                c.op("dve", lambda e: e.tensor_tensor(out=osum[:], in0=pso[:, 0:128], in1=Of[:, n, :], op=ALU.add), reads=[tpso, t_Of], writes=[t_osum])
                c.op("act", lambda e: e.activation(out=junk[:], in_=osum[:], func=AF.Square, accum_out=ss[:]), reads=[t_osum], writes=[t_junk, t_ss])
                c.op("dve", lambda e: e.tensor_scalar(out=ss[:], in0=ss[:], scalar1=1.0 / 128, scalar2=EPS, op0=ALU.mult, op1=ALU.add), reads=[], writes=[t_ss])
                c.op("act", lambda e: e.activation(out=ss[:], in_=ss[:], func=AF.Sqrt), reads=[], writes=[t_ss])
                c.op("dve", lambda e: e.reciprocal(out=ss[:], in_=ss[:]), reads=[], writes=[t_ss])
                c.op("dve", lambda e: e.scalar_tensor_tensor(out=yb[:], in0=osum[:], scalar=ss[:, 0:1], in1=SG[:, n, :], op0=ALU.mult, op1=ALU.mult),
                     reads=[t_osum, t_ss, t_SG], writes=[t_yb])
                pT, tpT = s.pst()
                c.op("pe", lambda e: e.transpose(out=pT[:, 0:128], in_=yb[:], identity=s.ident_b[:]), reads=[t_yb, s.t_ident_b], writes=[tpT])
                y, ty = ysb[it % 2]
                c.op("dve", lambda e: e.tensor_copy(out=y[:], in_=pT[:, 0:128]), reads=[tpT], writes=[ty])
                s.scatter_y(y[:], ty, 0, n * 128, 128)

        it = 0
        for d in range(2):
            c.op("dve", lambda e: e.memset(S[:], 0.0), reads=[], writes=[t_S])
            c.op("dve", lambda e: e.memset(Sb[:], 0.0), reads=[], writes=[t_Sb])
            order = orders[d]
            r_prep(order[0], d, it % 2)
            for i, n in enumerate(order):
                if i + 1 < len(order):
                    r_prep(order[i + 1], d, (it + 1) % 2)
                r_seq(n, d, it % 2, it)
                it += 1

    def phase_dn(self, l):
        s, c = self, self.c
        s.alloc_scatter()
        cw0 = l * 15
        def mask(name, pattern, cm, op):
            t, tt = c.sbuf(name, [128, 128], F32)
            c.op("pool", lambda e: e.affine_select(out=t[:], in_=s.ones_f[:], pattern=pattern, compare_op=op, fill=0.0, base=0, channel_multiplier=cm),
                 reads=[s.t_ones_f], writes=[tt])
            return t, tt
        Mge = mask("d_Mge", [[1, 128]], -1, ALU.is_ge)
        Mgt = mask("d_Mgt", [[1, 128]], -1, ALU.is_gt)
        Mle = mask("d_Mle", [[-1, 128]], 1, ALU.is_ge)
        Mlt = mask("d_Mlt", [[-1, 128]], 1, ALU.is_gt)
        TRI = [Mge, Mle]
        STL = [Mlt, Mgt]
        QTd, t_QTd = c.sbuf("d_QT", [128, T], BF16)
        KTd, t_KTd = c.sbuf("d_KT", [128, T], BF16)
        Ktm, t_Ktm = c.sbuf("d_Ktm", [128, 66, 128], BF16)
        Vtm, t_Vtm = c.sbuf("d_Vtm", [128, 66, 128], BF16)
        Of, t_Of = c.sbuf("d_Of", [128, 66, 128], F32)
        xin = [c.sbuf(f"d_xin{i}", [128, 516], F32) for i in range(2)]
        acc = [c.sbuf(f"d_acc{i}", [128, 512], F32) for i in range(2)]
        sq, t_sq = c.sbuf("d_sq", [128, 512], F32)
        rn, t_rn = c.sbuf("d_rn", [128, 512], F32)
        vt, t_vt = c.sbuf("d_vt", [128, 512], BF16)
        PFv = s.PF.rearrange("(s p) t -> s p t", p=128)
        seqs = [(0, LC)] + [(LC, T)]
        tiles = []
        for (sa, sb_) in seqs:
            t0 = sa
            while t0 < sb_:
                n = min(512, sb_ - t0)
                tiles.append((t0, n, t0 == sa, t0 + n == sb_))
                t0 += n
        k = 0
        for q in range(3):
            for (t0, n, first, last) in tiles:
                xi, txi = xin[k % 2]
                ac, tac = acc[k % 2]
                k += 1
                lo = 0 if not first else 2
                hi = n + 4 if not last else n + 2
                if first:
                    c.op("pool", lambda e, xi=xi: e.memset(xi[:, 0:2], 0.0), reads=[], writes=[txi])
                if last:
                    c.op("pool", lambda e, xi=xi, n=n: e.memset(xi[:, n + 2:n + 4], 0.0), reads=[], writes=[txi])
                c.dma("sp", xi[:, lo:hi], PFv[q][:, t0 - 2 + lo:t0 - 2 + hi], reads=[s.t_PF], writes=[txi])
                wq = cw0 + q * 5
                c.op("dve", lambda e, xi=xi, ac=ac, n=n, wq=wq: e.tensor_scalar(out=ac[:, 0:n], in0=xi[:, 0:n], scalar1=s.convw_s[:, wq:wq + 1], scalar2=None, op0=ALU.mult),
                     reads=[txi, s.t_convw], writes=[tac])
                for tap in range(1, 5):
                    c.op("dve", lambda e, xi=xi, ac=ac, n=n, wq=wq, tap=tap: e.scalar_tensor_tensor(out=ac[:, 0:n], in0=xi[:, tap:tap + n], scalar=s.convw_s[:, wq + tap:wq + tap + 1],
                                                                                                  in1=ac[:, 0:n], op0=ALU.mult, op1=ALU.add), reads=[txi, s.t_convw], writes=[tac])
                c.op("act", lambda e, ac=ac, n=n: e.activation(out=ac[:, 0:n], in_=ac[:, 0:n], func=AF.Silu), reads=[], writes=[tac])
                if q < 2:
                    c.op("dve", lambda e, ac=ac, n=n: e.tensor_tensor(out=sq[:, 0:n], in0=ac[:, 0:n], in1=ac[:, 0:n], op=ALU.mult), reads=[tac], writes=[t_sq])
                    ps, tps = s.ps()
                    c.op("pe", lambda e, ps=ps, n=n: e.matmul(ps[:, 0:n], lhsT=s.ones_f[:], rhs=sq[:, 0:n], start=True, stop=True), reads=[t_sq, s.t_ones_f], writes=[tps])
                    c.op("act", lambda e, ps=ps, n=n: e.activation(out=rn[:, 0:n], in_=ps[:, 0:n], func=AF.Sqrt, bias=s.eps_t[:, 0:1]), reads=[tps, s.t_eps], writes=[t_rn])
                    c.op("dve", lambda e, n=n: e.reciprocal(out=rn[:, 0:n], in_=rn[:, 0:n]), reads=[], writes=[t_rn])
                    dst, tdst = (QTd, t_QTd) if q == 0 else (KTd, t_KTd)
                    scl = 128.0 ** -0.5 if q == 0 else 1.0
                    c.op("dve", lambda e, ac=ac, n=n, dst=dst, t0=t0, scl=scl: e.scalar_tensor_tensor(out=dst[:, t0:t0 + n], in0=ac[:, 0:n], scalar=scl, in1=rn[:, 0:n],
                                                                                                     op0=ALU.mult, op1=ALU.mult), reads=[tac, t_rn], writes=[tdst])
                    src, tsrc = dst, tdst
                    soff = t0
                else:
                    c.op("dve", lambda e, ac=ac, n=n: e.tensor_copy(out=vt[:, 0:n], in_=ac[:, 0:n]), reads=[tac], writes=[t_vt])
                    src, tsrc = vt, t_vt
                    soff = 0
                if q >= 1:
                    dtm, tdtm = (Ktm, t_Ktm) if q == 1 else (Vtm, t_Vtm)
                    for sub in range(n // 128):
                        pT, tpT = s.pst()
                        c.op("pe", lambda e, pT=pT, src=src, soff=soff, sub=sub: e.transpose(out=pT[:, 0:128], in_=src[:, soff + sub * 128:soff + (sub + 1) * 128], identity=s.ident_b[:]),
                             reads=[tsrc, s.t_ident_b], writes=[tpT])
                        nb_ = (t0 + sub * 128) // 128
                        c.op("dve", lambda e, pT=pT, dtm=dtm, nb_=nb_: e.tensor_copy(out=dtm[:, nb_, :], in_=pT[:, 0:128]), reads=[tpT], writes=[tdtm])
        AB, t_AB = c.sbuf("d_AB", [128, 66, 4], F32)
        c.dma("sp", AB[:], s.PT.rearrange("(n p) c -> p n c", p=128)[:, :, 1152:1156], reads=[s.t_PT], writes=[t_AB])
        nega, t_nega = c.sbuf("d_nega", [128, 2], F32)
        c.op("act", lambda e: e.activation(out=nega[:], in_=s.hp_s[:, 8 + l * 2:8 + l * 2 + 2], func=AF.Exp), reads=[s.t_hp], writes=[t_nega])
        c.op("dve", lambda e: e.tensor_scalar(out=nega[:], in0=nega[:], scalar1=-1.0, scalar2=None, op0=ALU.mult), reads=[], writes=[t_nega])
        G, t_G = c.sbuf("d_G", [128, 2, 66], F32)
        Bt, t_Bt = c.sbuf("d_B", [128, 2, 66], F32)
        NB, t_NB = c.sbuf("d_NB", [128, 2, 66], F32)
        for d in range(2):
            c.op("act", lambda e, d=d: e.activation(out=G[:, d, :], in_=AB[:, :, d], func=AF.Exp, bias=s.hp_s[:, 16 + l * 2 + d:16 + l * 2 + d + 1]),
                 reads=[t_AB, s.t_hp], writes=[t_G])
            c.op("act", lambda e, d=d: e.activation(out=G[:, d, :], in_=G[:, d, :], func=AF.Ln, bias=s.one_t[:, 0:1]), reads=[s.t_eps], writes=[t_G])
            c.op("dve", lambda e, d=d: e.tensor_scalar(out=G[:, d, :], in0=G[:, d, :], scalar1=nega[:, d:d + 1], scalar2=None, op0=ALU.mult), reads=[t_nega], writes=[t_G])
            c.op("act", lambda e, d=d: e.activation(out=Bt[:, d, :], in_=AB[:, :, 2 + d], func=AF.Sigmoid), reads=[t_AB], writes=[t_Bt])
        c.op("dve", lambda e: e.tensor_scalar(out=NB[:], in0=Bt[:], scalar1=-1.0, scalar2=None, op0=ALU.mult), reads=[t_Bt], writes=[t_NB])
        def T2(name, dt=F32, shape=(128, 128)):
            return [c.sbuf(f"{name}{i}", list(shape), dt) for i in range(2)]
        Xg = T2("d_Xg"); sm = T2("d_sm", F32, (128, 8)); Er = T2("d_Er"); DTm = T2("d_DTm"); DLm = T2("d_DLm")
        Pm = T2("d_P"); PTm = T2("d_PT"); Mi = T2("d_M"); Xa = T2("d_Xa"); Xb = T2("d_Xb"); XTa = T2("d_XTa"); XTb = T2("d_XTb")
        TTb = T2("d_TT", BF16); bv = T2("d_bv", BF16); kbg = T2("d_kbg", BF16)
        wv = T2("d_wv"); kcT = T2("d_kcT", BF16); qkT = T2("d_qkT", BF16); qgT = T2("d_qgT", BF16); kg = T2("d_kg", BF16)
        vnb, t_vnb = c.sbuf("d_vnb", [128, 128], BF16)
        S, t_S = c.sbuf("d_S", [128, 128], F32)
        Sb, t_Sb = c.sbuf("d_Sb", [128, 128], BF16)
        zt = [c.sbuf(f"d_z{i}", [128, 128], F32) for i in range(2)]
        osum, t_osum = c.sbuf("d_osum", [128, 128], F32)
        junk, t_junk = c.sbuf("d_junk", [128, 128], F32)
        ss, t_ss = c.sbuf("d_ss", [128, 1], F32)
        yb, t_yb = c.sbuf("d_yb", [128, 128], BF16)
        ysb = [c.sbuf(f"d_ysb{i}", [128, 128], BF16) for i in range(2)]
        nwd = s.nw_s[:, (l * 3) * 128:(l * 3 + 1) * 128]

        def evac(eng, dst, tdst, ps, tps):
            if eng == "act":
                c.op("act", lambda e: e.activation(out=dst[:], in_=ps[:, 0:128], func=AF.Copy), reads=[tps], writes=[tdst])
            else:
                c.op("dve", lambda e: e.tensor_copy(out=dst[:], in_=ps[:, 0:128]), reads=[tps], writes=[tdst])

        def block_prep(n, d, par):
            blk = slice(n * 128, (n + 1) * 128)
            tri, ttri = TRI[d]
            stl, tstl = STL[d]
            g = G[:, d, n:n + 1]
            xg, txg = Xg[par]
            smt, tsm = sm[par]
            c.op("dve", lambda e: e.tensor_scalar(out=xg[:], in0=tri[:], scalar1=g, scalar2=None, op0=ALU.mult), reads=[ttri, t_G], writes=[txg])
            psm, tpsm = s.ps()
            c.op("pe", lambda e: e.matmul(psm[:, 0:1], lhsT=tri[:], rhs=G[:, d, n:n + 1], start=True, stop=True), reads=[ttri, t_G], writes=[tpsm])
            c.op("pe", lambda e: e.matmul(psm[:, 2:3], lhsT=s.ones_f[:], rhs=G[:, d, n:n + 1], start=True, stop=True), reads=[s.t_ones_f, t_G], writes=[tpsm])
            psR, tpsR = s.ps()
            c.op("pe", lambda e: e.matmul(psR[:, 0:128], lhsT=s.ones_f[:], rhs=xg[:], start=True, stop=True), reads=[s.t_ones_f, txg], writes=[tpsR])
            c.op("dve", lambda e: e.tensor_copy(out=smt[:, 0:1], in_=psm[:, 0:1]), reads=[tpsm], writes=[tsm])
            c.op("dve", lambda e: e.tensor_copy(out=smt[:, 1:2], in_=psm[:, 2:3]), reads=[tpsm], writes=[tsm])
            c.op("act", lambda e: e.activation(out=smt[:, 2:3], in_=smt[:, 0:1], func=AF.Exp), reads=[], writes=[tsm])
            c.op("act", lambda e: e.activation(out=smt[:, 3:4], in_=smt[:, 0:1], func=AF.Exp, scale=-1.0, bias=smt[:, 1:2]), reads=[], writes=[tsm])
            c.op("act", lambda e: e.activation(out=smt[:, 4:5], in_=smt[:, 1:2], func=AF.Exp), reads=[], writes=[tsm])
            c.op("dve", lambda e: e.tensor_tensor(out=smt[:, 5:6], in0=smt[:, 2:3], in1=Bt[:, d, n:n + 1], op=ALU.mult), reads=[t_Bt], writes=[tsm])
            er, ter = Er[par]
            c.op("act", lambda e: e.activation(out=er[:], in_=psR[:, 0:128], func=AF.Exp), reads=[tpsR], writes=[ter])
            dtm, tdtm = DTm[par]
            c.op("dve", lambda e: e.tensor_scalar(out=dtm[:], in0=psR[:, 0:128], scalar1=smt[:, 0:1], scalar2=0.0, op0=ALU.subtract, op1=ALU.min), reads=[tpsR, tsm], writes=[tdtm])
            c.op("act", lambda e: e.activation(out=dtm[:], in_=dtm[:], func=AF.Exp), reads=[], writes=[tdtm])
            c.op("dve", lambda e: e.tensor_tensor(out=dtm[:], in0=dtm[:], in1=tri[:], op=ALU.mult), reads=[ttri], writes=[tdtm])
            dlm, tdlm = DLm[par]
            c.op("dve", lambda e: e.tensor_scalar(out=dlm[:], in0=psR[:, 0:128], scalar1=smt[:, 0:1], scalar2=0.0, op0=ALU.subtract, op1=ALU.max), reads=[tpsR, tsm], writes=[tdlm])
            c.op("act", lambda e: e.activation(out=dlm[:], in_=dlm[:], func=AF.Exp, scale=-1.0), reads=[], writes=[tdlm])
            c.op("pool", lambda e: e.tensor_tensor(out=dlm[:], in0=dlm[:], in1=stl[:], op=ALU.mult), reads=[tstl], writes=[tdlm])
            psK, tpsK = s.ps()
            c.op("pe", lambda e: e.matmul(psK[:, 0:128], lhsT=KTd[:, blk], rhs=KTd[:, blk], start=True, stop=True), reads=[t_KTd], writes=[tpsK])
            pm, tpm = Pm[par]
            c.op("dve", lambda e: e.scalar_tensor_tensor(out=pm[:], in0=psK[:, 0:128], scalar=NB[:, d, n:n + 1], in1=dlm[:], op0=ALU.mult, op1=ALU.mult),
                 reads=[tpsK, t_NB, tdlm], writes=[tpm])
            psT_, tpsT_ = s.ps()
            c.op("pe", lambda e: e.transpose(out=psT_[:, 0:128], in_=pm[:], identity=s.ident_f[:]), reads=[tpm, s.t_ident_f], writes=[tpsT_])
            ptm, tptm = PTm[par]
            evac("act", ptm, tptm, psT_, tpsT_)
            mi, tmi = Mi[par]
            c.op("dve", lambda e: e.tensor_tensor(out=mi[:], in0=ptm[:], in1=s.ident_f[:], op=ALU.add), reads=[tptm, s.t_ident_f], writes=[tmi])
            X, tX = ptm, tptm
            XT_, tXT = pm, tpm
            bufs = [(Xa[par], XTa[par]), (Xb[par], XTb[par])]
            for m in range(1, 7):
                (xn, txn), (xtn, txtn) = bufs[m % 2]
                if m < 6:
                    p1, tp1 = s.ps()
                    c.op("pe", lambda e, p1=p1, X=X, XT_=XT_: e.matmul(p1[:, 0:128], lhsT=XT_[:], rhs=X[:], start=True, stop=True), reads=[tX, tXT], writes=[tp1])
                    evac("act", xn, txn, p1, tp1)
                p2, tp2 = s.ps()
                c.op("pe", lambda e, p2=p2, X=X, XT_=XT_: e.matmul(p2[:, 0:128], lhsT=X[:], rhs=XT_[:], start=True, stop=True), reads=[tX, tXT], writes=[tp2])
                evac("dve", xtn, txtn, p2, tp2)
                p3, tp3 = s.ps()
                c.op("pe", lambda e, p3=p3, xtn=xtn: e.matmul(p3[:, 0:128], lhsT=xtn[:], rhs=mi[:], start=True, stop=True), reads=[txtn, tmi], writes=[tp3])
                c.op("dve", lambda e, p3=p3: e.tensor_tensor(out=mi[:], in0=mi[:], in1=p3[:, 0:128], op=ALU.add), reads=[tp3], writes=[tmi])
                X, tX, XT_, tXT = xn, txn, xtn, txtn
            tt, ttt = TTb[par]
            c.op("act", lambda e: e.activation(out=tt[:], in_=mi[:], func=AF.Copy), reads=[tmi], writes=[ttt])
            b_, tb_ = bv[par]
            c.op("pool", lambda e: e.tensor_scalar(out=b_[:], in0=Vtm[:, n, :], scalar1=Bt[:, d, n:n + 1], scalar2=None, op0=ALU.mult), reads=[t_Vtm, t_Bt], writes=[tb_])
            kb_, tkb_ = kbg[par]
            c.op("pool", lambda e: e.tensor_scalar(out=kb_[:], in0=Ktm[:, n, :], scalar1=smt[:, 5:6], scalar2=None, op0=ALU.mult), reads=[t_Ktm, tsm], writes=[tkb_])
            pw, tpw = s.ps()
            c.op("pe", lambda e: e.matmul(pw[:, 0:128], lhsT=tt[:], rhs=b_[:], start=True, stop=True), reads=[ttt, tb_], writes=[tpw])
            evac("act", wv[par][0], wv[par][1], pw, tpw)
            pk, tpk = s.ps()
            c.op("pe", lambda e: e.matmul(pk[:, 0:128], lhsT=kb_[:], rhs=tt[:], start=True, stop=True), reads=[tkb_, ttt], writes=[tpk])
            evac("dve", kcT[par][0], kcT[par][1], pk, tpk)
            pq, tpq = s.ps()
            c.op("pe", lambda e: e.matmul(pq[:, 0:128], lhsT=KTd[:, blk], rhs=QTd[:, blk], start=True, stop=True), reads=[t_KTd, t_QTd], writes=[tpq])
            c.op("dve", lambda e: e.tensor_tensor(out=qkT[par][0][:], in0=pq[:, 0:128], in1=dtm[:], op=ALU.mult), reads=[tpq, tdtm], writes=[qkT[par][1]])
            c.op("pool", lambda e: e.tensor_tensor(out=qgT[par][0][:], in0=QTd[:, blk], in1=er[:], op=ALU.mult), reads=[t_QTd, ter], writes=[qgT[par][1]])
            c.op("pool", lambda e: e.tensor_scalar(out=kg[par][0][:], in0=Ktm[:, n, :], scalar1=smt[:, 3:4], scalar2=None, op0=ALU.mult), reads=[t_Ktm, tsm], writes=[kg[par][1]])

        def block_seq(n, d, par, it):
            smt, tsm = sm[par]
            p1, tp1 = s.ps()
            c.op("pe", lambda e: e.matmul(p1[:, 0:128], lhsT=kcT[par][0][:], rhs=Sb[:], start=True, stop=True), reads=[kcT[par][1], t_Sb], writes=[tp1])
            c.op("dve", lambda e: e.tensor_tensor(out=vnb[:], in0=wv[par][0][:], in1=p1[:, 0:128], op=ALU.subtract), reads=[wv[par][1], tp1], writes=[t_vnb])
            po, tpo = s.ps()
            c.op("pe", lambda e: e.matmul(po[:, 0:128], lhsT=qgT[par][0][:], rhs=Sb[:], start=True, stop=False), reads=[qgT[par][1], t_Sb], writes=[tpo])
            c.op("pe", lambda e: e.matmul(po[:, 0:128], lhsT=qkT[par][0][:], rhs=vnb[:], start=False, stop=True), reads=[qkT[par][1], t_vnb], writes=[tpo])
            pS, tpS = s.ps()
            c.op("pe", lambda e: e.matmul(pS[:, 0:128], lhsT=kg[par][0][:], rhs=vnb[:], start=True, stop=True), reads=[kg[par][1], t_vnb], writes=[tpS])
            c.op("dve", lambda e: e.scalar_tensor_tensor(out=S[:], in0=S[:], scalar=smt[:, 4:5], in1=pS[:, 0:128], op0=ALU.mult, op1=ALU.add), reads=[tpS, tsm], writes=[t_S])
            c.op("dve", lambda e: e.tensor_copy(out=Sb[:], in_=S[:]), reads=[t_S], writes=[t_Sb])
            if d == 0:
                c.op("act", lambda e: e.activation(out=Of[:, n, :], in_=po[:, 0:128], func=AF.Copy), reads=[tpo], writes=[t_Of])
            else:
                z, tz = zt[it % 2]
                c.dma("sp", z[:], s.PT[n * 128:(n + 1) * 128, 1024:1152], reads=[s.t_PT], writes=[tz])
                c.op("act", lambda e: e.activation(out=z[:], in_=z[:], func=AF.Silu), reads=[], writes=[tz])
                c.op("dve", lambda e: e.tensor_tensor(out=osum[:], in0=po[:, 0:128], in1=Of[:, n, :], op=ALU.add), reads=[tpo, t_Of], writes=[t_osum])
                c.op("act", lambda e: e.activation(out=junk[:], in_=osum[:], func=AF.Square, accum_out=ss[:]), reads=[t_osum], writes=[t_junk, t_ss])
                c.op("dve", lambda e: e.tensor_scalar(out=ss[:], in0=ss[:], scalar1=1.0 / 128, scalar2=EPS, op0=ALU.mult, op1=ALU.add), reads=[], writes=[t_ss])
                c.op("act", lambda e: e.activation(out=ss[:], in_=ss[:], func=AF.Sqrt), reads=[], writes=[t_ss])
                c.op("dve", lambda e: e.reciprocal(out=ss[:], in_=ss[:]), reads=[], writes=[t_ss])
                c.op("dve", lambda e: e.scalar_tensor_tensor(out=osum[:], in0=osum[:], scalar=ss[:, 0:1], in1=nwd, op0=ALU.mult, op1=ALU.mult), reads=[t_ss, s.t_nw], writes=[t_osum])
                c.op("dve", lambda e: e.tensor_tensor(out=yb[:], in0=osum[:], in1=z[:], op=ALU.mult), reads=[tz, t_osum], writes=[t_yb])
                pT, tpT = s.pst()
                c.op("pe", lambda e: e.transpose(out=pT[:, 0:128], in_=yb[:], identity=s.ident_b[:]), reads=[t_yb, s.t_ident_b], writes=[tpT])
                y, ty = ysb[it % 2]
                c.op("dve", lambda e: e.tensor_copy(out=y[:], in_=pT[:, 0:128]), reads=[tpT], writes=[ty])
                s.scatter_y(y[:], ty, 1, n * 128, 128)

        orders = [[0, 1] + list(range(2, 66)), [1, 0] + list(range(65, 1, -1))]
        it = 0
        for d in range(2):
            c.op("dve", lambda e: e.memset(S[:], 0.0), reads=[], writes=[t_S])
            c.op("dve", lambda e: e.memset(Sb[:], 0.0), reads=[], writes=[t_Sb])
            order = orders[d]
            block_prep(order[0], d, it % 2)
            for i, n in enumerate(order):
                if i + 1 < len(order):
                    block_prep(order[i + 1], d, (it + 1) % 2)
                block_seq(n, d, it % 2, it)
                it += 1

    def layer_norm_fm(self, xb, t_xb, n, l, which_w, which_b):
        s, c = self, self.c
        ps1, tps1 = s.ps()
        ps2, tps2 = s.ps()
        for f in range(KC):
            zb, tzb = s.c_zb[f % 2]
            zq, tzq = s.c_zq[f % 2]
            c.op("dve", lambda e, zb=zb, f=f: e.tensor_copy(out=zb[:, 0:n], in_=xb[:, f, 0:n]), reads=[t_xb], writes=[tzb])
            c.op("act", lambda e, zq=zq, f=f: e.activation(out=zq[:, 0:n], in_=xb[:, f, 0:n], func=AF.Square), reads=[t_xb], writes=[tzq])
            c.op("pe", lambda e, zb=zb, f=f: e.matmul(ps1[:, 0:n], lhsT=s.ones_b[:], rhs=zb[:, 0:n], start=(f == 0), stop=(f == KC - 1)),
                 reads=[tzb, s.t_ones_b], writes=[tps1])
            c.op("pe", lambda e, zq=zq, f=f: e.matmul(ps2[:, 0:n], lhsT=s.ones_b[:], rhs=zq[:, 0:n], start=(f == 0), stop=(f == KC - 1)),
                 reads=[tzq, s.t_ones_b], writes=[tps2])
        mean, msq, rstd, nmr = s.c_ln
        t_ln = s.t_c_ln
        c.op("act", lambda e: e.activation(out=mean[:, 0:n], in_=ps1[:, 0:n], func=AF.Copy, scale=1.0 / D), reads=[tps1], writes=[t_ln])
        c.op("dve", lambda e: e.tensor_tensor(out=msq[:, 0:n], in0=mean[:, 0:n], in1=mean[:, 0:n], op=ALU.mult), reads=[], writes=[t_ln])
        c.op("dve", lambda e: e.scalar_tensor_tensor(out=rstd[:, 0:n], in0=ps2[:, 0:n], scalar=1.0 / D, in1=msq[:, 0:n], op0=ALU.mult, op1=ALU.subtract),
             reads=[tps2], writes=[t_ln])
        c.op("dve", lambda e: e.tensor_scalar(out=rstd[:, 0:n], in0=rstd[:, 0:n], scalar1=0.0, scalar2=EPS / (ALPHA * ALPHA), op0=ALU.max, op1=ALU.add), reads=[], writes=[t_ln])
        c.op("act", lambda e: e.activation(out=rstd[:, 0:n], in_=rstd[:, 0:n], func=AF.Sqrt), reads=[], writes=[t_ln])
        c.op("dve", lambda e: e.reciprocal(out=rstd[:, 0:n], in_=rstd[:, 0:n]), reads=[], writes=[t_ln])
        c.op("dve", lambda e: e.tensor_tensor(out=nmr[:, 0:n], in0=mean[:, 0:n], in1=rstd[:, 0:n], op=ALU.mult), reads=[], writes=[t_ln])
        for f in range(KC):
            eng = "dve" if f % 2 == 0 else "pool"
            c.op(eng, lambda e, f=f: e.tensor_tensor(out=xb[:, f, 0:n], in0=xb[:, f, 0:n], in1=rstd[:, 0:n], op=ALU.mult), reads=[t_ln], writes=[t_xb])
            c.op(eng, lambda e, f=f: e.tensor_tensor(out=xb[:, f, 0:n], in0=xb[:, f, 0:n], in1=nmr[:, 0:n], op=ALU.subtract), reads=[t_ln], writes=[t_xb])
            wi = (which_w * 4 + l) * 16 + f
            bi_ = (which_b * 4 + l) * 16 + f
            c.op("act", lambda e, f=f, wi=wi, bi_=bi_: e.activation(out=xb[:, f, 0:n], in_=xb[:, f, 0:n], func=AF.Identity,
                                                                   scale=s.lnp_s[:, wi:wi + 1], bias=s.lnp_s[:, bi_:bi_ + 1]),
                 reads=[s.t_lnp], writes=[t_xb])

    def phase_c(self, l):
        s, c = self, self.c
        xb, t_xb = c.sbuf("c_x", [128, KC, 512], F32)
        yb, t_yb = c.sbuf("c_y", [128, KC, 512], BF16)
        hid, t_hid = c.sbuf("c_hid", [128, 44, 512], BF16)
        s.c_zb = [c.sbuf(f"c_zb{i}", [128, 512], BF16) for i in range(2)]
        s.c_zq = [c.sbuf(f"c_zq{i}", [128, 512], BF16) for i in range(2)]
        lnt, s.t_c_ln = c.sbuf("c_ln", [128, 4, 512], F32)
        s.c_ln = [lnt[:, i, :] for i in range(4)]
        sg = [c.sbuf(f"c_sg{i}", [128, 512], F32) for i in range(2)]
        wo = [c.sbuf(f"c_wo{i}", [128, 2048], BF16) for i in range(2)]
        wf1 = [c.sbuf(f"c_wf1{i}", [128, 4096], BF16) for i in range(2)]
        wf2 = [c.sbuf(f"c_wf2{i}", [128, DFF], BF16) for i in range(2)]
        XTv = s.XT.rearrange("(kc p) t -> p kc t", p=128)
        YOv = s.yown.rearrange("(kc p) t -> p kc t", p=128)
        tiles = s.own_tiles()
        if l == NL - 1:
            tiles = tiles[1:]
        for (t0_, n_, w_) in tiles:
            s.phase_c_tile(l, t0_, n_, w_, xb, t_xb, yb, t_yb, hid, t_hid, sg, wo, wf1, wf2, XTv, YOv)

    def phase_c_tile(self, l, t0, n, w, xb, t_xb, yb, t_yb, hid, t_hid, sg, wo, wf1, wf2, XTv, YOv):
        s, c = self, self.c
        if True:
            c.dma("sp", yb[:, :, 0:n], YOv[:, :, t0:t0 + n], reads=[s.t_yown], writes=[t_yb])
            c.dma("sp", xb[:, :, 0:n], XTv[:, :, t0:t0 + n], reads=[s.t_XT], writes=[t_xb])
            for f in range(KC):
                wt, twt = wo[f % 2]
                c.dma("sp", wt[:], s.Wo[l][f * 128:(f + 1) * 128, :], reads=[s.t_W[l]], writes=[twt])
                ps, tps = s.ps()
                for kc in range(KC):
                    c.op("pe", lambda e, ps=ps, wt=wt, kc=kc: e.matmul(ps[:, 0:n], lhsT=wt[:, kc * 128:(kc + 1) * 128], rhs=yb[:, kc, 0:n],
                                                                      start=(kc == 0), stop=(kc == KC - 1)), reads=[twt, t_yb], writes=[tps], defer=(kc != KC - 1))
                c.op("dve", lambda e, ps=ps, f=f: e.scalar_tensor_tensor(out=xb[:, f, 0:n], in0=ps[:, 0:n], scalar=s.MM[:, l, 32 + f, w:w + 1],
                                                                        in1=xb[:, f, 0:n], op0=ALU.mult, op1=ALU.add), reads=[tps, s.t_MM], writes=[t_xb])
            s.layer_norm_fm(xb, t_xb, n, l, 0, 1)
            for f in range(KC):
                eng = "dve" if f % 2 == 0 else "pool"
                c.op(eng, lambda e, f=f: e.tensor_scalar(out=yb[:, f, 0:n], in0=xb[:, f, 0:n], scalar1=s.MM[:, l, 64 + f, w:w + 1],
                                                        scalar2=s.MM[:, l, 48 + f, w:w + 1], op0=ALU.mult, op1=ALU.add), reads=[t_xb, s.t_MM], writes=[t_yb])
            for hc in range(44):
                wt, twt = wf1[hc % 2]
                c.dma("sp", wt[:], s.Wf1[l][hc * 128:(hc + 1) * 128, :], reads=[s.t_W[l]], writes=[twt])
                psG, tpsG = s.ps()
                psU, tpsU = s.ps()
                for kc in range(KC):
                    c.op("pe", lambda e, psG=psG, wt=wt, kc=kc: e.matmul(psG[:, 0:n], lhsT=wt[:, kc * 256:kc * 256 + 128], rhs=yb[:, kc, 0:n],
                                                                        start=(kc == 0), stop=(kc == KC - 1)), reads=[twt, t_yb], writes=[tpsG], defer=(kc != KC - 1))
                for kc in range(KC):
                    c.op("pe", lambda e, psU=psU, wt=wt, kc=kc: e.matmul(psU[:, 0:n], lhsT=wt[:, kc * 256 + 128:kc * 256 + 256], rhs=yb[:, kc, 0:n],
                                                                        start=(kc == 0), stop=(kc == KC - 1)), reads=[twt, t_yb], writes=[tpsU], defer=(kc != KC - 1))
                g, tg = sg[hc % 2]
                c.op("act", lambda e, psG=psG, g=g: e.activation(out=g[:, 0:n], in_=psG[:, 0:n], func=AF.Silu), reads=[tpsG], writes=[tg])
                c.op("dve", lambda e, psU=psU, g=g, hc=hc: e.tensor_tensor(out=hid[:, hc, 0:n], in0=psU[:, 0:n], in1=g[:, 0:n], op=ALU.mult),
                     reads=[tpsU, tg], writes=[t_hid])
            for f in range(KC):
                wt, twt = wf2[f % 2]
                c.dma("sp", wt[:], s.Wf2[l][f * 128:(f + 1) * 128, :], reads=[s.t_W[l]], writes=[twt])
                ps, tps = s.ps()
                for hc in range(44):
                    c.op("pe", lambda e, ps=ps, wt=wt, hc=hc: e.matmul(ps[:, 0:n], lhsT=wt[:, hc * 128:(hc + 1) * 128], rhs=hid[:, hc, 0:n],
                                                                      start=(hc == 0), stop=(hc == 43)), reads=[twt, t_hid], writes=[tps], defer=(hc != 43))
                c.op("dve", lambda e, ps=ps, f=f: e.scalar_tensor_tensor(out=xb[:, f, 0:n], in0=ps[:, 0:n], scalar=s.MM[:, l, 80 + f, w:w + 1],
                                                                        in1=xb[:, f, 0:n], op0=ALU.mult, op1=ALU.add), reads=[tps, s.t_MM], writes=[t_xb])
            s.layer_norm_fm(xb, t_xb, n, l, 2, 3)
            c.dma("sp", XTv[:, :, t0:t0 + n], xb[:, :, 0:n], reads=[t_xb], writes=[s.t_XT])

    def dump_a1(self, l):
        s, c = self, self.c
        s.add_dbg("d_MM", s.MM[:].rearrange("p l c w -> p (l c w)"), [128, NL * 96 * 2], [s.t_MM])
        for i, r0 in enumerate([0, 256, 4096, 8320]):
            s.add_dbg(f"d_PT{i}", s.PT[r0:r0 + 128, :], [128, PW], [s.t_PT])
        s.add_dbg("d_PF0", s.PF[:, 0:512], [384, 512], [s.t_PF])
        s.add_dbg("d_PF1", s.PF[:, T - 512:T], [384, 512], [s.t_PF])
        s.add_dbg("d_COS", s.COS, [L, 64], [s.t_COS])
        s.add_dbg("d_SIN", s.SIN, [L, 64], [s.t_COS])

    def final_phase(self):
        s, c = self, self.c
        with contextlib.ExitStack() as stp:
            c.stack = stp
            xin = [c.sbuf(f"fx{i}", [128, KC, 128], F32) for i in range(2)]
            xo = [c.sbuf(f"fo{i}", [128, D], F32) for i in range(2)]
            XTv = s.XT.rearrange("(kc p) t -> p kc t", p=128)
            t_out = Trk("out")
            for ti in range(16):
                a, ta = xin[ti % 2]
                o, to = xo[ti % 2]
                c.dma("sp", a[:], XTv[:, :, 64 + ti * 128:64 + (ti + 1) * 128], reads=[s.t_XT], writes=[ta])
                for g in range(4):
                    ps, tps = s.ps()
                    for q in range(4):
                        kc = g * 4 + q
                        c.op("pe", lambda e, a=a, ps=ps, kc=kc, q=q: e.transpose(out=ps[:, q * 128:(q + 1) * 128], in_=a[:, kc, :], identity=s.ident_f[:]),
                             reads=[ta, s.t_ident_f], writes=[tps])
                    c.op("dve", lambda e, o=o, ps=ps, g=g: e.tensor_copy(out=o[:, g * 512:(g + 1) * 512], in_=ps[:, :]), reads=[tps], writes=[to])
                c.dma("sp", s.out[ti * 128:(ti + 1) * 128, :], o[:], reads=[to], writes=[t_out])
            c.barrier()


def _in_tile_cols(j):
    hk = j // 2
    return [
        (0 + j * 128, 128), (512 + j * 128, 128), (1024 + j * 128, 128), (1536 + j * 128, 128),
        (4112 + (2 * j) * 128, 128), (4112 + (2 * j + 1) * 128, 128), (5136 + hk * 128, 128), (5392 + hk * 128, 128),
        (3584 + j * 128, 128), ([4096 + j, 4100 + j, 4104 + j, 4108 + j], 4),
        (2048 + j * 128, 128), (2560 + j * 128, 128), (3072 + j * 128, 128),
    ]


def make_in_maps(inp, n_layers=NL):
    x, cvec, ctx, c_ctx = inp["x"], inp["c"], inp["ctx"], inp["c_ctx"]
    f32 = np.float32
    lnp = np.stack([inp["ln1_w"], inp["ln1_b"], inp["ln2_w"], inp["ln2_b"]], 0)
    lnp = lnp.reshape(4, NL, KC, 128).transpose(3, 0, 1, 2).reshape(128, 256)
    w_o, w_f1, w_f2 = inp["w_o"], inp["w_ffn_in"], inp["w_ffn_out"]
    Wot = np.empty((NL, 2048, 2048), f32)
    Wf1t = np.empty((NL, DFF, 4096), f32)
    Wf2t = np.empty((NL, 2048, DFF), f32)
    rows = []
    for blk in range(4):
        rows += [blk * 128, 512 + blk * 128, 1024 + (2 * blk) * 128, 1024 + (2 * blk + 1) * 128]
    for l in range(NL):
        wo_perm = np.concatenate([w_o[l, r:r + 128, :] for r in rows], 0)
        Wot[l] = wo_perm.reshape(KC, 128, KC, 128).transpose(2, 1, 0, 3).reshape(2048, 2048)
        Wf1t[l] = w_f1[l].reshape(KC, 128, 2, 44, 128).transpose(3, 1, 0, 2, 4).reshape(DFF, 4096)
        Wf2t[l] = w_f2[l].reshape(44, 128, KC, 128).transpose(2, 1, 0, 3).reshape(2048, DFF)
    w_ada = inp["w_ada"]
    maps = []
    for r in range(NCORE):
        b, j = r // 4, r % 4
        m = {}
        m["x_own"] = np.ascontiguousarray(np.concatenate([ctx[b, 64 * j:64 * j + 64], x[b, 2048 * j:2048 * (j + 1)]], 0))
        m["cT"] = np.ascontiguousarray(np.stack([cvec[b], c_ctx], -1).reshape(KC, 128, 2).transpose(1, 0, 2).reshape(128, 32))
        wa = w_ada[:, :, 3072 * j:3072 * (j + 1)].reshape(NL, KC, 128, 24, 128).transpose(0, 3, 2, 1, 4)
        m["w_ada_t"] = np.ascontiguousarray(wa).reshape(NL * 24 * 128, 2048)[:n_layers * 24 * 128]
        m["b_ada_s"] = np.ascontiguousarray(inp["b_ada"][:, 3072 * j:3072 * (j + 1)].reshape(NL, 24, 128).transpose(2, 0, 1).reshape(128, 96))
        sb = np.zeros((128, 2), f32); sb[:, b] = 1.0
        m["selb"] = sb
        s4 = np.zeros((128, 4), f32); s4[:, j] = 1.0
        m["sel4"] = s4
        wsel = np.zeros((NL, 13, 128, KC, 128), f32)
        for ti, (c0, nc_) in enumerate(_in_tile_cols(j)):
            idx = np.asarray(c0) if isinstance(c0, list) else np.arange(c0, c0 + nc_)
            wsel[:, ti, :, :, :nc_] = inp["w_in"][:, :, idx].reshape(NL, KC, 128, nc_).transpose(0, 2, 1, 3)
        m["w_in_sel"] = np.ascontiguousarray(wsel.transpose(0, 2, 1, 3, 4)).reshape(NL * 128, 13 * 2048)[:n_layers * 128]
        m["w_o_s"] = np.ascontiguousarray(Wot.reshape(NL, 4, 4, 128, 2048)[:, :, j]).reshape(NL * 128, 8192)[:n_layers * 128]
        m["w_f1_s"] = np.ascontiguousarray(Wf1t.reshape(NL, 11, 4, 128, 4096)[:, :, j]).reshape(NL * 128, 45056)[:n_layers * 128]
        m["w_f2_s"] = np.ascontiguousarray(Wf2t.reshape(NL, 8, 4, 64, DFF)[:, :, j]).reshape(NL * 128, 22528)[:n_layers * 128]
        m["lnp"] = np.ascontiguousarray(lnp)
        hp = np.concatenate([inp["ret_decay_logit"][:, :, j].reshape(-1), inp["dn_a_log"][:, :, j].reshape(-1),
                             inp["dn_dt_bias"][:, :, j].reshape(-1)]).astype(f32)
        m["hp"] = np.ascontiguousarray(np.broadcast_to(hp[None, :], (128, 24)))
        cw = inp["dn_conv_w"]
        cwj = np.stack([cw[:, :, s_ * 512 + j * 128: s_ * 512 + (j + 1) * 128] for s_ in range(3)], 1)
        m["convw"] = np.ascontiguousarray(cwj.transpose(3, 0, 1, 2).reshape(128, 60))
        nw = np.stack([inp["dn_norm_w"], inp["att_qn_w"], inp["att_kn_w"]], 1).reshape(-1)
        m["nw"] = np.ascontiguousarray(np.broadcast_to(nw[None, :], (128, NL * 3 * 128)))
        maps.append(m)
    return maps


_CACHE = {}


def kernel(**inputs):
    inp = {k: np.asarray(v) for k, v in inputs.items()}
    if "nc" not in _CACHE:
        _CACHE["nc"] = Builder().build()
    nc = _CACHE["nc"]
    maps = make_in_maps(inp)
    res = run_bass_kernel_spmd(nc, maps, core_ids=list(range(NCORE)))
    out = np.empty((2, L, D), np.float32)
    for r in range(NCORE):
        b, j = r // 4, r % 4
        out[b, 2048 * j:2048 * (j + 1)] = res.results[r]["out"]
    return out
```
